# Optimizing a Trainium2 kernel written in Bass

```python
import math
import jax, jax.numpy as jnp
from jax import lax
import numpy as np

D_MODEL = 4096
BATCH = 2
SEQ = 4096
DEPTH = 1
DEC_BATCH = 4
DEC_SEQ = 4096
PAST_LEN = 128

N_META = 16
EPS = 1e-6
SSD_D_INNER = D_MODEL
SSD_HEAD_DIM = 64
SSD_N_HEADS = SSD_D_INNER // SSD_HEAD_DIM
SSD_N_GROUPS = 8
SSD_HEADS_PER_GROUP = SSD_N_HEADS // SSD_N_GROUPS
SSD_D_STATE = 128
SSD_CONV = 5
SSD_CHUNK = 128
SSD_CONV_CH = SSD_D_INNER + 2 * SSD_N_GROUPS * SSD_D_STATE
SSD_FRONT_PAD = SSD_CHUNK - N_META
ATT_N_HEADS = 16
ATT_HALF_DIM = 64
ATT_QK_DIM = 2 * ATT_HALF_DIM
ATT_V_DIM = 2 * ATT_HALF_DIM
ATT_WIDTH = ATT_N_HEADS * ATT_V_DIM
ATT_Q_BLOCK = 128
IN_Z = SSD_D_INNER
IN_XBC = SSD_CONV_CH
IN_DT = 2 * SSD_N_HEADS
IN_Q = ATT_N_HEADS * ATT_QK_DIM
IN_K = ATT_N_HEADS * ATT_QK_DIM
IN_V = ATT_WIDTH
D_IN_PROJ = IN_Z + IN_XBC + IN_DT + IN_Q + IN_K + IN_V
D_FF = -(-8 * D_MODEL // (3 * 256)) * 256

kernel_name = "hybrid_ssd_diffattn_encoder"


def rms_norm(x, g):
    xf = x.astype(jnp.float32)
    y = xf * lax.rsqrt(jnp.mean(xf * xf, axis=-1, keepdims=True) + EPS)
    return (y * g.astype(jnp.float32)).astype(x.dtype)


def centred_depthwise_conv(x, w, bias):
    half = (SSD_CONV - 1) // 2
    out = lax.conv_general_dilated(
        x, w[:, None, :].astype(x.dtype), window_strides=(1,), padding=[(half, half)],
        dimension_numbers=("NWC", "WIO", "NWC"), feature_group_count=x.shape[-1])
    return out + bias.astype(x.dtype)


def ssd_chunked(x, dt, A, Bm, Cm):
    b, Lp = x.shape[:2]
    c, T, G, J = Lp // SSD_CHUNK, SSD_CHUNK, SSD_N_GROUPS, SSD_HEADS_PER_GROUP
    P, N = SSD_HEAD_DIM, SSD_D_STATE
    dtc = dt.reshape(b, c, T, G, J)
    xdt = x.astype(jnp.float32).reshape(b, c, T, G, J, P) * dtc[..., None]
    Bc = Bm.astype(jnp.float32).reshape(b, c, T, G, N)
    Cc = Cm.astype(jnp.float32).reshape(b, c, T, G, N)
    cs = jnp.cumsum(dtc * A.reshape(G, J), axis=2)
    seg = cs[:, :, :, None] - cs[:, :, None, :]
    mask = jnp.tril(jnp.ones((T, T), dtype=bool))[:, :, None, None]
    Lmat = jnp.where(mask, jnp.exp(jnp.where(mask, seg, 0.0)), 0.0)
    CB = jnp.einsum("bclgn,bcsgn->bclsg", Cc, Bc)
    y_diag = jnp.einsum("bclsgj,bcsgjp->bclgjp", CB[..., None] * Lmat, xdt)
    decay_s = jnp.exp(cs[:, :, -1:] - cs)
    states = jnp.einsum("bcsgn,bcsgjp->bcgjpn", Bc, xdt * decay_s[..., None])
    chunk_decay = jnp.exp(cs[:, :, -1])

    def step(carry, inp):
        st, dec = inp
        return carry * dec[..., None, None] + st, carry

    init = jnp.zeros((b, G, J, P, N), jnp.float32)
    _, prev = lax.scan(step, init, (jnp.moveaxis(states, 1, 0), jnp.moveaxis(chunk_decay, 1, 0)))
    prev = jnp.moveaxis(prev, 0, 1)
    y_off = jnp.einsum("bclgn,bcgjpn->bclgjp", Cc, prev) * jnp.exp(cs)[..., None]
    return (y_diag + y_off).reshape(b, Lp, SSD_N_HEADS, P)


def bidirectional_ssd(xs, Bm, Cm, dt_raw, dt_bias, a_log, d_skip):
    L = xs.shape[1]
    pad = SSD_FRONT_PAD

    def padf(t):
        return jnp.pad(t, [(0, 0), (pad, 0)] + [(0, 0)] * (t.ndim - 2))

    valid = (jnp.arange(pad + L) >= pad).astype(jnp.float32)
    xp, Bp, Cp = padf(xs), padf(Bm), padf(Cm)
    dt = jax.nn.softplus(padf(dt_raw.astype(jnp.float32)) + dt_bias.astype(jnp.float32))
    dt = dt * valid[None, :, None, None]
    A = -jnp.exp(a_log.astype(jnp.float32))

    def rev(t):
        return jnp.flip(t, axis=1)

    y_fwd = ssd_chunked(xp, dt[:, :, 0], A[0], Bp, Cp)
    y_bwd = rev(ssd_chunked(rev(xp), rev(dt[:, :, 1]), A[1], rev(Bp), rev(Cp)))
    y = (y_fwd + y_bwd)[:, pad:] + xs.astype(jnp.float32) * d_skip.astype(jnp.float32)[:, None]
    return y.astype(xs.dtype)


def diff_attention(q, k, v, layer_idx, q_norm_g, k_norm_g, lambda_q1, lambda_k1,
                   lambda_q2, lambda_k2, subln_g):
    b, L, _ = q.shape
    H, d = ATT_N_HEADS, ATT_HALF_DIM
    qh = rms_norm(q.reshape(b, L, H, 2, d), q_norm_g) * (d ** -0.5)
    kh = rms_norm(k.reshape(b, L, H, 2, d), k_norm_g)
    vf = v.reshape(b, L, H, ATT_V_DIM).astype(jnp.float32)
    lam_init = 0.8 - 0.6 * math.exp(-0.3 * layer_idx)
    lam = (jnp.exp(jnp.sum(lambda_q1.astype(jnp.float32) * lambda_k1.astype(jnp.float32)))
           - jnp.exp(jnp.sum(lambda_q2.astype(jnp.float32) * lambda_k2.astype(jnp.float32)))
           + lam_init)
    slopes = jnp.exp2(-8.0 * jnp.arange(1, H + 1, dtype=jnp.float32) / H)
    pos = jnp.arange(L, dtype=jnp.float32)

    def attend(qb, pos_q):
        s = jnp.einsum("bqhid,bkhid->bihqk", qb, kh, preferred_element_type=jnp.float32)
        s = s - slopes[:, None, None] * jnp.abs(pos_q[:, None] - pos[None, :])
        p = jax.nn.softmax(s, axis=-1)
        a = p[:, 0] - lam * p[:, 1]
        return jnp.einsum("bhqk,bkhd->bqhd", a, vf)

    o_meta = attend(qh[:, :N_META], pos[:N_META])
    nb = (L - N_META) // ATT_Q_BLOCK
    q_blocks = jnp.moveaxis(qh[:, N_META:].reshape(b, nb, ATT_Q_BLOCK, H, 2, d), 1, 0)
    pos_blocks = pos[N_META:].reshape(nb, ATT_Q_BLOCK)
    o_real = lax.map(lambda args: attend(args[0], args[1]), (q_blocks, pos_blocks))
    o_real = jnp.moveaxis(o_real, 0, 1).reshape(b, L - N_META, H, ATT_V_DIM)
    o = jnp.concatenate([o_meta, o_real], axis=1)
    o = rms_norm(o, subln_g) * (1.0 - lam_init)
    return o.reshape(b, L, ATT_WIDTH).astype(v.dtype)


def hybrid_layer(h, layer_idx, norm_mix_g, w_in, conv_w, conv_b, dt_bias, a_log, d_skip,
                 ssd_norm_g, q_norm_g, k_norm_g, lambda_q1, lambda_k1, lambda_q2, lambda_k2,
                 subln_g, w_branch_gate, b_branch_gate, w_ssd_out, w_att_out, w_o,
                 norm_ffn_g, w_gate_up, w_down):
    b, L, _ = h.shape
    hn = rms_norm(h, norm_mix_g)
    proj = hn @ w_in
    o1 = IN_Z
    o2 = o1 + IN_XBC
    o3 = o2 + IN_DT
    o4 = o3 + IN_Q
    o5 = o4 + IN_K
    z, xbc, dt_raw, q, k, v = jnp.split(proj, [o1, o2, o3, o4, o5], axis=-1)
    xbc = jax.nn.silu(centred_depthwise_conv(xbc, conv_w, conv_b))
    gn = SSD_N_GROUPS * SSD_D_STATE
    xs, Bm, Cm = jnp.split(xbc, [SSD_D_INNER, SSD_D_INNER + gn], axis=-1)
    y_ssd = bidirectional_ssd(
        xs.reshape(b, L, SSD_N_HEADS, SSD_HEAD_DIM),
        Bm.reshape(b, L, SSD_N_GROUPS, SSD_D_STATE),
        Cm.reshape(b, L, SSD_N_GROUPS, SSD_D_STATE),
        dt_raw.reshape(b, L, 2, SSD_N_HEADS), dt_bias, a_log, d_skip)
    y_ssd = y_ssd.reshape(b, L, SSD_D_INNER) * jax.nn.silu(z)
    y_ssd = rms_norm(y_ssd.reshape(b, L, SSD_N_GROUPS, -1),
                     ssd_norm_g.reshape(SSD_N_GROUPS, -1)).reshape(b, L, SSD_D_INNER)
    y_att = diff_attention(q, k, v, layer_idx, q_norm_g, k_norm_g, lambda_q1, lambda_k1,
                           lambda_q2, lambda_k2, subln_g)
    gates = jax.nn.sigmoid((hn @ w_branch_gate + b_branch_gate).astype(jnp.float32))
    gates = gates.astype(h.dtype).reshape(b, L, 2, D_MODEL)
    mixed = gates[:, :, 0] * (y_ssd @ w_ssd_out) + gates[:, :, 1] * (y_att @ w_att_out)
    h = h + mixed @ w_o
    hf = rms_norm(h, norm_ffn_g)
    gate, up = jnp.split(hf @ w_gate_up, 2, axis=-1)
    return h + (jax.nn.silu(gate) * up) @ w_down


def run_group(x, meta_tokens, layer_weights):
    b = x.shape[0]
    meta = jnp.broadcast_to(meta_tokens.astype(x.dtype)[None], (b, N_META, D_MODEL))
    h = jnp.concatenate([meta, x], axis=1)
    for l in range(DEPTH):
        h = hybrid_layer(h, l, *[w[l] for w in layer_weights])
    return h[:, N_META:]


def setup_inputs(seed: int = 0) -> dict:
    key = jax.random.key(seed)
    ks = jax.random.split(key, 26)
    f32 = jnp.float32

    def nrm(k, shape, scale):
        return jax.random.normal(k, shape, f32) * scale

    def gain(k, shape):
        return 1.0 + 0.02 * jax.random.normal(k, shape, f32)

    dt0 = jnp.exp(jax.random.uniform(ks[7], (DEPTH, 2, SSD_N_HEADS), f32,
                                     minval=math.log(1e-3), maxval=math.log(1e-1)))
    return {
        "x_prompt": nrm(ks[0], (BATCH, SEQ, D_MODEL), 1.0),
        "x_sample": nrm(ks[1], (DEC_BATCH, DEC_SEQ, D_MODEL), 1.0),
        "meta_tokens": nrm(ks[2], (N_META, D_MODEL), 1.0),
        "norm_mix_g": gain(ks[3], (DEPTH, D_MODEL)),
        "w_in": nrm(ks[4], (DEPTH, D_MODEL, D_IN_PROJ), D_MODEL ** -0.5),
        "conv_w": nrm(ks[5], (DEPTH, SSD_CONV, SSD_CONV_CH), SSD_CONV ** -0.5),
        "conv_b": nrm(ks[6], (DEPTH, SSD_CONV_CH), 0.02),
        "dt_bias": dt0 + jnp.log(-jnp.expm1(-dt0)),
        "a_log": jnp.log(jax.random.uniform(ks[8], (DEPTH, 2, SSD_N_HEADS), f32, minval=1.0, maxval=16.0)),
        "d_skip": gain(ks[9], (DEPTH, SSD_N_HEADS)),
        "ssd_norm_g": gain(ks[10], (DEPTH, SSD_D_INNER)),
        "q_norm_g": gain(ks[11], (DEPTH, ATT_HALF_DIM)),
        "k_norm_g": gain(ks[12], (DEPTH, ATT_HALF_DIM)),
        "lambda_q1": nrm(ks[13], (DEPTH, ATT_HALF_DIM), 0.1),
        "lambda_k1": nrm(ks[14], (DEPTH, ATT_HALF_DIM), 0.1),
        "lambda_q2": nrm(ks[15], (DEPTH, ATT_HALF_DIM), 0.1),
        "lambda_k2": nrm(ks[16], (DEPTH, ATT_HALF_DIM), 0.1),
        "subln_g": gain(ks[17], (DEPTH, ATT_V_DIM)),
        "w_branch_gate": nrm(ks[18], (DEPTH, D_MODEL, 2 * D_MODEL), D_MODEL ** -0.5),
        "b_branch_gate": nrm(ks[19], (DEPTH, 2 * D_MODEL), 0.02),
        "w_ssd_out": nrm(ks[20], (DEPTH, SSD_D_INNER, D_MODEL), SSD_D_INNER ** -0.5),
        "w_att_out": nrm(ks[21], (DEPTH, ATT_WIDTH, D_MODEL), ATT_WIDTH ** -0.5),
        "w_o": nrm(ks[22], (DEPTH, D_MODEL, D_MODEL), D_MODEL ** -0.5),
        "norm_ffn_g": gain(ks[23], (DEPTH, D_MODEL)),
        "w_gate_up": nrm(ks[24], (DEPTH, D_MODEL, 2 * D_FF), D_MODEL ** -0.5),
        "w_down": nrm(ks[25], (DEPTH, D_FF, D_MODEL), D_FF ** -0.5),
    }


def reference(x_prompt, x_sample, meta_tokens, norm_mix_g, w_in, conv_w, conv_b, dt_bias,
              a_log, d_skip, ssd_norm_g, q_norm_g, k_norm_g, lambda_q1, lambda_k1,
              lambda_q2, lambda_k2, subln_g, w_branch_gate, b_branch_gate, w_ssd_out,
              w_att_out, w_o, norm_ffn_g, w_gate_up, w_down):
    layer_weights = (norm_mix_g, w_in, conv_w, conv_b, dt_bias, a_log, d_skip, ssd_norm_g,
                     q_norm_g, k_norm_g, lambda_q1, lambda_k1, lambda_q2, lambda_k2, subln_g,
                     w_branch_gate, b_branch_gate, w_ssd_out, w_att_out, w_o, norm_ffn_g,
                     w_gate_up, w_down)
    y_prompt = run_group(x_prompt, meta_tokens, layer_weights)
    y_sample = run_group(x_sample, meta_tokens, layer_weights)
    return (y_prompt, y_sample)
```

```python
import numpy as np
import ml_dtypes
from contextlib import ExitStack
import concourse.bass as bass
import concourse.mybir as mybir
from concourse.bass_utils import run_bass_kernel_spmd

F32 = mybir.dt.float32
BF16 = mybir.dt.bfloat16
AF = mybir.ActivationFunctionType
ALU = mybir.AluOpType
AX = mybir.AxisListType

D = 4096
NKC = 32
D_IN = 16512
C_XBC = 6144
O_XBC = 4096
O_DT = 10240
O_Q = 10368
O_K = 12416
O_V = 14464
D_FF = 11008
EPS = 1e-6
NPAD = 112


class Res:
    __slots__ = ("name", "lw", "rd", "dsem", "excl", "base")

    def __init__(self, name, dsem=None, excl=False):
        self.name = name
        self.lw = {}
        self.rd = {}
        self.dsem = dsem
        self.excl = excl
        self.base = {}


class DSem:
    def __init__(self, h):
        self.h = h
        self.count = 0


class Eng:
    def __init__(self, name, sem):
        self.name = name
        self.sem = sem
        self.count = 0
        self.waited = {}
        self.ops = []
        self.pending_nosig = False


class Prog:
    def __init__(self, nc, stack):
        self.nc = nc
        self.stack = stack
        self.E = {}
        self.semobj = {}
        for n in ("pe", "act", "dve", "pool", "sp"):
            s = stack.enter_context(nc.semaphore("e_" + n))
            self.E[n] = Eng(n, s)
            self.semobj[id(s)] = ("eng", self.E[n], s)
        self.dsems = []
        self.nres = 0

    def new_dsem(self, name):
        s = self.stack.enter_context(self.nc.semaphore("d_" + name))
        d = DSem(s)
        self.semobj[id(s)] = ("dma", d, s)
        self.dsems.append(d)
        return d

    def res(self, name, dma=False, excl=False):
        self.nres += 1
        return Res(name, self.new_dsem(name + str(self.nres)) if dma else None, excl)

    def _waits(self, eng, reads, writes, partial):
        need = {}

        def add(tok):
            for k, v in tok.items():
                if need.get(k, 0) < v:
                    need[k] = v

        for r in reads:
            add(r.lw)
            if r.excl:
                add(r.rd)
        for w in writes:
            if not partial:
                add(w.lw)
            else:
                add(w.base)
            add(w.rd)
        out = []
        for k, v in need.items():
            kind, obj, h = self.semobj[k]
            if kind == "dma":
                v = obj.count
            else:
                if obj is eng and eng.name == "pe":
                    continue
            if eng.waited.get(k, 0) >= v:
                continue
            eng.waited[k] = v
            out.append((h, v))
        return out

    def _commit(self, key, val, reads, writes, partial):
        for r in reads:
            if r.rd.get(key, 0) < val:
                r.rd[key] = val
        for w in writes:
            if partial:
                if w.lw.get(key, 0) < val:
                    w.lw[key] = val
            else:
                b = dict(w.lw)
                for k2, v2 in w.rd.items():
                    if b.get(k2, 0) < v2:
                        b[k2] = v2
                w.base = b
                w.lw = {key: val}
                w.rd = {}

    def op(self, en, fn, reads=(), writes=(), sig=True, partial=False):
        eng = self.E[en]
        waits = self._waits(eng, reads, writes, partial)
        if sig:
            eng.count += 1
            val = eng.count
            inc = (eng.sem, 1)
            eng.pending_nosig = False
        else:
            val = eng.count + 1
            inc = None
            eng.pending_nosig = True
        eng.ops.append((waits, fn, inc))
        self._commit(id(eng.sem), val, reads, writes, partial)

    def dma(self, qn, out, in_, dsem, reads=(), writes=(), partial=False):
        eng = self.E[qn]
        waits = self._waits(eng, reads, writes, partial)
        dsem.count += 16
        eng.ops.append((waits, lambda h: h.dma_start(out=out, in_=in_), (dsem.h, 16)))
        self._commit(id(dsem.h), dsem.count, reads, writes, partial)

    def emit(self):
        nc = self.nc
        sp = self.E["sp"]
        fin = [(d.h, d.count) for d in self.dsems if d.count > 0]
        for n, e in self.E.items():
            assert not e.pending_nosig, n
        with nc.Block() as block:
            def mk(eng, final):
                def f(h):
                    for waits, fn, inc in eng.ops:
                        for (s, v) in waits:
                            h.wait_ge(s, v)
                        ins = fn(h)
                        if inc is not None:
                            ins.then_inc(inc[0], inc[1])
                    if final:
                        for (s, v) in fin:
                            h.wait_ge(s, v)
                return f
            block.tensor(mk(self.E["pe"], False))
            block.scalar(mk(self.E["act"], False))
            block.vector(mk(self.E["dve"], False))
            block.gpsimd(mk(self.E["pool"], False))
            block.sync(mk(self.E["sp"], True))


class Ring:
    def __init__(self, items):
        self.items = items
        self.i = 0

    def next(self):
        it = self.items[self.i % len(self.items)]
        self.i += 1
        return it


def make_consts(NT):
    T = NT * 128
    i = np.arange(128)[:, None]
    j = np.arange(128)[None, :]
    c = {}
    c["ident"] = (i == j).astype(ml_dtypes.bfloat16)
    c["blk64"] = ((i // 64) == (j // 64)).astype(ml_dtypes.bfloat16)
    c["ones"] = np.ones((128, 128), ml_dtypes.bfloat16)
    u = np.arange(2 * T + 512)[None, :]
    c["absd"] = np.abs(u - i - T).astype(np.float32)
    pb = np.zeros((128, 1), np.float32)
    pb[:NPAD] = -30000.0
    c["padbias"] = pb
    c["m_kles"] = (i <= j).astype(np.float32)
    c["m_kgts"] = (i > j).astype(np.float32)
    c["m_kgel"] = (i >= j).astype(np.float32)
    c["m_klts"] = (i < j).astype(np.float32)
    c["onesf"] = np.ones((128, 128), np.float32)
    return c


class Cx:
    pass


class _Stop(Exception):
    pass


def _chk(cx, n):
    import os
    if int(os.environ.get('K_STOP', '0')) == n:
        raise _Stop()


def _barrier(P):
    toks = []
    for n, e in P.E.items():
        assert not e.pending_nosig
        if e.count > 0:
            toks.append((id(e.sem), e.sem, e.count))
    for d in P.dsems:
        if d.count > 0:
            toks.append((id(d.h), d.h, d.count))
    for n, e in P.E.items():
        waits = []
        for k, h, v in toks:
            if k == id(e.sem):
                continue
            if e.waited.get(k, 0) >= v:
                continue
            e.waited[k] = v
            waits.append((h, v))
        if waits:
            e.ops.append((waits, None, None))


def _flush(P, final=False):
    nc = P.nc
    fin = [(d.h, d.count) for d in P.dsems if d.count > 0] if final else []
    for n, e in P.E.items():
        assert not e.pending_nosig, n
    with nc.Block() as block:
        def mk(eng, fl):
            ops = eng.ops

            def f(h):
                for waits, fn, inc in ops:
                    for (s, v) in waits:
                        h.wait_ge(s, v)
                    if fn is None:
                        continue
                    ins = fn(h)
                    if inc is not None:
                        ins.then_inc(inc[0], inc[1])
                if fl:
                    for (s, v) in fin:
                        h.wait_ge(s, v)
            return f
        block.tensor(mk(P.E["pe"], False))
        block.scalar(mk(P.E["act"], False))
        block.vector(mk(P.E["dve"], False))
        block.gpsimd(mk(P.E["pool"], False))
        block.sync(mk(P.E["sp"], final))
    for e in P.E.values():
        e.ops = []


def _blocks(NT):
    out = []
    t = 0
    while t < NT:
        n = min(4, NT - t)
        out.append((t, n))
        t += n
    return out


def phase_A(cx):
    nc, P, NT = cx.nc, cx.P, cx.NT
    T = NT * 128
    with ExitStack() as st:
        sb = lambda n, s, d: st.enter_context(nc.sbuf_tensor(n, s, d))
        xt = sb("a_xt", [128, D], F32)
        xt_r = P.res("a_xt", dma=True)
        hn = sb("a_hn", [128, D], F32)
        hn_r = P.res("a_hn")
        hnT = sb("a_hnT", [128, NKC, 512], BF16)
        hnT_r = P.res("a_hnT", dma=True)
        gcol = sb("a_gcol", [128, NKC], F32)
        gcol_r = P.res("a_gcol", dma=True)
        NW = 3
        wt = [sb("a_w%d" % i, [128, 16, 512], BF16) for i in range(NW)]
        wt_r = [P.res("a_w", dma=True) for i in range(NW)]
        wring = Ring(list(zip(wt, wt_r)))
        wdt = sb("a_wdt", [128, NKC, 128], BF16)
        wdt_r = P.res("a_wdt", dma=True)
        sm = sb("a_sm", [128, 8], F32)
        sm_r = P.res("a_sm")
        NS = 4
        sf = [sb("a_sf%d" % i, [128, 512], F32) for i in range(NS)]
        sf_r = [P.res("a_sf", dma=True) for i in range(NS)]
        sfring = Ring(list(zip(sf, sf_r)))
        sh = [sb("a_sh%d" % i, [128, 512], BF16) for i in range(NS)]
        sh_r = [P.res("a_sh", dma=True) for i in range(NS)]
        shring = Ring(list(zip(sh, sh_r)))
        sq = [sb("a_sq%d" % i, [128, 512], BF16) for i in range(2)]
        sq_r = [P.res("a_sq") for i in range(2)]
        sqring = Ring(list(zip(sq, sq_r)))
        rq = [sb("a_rq%d" % i, [128, 512], F32) for i in range(2)]
        rq_r = [P.res("a_rq") for i in range(2)]
        rqring = Ring(list(zip(rq, rq_r)))
        dtb = sb("a_dtb", [128, 128], F32)
        dtb_r = P.res("a_dtb", dma=True)
        Ab = sb("a_Ab", [128, 128], F32)
        Ab_r = P.res("a_Ab", dma=True)
        gq = sb("a_gq", [128, 2], F32)
        gq_r = P.res("a_gq", dma=True)
        zt = sb("a_zero", [128, 48, 2], F32)
        zt_r = P.res("a_zero", dma=True)
        ps = [st.enter_context(nc.psum_tensor("a_ps%d" % i, [128, 512], F32)) for i in range(8)]
        ps_r = [P.res("a_ps", excl=True) for i in range(8)]

        _barrier(P)
        try:
          _phase_A_body(cx, locals())
        except _Stop:
          pass
        _flush(P)


def _phase_A_body(cx, L):
        globals().update({})
        nc, P, NT = cx.nc, cx.P, cx.NT
        T = NT * 128
        for _k, _v in L.items():
            pass
        xt, xt_r, hn, hn_r, hnT, hnT_r, gcol, gcol_r, wring, wdt, wdt_r, sm, sm_r, sfring, shring, sqring, rqring, dtb, dtb_r, Ab, Ab_r, gq, gq_r, zt, zt_r, ps, ps_r = [L[k] for k in 'xt xt_r hn hn_r hnT hnT_r gcol gcol_r wring wdt wdt_r sm sm_r sfring shring sqring rqring dtb dtb_r Ab Ab_r gq gq_r zt zt_r ps ps_r'.split()]
        P.dma("sp", gcol[:, :], cx.gcol_mix[:, :], gcol_r.dsem, writes=[gcol_r])
        P.dma("sp", dtb[:, :], cx.dt_bias[0:1, :].partition_broadcast(128), dtb_r.dsem, writes=[dtb_r])
        P.dma("sp", Ab[:, :], cx.a_log[0:1, :].partition_broadcast(128), Ab_r.dsem, writes=[Ab_r])
        P.op("act", lambda h: h.activation(out=Ab[:, :], in_=Ab[:, :], func=AF.Exp), reads=[Ab_r], writes=[Ab_r])
        P.op("dve", lambda h: h.tensor_scalar(out=Ab[:, :], in0=Ab[:, :], scalar1=-1.0, scalar2=None, op0=ALU.mult),
             reads=[Ab_r], writes=[Ab_r])
        P.dma("sp", gq[:, :], cx.gqk[:, :], gq_r.dsem, writes=[gq_r])
        P.op("dve", lambda h: h.tensor_scalar(out=gq[:, 0:1], in0=gq[:, 0:1], scalar1=0.125, scalar2=None, op0=ALU.mult),
             reads=[gq_r], writes=[gq_r])
        P.op("pool", lambda h: h.memset(zt[:, :, :], 0.0), writes=[zt_r])
        P.dma("sp", cx.XBC[:, :, 0:2], zt[:, :, :], zt_r.dsem, reads=[zt_r], writes=[cx.XBC_r], partial=True)
        P.dma("sp", cx.XBC[:, :, T + 2:T + 4], zt[:, :, :], zt_r.dsem, reads=[zt_r], writes=[cx.XBC_r], partial=True)

        _chk(cx, 1)
        win = cx.w_in.rearrange("(kc p) n -> p kc n", p=128)
        identf = cx.c_identf
        evi = [0]

        def evac_copy(out_ap, in_ap, reads, writes):
            evi[0] += 1
            if evi[0] % 2 == 0:
                P.op("act", lambda h: h.copy(out=out_ap, in_=in_ap), reads=reads, writes=writes)
            else:
                P.op("dve", lambda h: h.tensor_copy(out=out_ap, in_=in_ap), reads=reads, writes=writes)

        def load_w(c0, kh, ncols=512):
            w, wr = wring.next()
            P.dma("pool", w[:, :, 0:ncols], win[:, kh * 16:(kh + 1) * 16, c0:c0 + ncols], wr.dsem, writes=[wr])
            return w, wr

        grp = [0]
        for (t0, nt) in _blocks(NT):
            W = nt * 128
            tok0 = t0 * 128
            for i in range(nt):
                P.dma("sp", xt[:, :], cx.xin[tok0 + i * 128: tok0 + (i + 1) * 128, :], xt_r.dsem, writes=[xt_r])
                P.op("act", lambda h: h.activation(out=hn[:, :], in_=xt[:, :], func=AF.Square, accum_out=sm[:, 0:1]),
                     reads=[xt_r], writes=[hn_r, sm_r])
                P.op("act", lambda h: h.activation(out=sm[:, 1:2], in_=sm[:, 0:1], func=AF.Sqrt, bias=EPS, scale=1.0 / D),
                     reads=[sm_r], writes=[sm_r])
                P.op("dve", lambda h: h.reciprocal(out=sm[:, 2:3], in_=sm[:, 1:2]), reads=[sm_r], writes=[sm_r])
                P.op("dve", lambda h: h.tensor_scalar(out=hn[:, :], in0=xt[:, :], scalar1=sm[:, 2:3], scalar2=None, op0=ALU.mult),
                     reads=[xt_r, sm_r], writes=[hn_r])
                for q4 in range(8):
                    pb, pbr = ps[q4 % 2], ps_r[q4 % 2]
                    for k4 in range(4):
                        kc = q4 * 4 + k4
                        P.op("pe", lambda h, pb=pb, k4=k4, kc=kc: h.transpose(pb[:, k4 * 128:(k4 + 1) * 128], hn[:, kc * 128:(kc + 1) * 128], identf[:, :]),
                             reads=[hn_r, cx.c_r], writes=[pbr], sig=(k4 == 3), partial=(k4 > 0))
                    P.op("dve", lambda h, pb=pb, q4=q4, i=i: h.tensor_tensor(
                        out=hnT[:, q4 * 4:(q4 + 1) * 4, i * 128:(i + 1) * 128],
                        in0=pb[:, :].rearrange("p (k n) -> p k n", k=4),
                        in1=gcol[:, q4 * 4:(q4 + 1) * 4].unsqueeze(2).to_broadcast([128, 4, 128]), op=ALU.mult),
                        reads=[pbr, gcol_r], writes=[hnT_r], partial=True)
            _chk(cx, 2)
            P.dma("sp", cx.HNT[:, :, tok0:tok0 + W], hnT[:, :, 0:W], hnT_r.dsem, reads=[hnT_r], writes=[cx.HNT_r], partial=True)

            _chk(cx, 3)
            P.dma("pool", wdt[:, :, :], win[:, :, O_DT:O_DT + 128], wdt_r.dsem, writes=[wdt_r])
            for i in range(nt):
                t = t0 + i
                pb, pbr = ps[i % 2], ps_r[i % 2]
                for kc in range(NKC):
                    P.op("pe", lambda h, pb=pb, kc=kc, i=i: h.matmul(pb[:, 0:128], hnT[:, kc, i * 128:(i + 1) * 128], wdt[:, kc, :],
                                                                   start=(kc == 0), stop=(kc == NKC - 1)),
                         reads=[hnT_r, wdt_r], writes=[pbr], sig=(kc == NKC - 1), partial=(kc > 0))
                dtt = cx.DT[:, t, :]
                dr = cx.DT_r[t]
                P.op("dve", lambda h, pb=pb, dtt=dtt: h.tensor_tensor(out=dtt, in0=pb[:, 0:128], in1=dtb[:, :], op=ALU.add),
                     reads=[pbr, dtb_r], writes=[dr])
                P.op("act", lambda h, dtt=dtt: h.activation(out=dtt, in_=dtt, func=AF.Exp), reads=[dr], writes=[dr])
                P.op("act", lambda h, dtt=dtt: h.activation(out=dtt, in_=dtt, func=AF.Ln, bias=1.0, scale=1.0),
                     reads=[dr], writes=[dr])
                if t == 0:
                    P.op("pool", lambda h: h.memset(cx.DT[0:NPAD, 0, :], 0.0), reads=[dr], writes=[dr])
                P.op("pool", lambda h, dtt=dtt, t=t: h.tensor_tensor(out=cx.DA[:, t, :], in0=dtt, in1=Ab[:, :], op=ALU.mult),
                     reads=[dr, Ab_r], writes=[cx.DA_r[t]])

            _chk(cx, 4)
            def group_b(c0, kind):
                base = 4 * (grp[0] % 2)
                grp[0] += 1
                for kh in range(2):
                    w, wr = load_w(c0, kh)
                    for j in range(4):
                        pb, pbr = ps[base + j], ps_r[base + j]
                        for k in range(16):
                            kc = kh * 16 + k
                            P.op("pe", lambda h, W=W, pb=pb, w=w, k=k, j=j, kc=kc: h.matmul(pb[:, 0:W], w[:, k, j * 128:(j + 1) * 128], hnT[:, kc, 0:W],
                                                                                 start=(kc == 0), stop=(kc == NKC - 1)),
                                 reads=[hnT_r, wr], writes=[pbr], sig=(k == 15), partial=(kc > 0))
                for j in range(4):
                    pb, pbr = ps[base + j], ps_r[base + j]
                    col = c0 + j * 128
                    if kind == "xbc":
                        s_, sr = sfring.next()
                        evac_copy(s_[:, 0:W], pb[:, 0:W], [pbr], [sr])
                        cc = (col - O_XBC) // 128
                        P.dma("sp", cx.XBC[:, cc, 2 + tok0: 2 + tok0 + W], s_[:, 0:W], sr.dsem, reads=[sr], writes=[cx.XBC_r], partial=True)
                    else:
                        isq = (kind == "q")
                        hd = (col - (O_Q if isq else O_K)) // 128
                        gc = gq[:, 0:1] if isq else gq[:, 1:2]
                        s_, sr = sfring.next()
                        q2, q2r = sqring.next()
                        r_, rr = rqring.next()
                        o_, orr = shring.next()
                        P.op("dve", lambda h, W=W, s_=s_, pb=pb: h.tensor_copy(out=s_[:, 0:W], in_=pb[:, 0:W]), reads=[pbr], writes=[sr])
                        P.op("act", lambda h, W=W, q2=q2, pb=pb: h.activation(out=q2[:, 0:W], in_=pb[:, 0:W], func=AF.Square), reads=[pbr], writes=[q2r])
                        import os as _os
                        _qv = int(_os.environ.get('K_QV', '0'))
                        if _qv == 1:
                            continue
                        P.op("pe", lambda h, W=W, pb=pb, q2=q2: h.matmul(pb[:, 0:W], cx.c_blk64[:, :], q2[:, 0:W], start=True, stop=True),
                             reads=[q2r, cx.c_r], writes=[pbr])
                        if _qv == 2:
                            continue
                        P.op("act", lambda h, W=W, r_=r_, pb=pb: h.activation(out=r_[:, 0:W], in_=pb[:, 0:W], func=AF.Sqrt, bias=EPS, scale=1.0 / 64),
                             reads=[pbr], writes=[rr])
                        if _qv == 3:
                            continue
                        P.op("dve", lambda h, W=W, r_=r_: h.reciprocal(out=r_[:, 0:W], in_=r_[:, 0:W]), reads=[rr], writes=[rr])
                        P.op("dve", lambda h, W=W, o_=o_, s_=s_, r_=r_, gc=gc: h.scalar_tensor_tensor(out=o_[:, 0:W], in0=s_[:, 0:W], scalar=gc, in1=r_[:, 0:W],
                                                                                      op0=ALU.mult, op1=ALU.mult),
                             reads=[sr, rr, gq_r], writes=[orr])
                        dst = cx.QT if isq else cx.KT
                        dst_r = cx.QT_r if isq else cx.KT_r
                        P.dma("sp", dst[:, hd, tok0:tok0 + W], o_[:, 0:W], orr.dsem, reads=[orr], writes=[dst_r], partial=True)

            for g in range(12):
                group_b(O_XBC + g * 512, "xbc")
                _chk(cx, 5)
            _chk(cx, 6)
            for g in range(4):
                group_b(O_Q + g * 512, "q")
            for g in range(4):
                group_b(O_K + g * 512, "k")
            _chk(cx, 7)

            def group_a(c0, kind):
                base = 4 * (grp[0] % 2)
                grp[0] += 1
                for kh in range(2):
                    w, wr = load_w(c0, kh)
                    for i in range(nt):
                        pb, pbr = ps[base + i], ps_r[base + i]
                        for k in range(16):
                            kc = kh * 16 + k
                            P.op("pe", lambda h, pb=pb, w=w, k=k, i=i, kc=kc: h.matmul(pb[:, :], hnT[:, kc, i * 128:(i + 1) * 128], w[:, k, :],
                                                                                 start=(kc == 0), stop=(kc == NKC - 1)),
                                 reads=[hnT_r, wr], writes=[pbr], sig=(k == 15), partial=(kc > 0))
                for i in range(nt):
                    pb, pbr = ps[base + i], ps_r[base + i]
                    r0 = tok0 + i * 128
                    if kind == "v":
                        o_, orr = shring.next()
                        evac_copy(o_[:, :], pb[:, :], [pbr], [orr])
                        cv = c0 - O_V
                        P.dma("sp", cx.V[r0:r0 + 128, cv:cv + 512], o_[:, :], orr.dsem, reads=[orr], writes=[cx.V_r], partial=True)
                    else:
                        s_, sr = sfring.next()
                        P.op("act", lambda h, s_=s_, pb=pb: h.activation(out=s_[:, :], in_=pb[:, :], func=AF.Silu), reads=[pbr], writes=[sr])
                        P.dma("sp", cx.ZS[r0:r0 + 128, c0:c0 + 512], s_[:, :], sr.dsem, reads=[sr], writes=[cx.ZS_r], partial=True)

            for g in range(4):
                group_a(O_V + g * 512, "v")
            _chk(cx, 8)
            for g in range(8):
                group_a(g * 512, "z")


def _bc(ap, shape, axis):
    return ap.unsqueeze(axis).to_broadcast(shape)


def phase_B(cx, sweep):
    nc, P, NT = cx.nc, cx.P, cx.NT
    T = NT * 128
    d = sweep
    with ExitStack() as st:
        sb = lambda n, s, dt_: st.enter_context(nc.sbuf_tensor(n, s, dt_))
        pfx = "b%d_" % sweep
        R = lambda n, **kw: P.res(pfx + n, **kw)
        xin_ = sb(pfx + "xin", [128, 16, 132] if sweep == 0 else [128, 1, 2], F32); xin_r = R("xin", dma=True)
        acc = sb(pfx + "acc", [128, 16, 128] if sweep == 0 else [128, 1, 2], F32); acc_r = R("acc")
        tmp = sb(pfx + "tmp", [128, 16, 128] if sweep == 0 else [128, 1, 2], F32); tmp_r = R("tmp")
        xbcT = sb(pfx + "xbcT", [128, 48, 128], BF16); xbcT_r = R("xbcT", dma=True)
        xtok = sb(pfx + "xtok", [128, D], BF16); xtok_r = R("xtok", dma=True)
        btok = sb(pfx + "btok", [128, 1024], BF16); btok_r = R("btok", dma=True)
        xw = sb(pfx + "xw", [128, D], BF16); xw_r = R("xw")
        S = sb(pfx + "S", [128, D], F32); S_r = R("S")
        Sb = sb(pfx + "Sb", [128, D], BF16); Sb_r = R("Sb")
        aU = sb(pfx + "aU", [128, 8, 128], F32); aU_r = R("aU")
        E = sb(pfx + "E", [128, 8, 128], F32); E_r = R("E")
        MT = sb(pfx + "MT", [128, 8, 128], BF16); MT_r = R("MT")
        CBm = sb(pfx + "CBm", [128, 128], F32); CBm_r = R("CBm")
        yacc = sb(pfx + "yacc", [128, D], F32); yacc_r = R("yacc", dma=True)
        yos = sb(pfx + "yos", [128, 512], F32); yos_r = R("yos")
        ecs = sb(pfx + "ecs", [128, 64], F32); ecs_r = R("ecs")
        wst = sb(pfx + "wst", [128, 64], F32); wst_r = R("wst")
        dec = sb(pfx + "dec", [128, 64], F32); dec_r = R("dec")
        cw = sb(pfx + "cw", [128, 48, 5], F32); cw_r = R("cw", dma=True)
        cb = sb(pfx + "cb", [128, 48], F32); cb_r = R("cb", dma=True)
        dsk = sb(pfx + "dsk", [128, 64], F32); dsk_r = R("dsk", dma=True)
        zs = sb(pfx + "zs", [128, D] if sweep == 1 else [128, 2], F32); zs_r = R("zs", dma=True)
        gs = sb(pfx + "gs", [128, D] if sweep == 1 else [128, 2], F32); gs_r = R("gs", dma=True)
        ynb = sb(pfx + "ynb", [128, D] if sweep == 1 else [128, 2], BF16); ynb_r = R("ynb")
        ynT = sb(pfx + "ynT", [128, NKC, 128] if sweep == 1 else [128, 1, 2], BF16); ynT_r = R("ynT", dma=True)
        sm = sb(pfx + "sm", [128, 24], F32); sm_r = R("sm")
        psA = [st.enter_context(nc.psum_tensor(pfx + "psA%d" % i, [128, 512], F32)) for i in range(5)]
        psA_r = [R("psA", excl=True) for i in range(5)]
        psT = [st.enter_context(nc.psum_tensor(pfx + "psT%d" % i, [128, 1024], BF16)) for i in range(2)]
        psT_r = [R("psT", excl=True) for i in range(2)]
        psS = st.enter_context(nc.psum_tensor(pfx + "psS", [128, 512], F32)); psS_r = R("psS", excl=True)
        ident = cx.c_ident
        _barrier(P)
        if sweep == 0:
            P.dma("sp", cw[:, :, :], cx.convw[:, :, :], cw_r.dsem, writes=[cw_r])
            P.dma("sp", cb[:, :], cx.convb[:, :], cb_r.dsem, writes=[cb_r])
            P.dma("sp", dsk[:, :], cx.d_skip[0:1, :].partition_broadcast(128), dsk_r.dsem, writes=[dsk_r])
        else:
            P.dma("sp", gs[:, :], cx.ssd_norm_g[0:1, :].partition_broadcast(128), gs_r.dsem, writes=[gs_r])
        P.op("pool", lambda h: h.memset(S[:, :], 0.0), writes=[S_r])
        P.op("pool", lambda h: h.memset(Sb[:, :], 0.0), writes=[Sb_r])
        m_cum = cx.c_m_kles if d == 0 else cx.c_m_kgel
        m_seg = cx.c_m_kgts if d == 0 else cx.c_m_klts
        order = list(range(NT)) if d == 0 else list(range(NT - 1, -1, -1))
        for t in order:
            tok0 = t * 128
            dtr, dar = cx.DT_r[t], cx.DA_r[t]
            DTt = cx.DT[:, t, d * 64:(d + 1) * 64]
            DAt = cx.DA[:, t, d * 64:(d + 1) * 64]
            if sweep == 0:
                for c3 in range(3):
                    P.dma("sp", xin_[:, :, :], cx.XBC[:, c3 * 16:(c3 + 1) * 16, tok0:tok0 + 132], xin_r.dsem, reads=[cx.XBC_r], writes=[xin_r])
                    for k in range(5):
                        wk = _bc(cw[:, c3 * 16:(c3 + 1) * 16, k], [128, 16, 128], 2)
                        if k == 0:
                            P.op("dve", lambda h, wk=wk: h.tensor_tensor(out=acc[:, :, :], in0=xin_[:, :, 0:128], in1=wk, op=ALU.mult),
                                 reads=[xin_r, cw_r], writes=[acc_r])
                        else:
                            P.op("pool", lambda h, wk=wk, k=k: h.tensor_tensor(out=tmp[:, :, :], in0=xin_[:, :, k:k + 128], in1=wk, op=ALU.mult),
                                 reads=[xin_r, cw_r], writes=[tmp_r])
                            P.op("dve", lambda h: h.tensor_tensor(out=acc[:, :, :], in0=acc[:, :, :], in1=tmp[:, :, :], op=ALU.add),
                                 reads=[acc_r, tmp_r], writes=[acc_r])
                    P.op("dve", lambda h, c3=c3: h.tensor_tensor(out=acc[:, :, :], in0=acc[:, :, :], in1=_bc(cb[:, c3 * 16:(c3 + 1) * 16], [128, 16, 128], 2), op=ALU.add),
                         reads=[acc_r, cb_r], writes=[acc_r])
                    P.op("act", lambda h, c3=c3: h.activation(out=xbcT[:, c3 * 16:(c3 + 1) * 16, :], in_=acc[:, :, :], func=AF.Silu),
                         reads=[acc_r], writes=[xbcT_r], partial=True)
                if t == 0:
                    P.op("pool", lambda h: h.memset(xbcT[:, :, 0:NPAD], 0.0), reads=[xbcT_r], writes=[xbcT_r])
                for q in range(5):
                    pb, pbr = psT[q % 2], psT_r[q % 2]
                    for k8 in range(8):
                        cc = q * 8 + k8
                        P.op("pe", lambda h, pb=pb, k8=k8, cc=cc: h.transpose(pb[:, k8 * 128:(k8 + 1) * 128], xbcT[:, cc, :], ident[:, :]),
                             reads=[xbcT_r, cx.c_r], writes=[pbr], sig=(k8 == 7), partial=(k8 > 0))
                    if q < 4:
                        P.op("act" if q % 2 else "dve", (lambda h, pb=pb, q=q: h.copy(out=xtok[:, q * 1024:(q + 1) * 1024], in_=pb[:, :])) if q % 2 else
                             (lambda h, pb=pb, q=q: h.tensor_copy(out=xtok[:, q * 1024:(q + 1) * 1024], in_=pb[:, :])),
                             reads=[pbr], writes=[xtok_r], partial=(q > 0))
                    else:
                        P.op("dve", lambda h, pb=pb: h.tensor_copy(out=btok[:, :], in_=pb[:, :]), reads=[pbr], writes=[btok_r])
                P.dma("sp", cx.XTOK[tok0:tok0 + 128, :], xtok[:, :], xtok_r.dsem, reads=[xtok_r], writes=[cx.XTOK_r], partial=True)
                P.dma("sp", cx.BTOK[tok0:tok0 + 128, :], btok[:, :], btok_r.dsem, reads=[btok_r], writes=[cx.BTOK_r], partial=True)
                P.dma("sp", cx.BCT[:, :, tok0:tok0 + 128], xbcT[:, 32:48, :], xbcT_r.dsem, reads=[xbcT_r], writes=[cx.BCT_r], partial=True)
            else:
                P.dma("sp", xtok[:, :], cx.XTOK[tok0:tok0 + 128, :], xtok_r.dsem, reads=[cx.XTOK_r], writes=[xtok_r])
                P.dma("sp", btok[:, :], cx.BTOK[tok0:tok0 + 128, :], btok_r.dsem, reads=[cx.BTOK_r], writes=[btok_r])
                P.dma("sp", xbcT[:, 32:48, :], cx.BCT[:, :, tok0:tok0 + 128], xbcT_r.dsem, reads=[cx.BCT_r], writes=[xbcT_r])
                P.dma("sp", yacc[:, :], cx.Y1[tok0:tok0 + 128, :], yacc_r.dsem, reads=[cx.Y1_r], writes=[yacc_r])
                P.dma("sp", zs[:, :], cx.ZS[tok0:tok0 + 128, :], zs_r.dsem, reads=[cx.ZS_r], writes=[zs_r])

            pb, pbr = psS, psS_r
            P.op("pe", lambda h, DAt=DAt: h.matmul(psS[:, 0:64], m_cum[:, :], DAt, start=True, stop=True), reads=[dar, cx.c_r], writes=[pbr])
            P.op("act", lambda h: h.activation(out=ecs[:, :], in_=psS[:, 0:64], func=AF.Exp), reads=[pbr], writes=[ecs_r])
            P.op("pe", lambda h, DAt=DAt: h.matmul(psS[:, 64:128], m_seg[:, :], DAt, start=True, stop=True), reads=[dar, cx.c_r], writes=[pbr])
            P.op("act", lambda h: h.activation(out=wst[:, :], in_=psS[:, 64:128], func=AF.Exp), reads=[pbr], writes=[wst_r])
            P.op("dve", lambda h, DTt=DTt: h.tensor_tensor(out=wst[:, :], in0=wst[:, :], in1=DTt, op=ALU.mult), reads=[wst_r, dtr], writes=[wst_r])
            P.op("pe", lambda h, DAt=DAt: h.matmul(psS[:, 128:192], cx.c_onesf[:, :], DAt, start=True, stop=True), reads=[dar, cx.c_r], writes=[pbr])
            P.op("act", lambda h: h.activation(out=dec[:, :], in_=psS[:, 128:192], func=AF.Exp), reads=[pbr], writes=[dec_r])
            P.op("pool", lambda h: h.tensor_tensor(out=xw[:, :].rearrange("p (h e) -> p h e", h=64), in0=xtok[:, :].rearrange("p (h e) -> p h e", h=64),
                                                  in1=_bc(wst[:, :], [128, 64, 64], 2), op=ALU.mult),
                 reads=[xtok_r, wst_r], writes=[xw_r])
            if sweep == 0:
                P.op("pool", lambda h: h.tensor_tensor(out=yacc[:, :].rearrange("p (h e) -> p h e", h=64), in0=xtok[:, :].rearrange("p (h e) -> p h e", h=64),
                                                      in1=_bc(dsk[:, :], [128, 64, 64], 2), op=ALU.mult),
                     reads=[xtok_r, dsk_r], writes=[yacc_r])
            for g in range(8):
                BTg = xbcT[:, 32 + g, :]
                CTg = xbcT[:, 40 + g, :]
                pCB, pCB_r = psA[0], psA_r[0]
                pSeg = [psA[1], psA[2]]; pSeg_r = [psA_r[1], psA_r[2]]
                pY, pY_r = psA[3], psA_r[3]
                pO, pO_r = psA[4], psA_r[4]
                hs = slice(g * 8, (g + 1) * 8)
                P.op("pe", lambda h, BTg=BTg, CTg=CTg, pCB=pCB: h.matmul(pCB[:, 0:128], BTg, CTg, start=True, stop=True), reads=[xbcT_r], writes=[pCB_r])
                P.op("dve", lambda h, pCB=pCB: h.tensor_tensor(out=CBm[:, :], in0=pCB[:, 0:128], in1=m_cum[:, :], op=ALU.mult), reads=[pCB_r, cx.c_r], writes=[CBm_r])
                P.op("pool", lambda h, DAt=DAt, hs=hs: h.tensor_tensor(out=aU[:, :, :], in0=_bc(DAt[:, hs], [128, 8, 128], 2), in1=_bc(m_cum[:, :], [128, 8, 128], 1), op=ALU.mult),
                     reads=[dar, cx.c_r], writes=[aU_r])
                for hf in range(2):
                    P.op("pe", lambda h, hf=hf, pSeg=pSeg: h.matmul(pSeg[hf][:, :], m_seg[:, :], aU[:, hf * 4:(hf + 1) * 4, :].rearrange("p a b -> p (a b)"), start=True, stop=True),
                         reads=[aU_r, cx.c_r], writes=[pSeg_r[hf]])
                    P.op("act", lambda h, hf=hf, pSeg=pSeg: h.activation(out=E[:, hf * 4:(hf + 1) * 4, :].rearrange("p a b -> p (a b)"), in_=pSeg[hf][:, :], func=AF.Exp),
                         reads=[pSeg_r[hf]], writes=[E_r], partial=(hf > 0))
                P.op("dve", lambda h, DTt=DTt, hs=hs: h.tensor_tensor(out=E[:, :, :], in0=E[:, :, :], in1=_bc(DTt[:, hs], [128, 8, 128], 2), op=ALU.mult),
                     reads=[E_r, dtr], writes=[E_r])
                P.op("dve", lambda h: h.tensor_tensor(out=MT[:, :, :], in0=E[:, :, :], in1=_bc(CBm[:, :], [128, 8, 128], 1), op=ALU.mult),
                     reads=[E_r, CBm_r], writes=[MT_r])
                for j in range(8):
                    hh = g * 8 + j
                    P.op("pe", lambda h, j=j, hh=hh, pY=pY: h.matmul(pY[:, j * 64:(j + 1) * 64], MT[:, j, :], xtok[:, hh * 64:(hh + 1) * 64], start=True, stop=True),
                         reads=[MT_r, xtok_r], writes=[pY_r], sig=(j == 7), partial=(j > 0))
                P.op("pe", lambda h, CTg=CTg, g=g, pO=pO: h.matmul(pO[:, :], CTg, Sb[:, g * 512:(g + 1) * 512], start=True, stop=True),
                     reads=[xbcT_r, Sb_r], writes=[pO_r])
                P.op("dve", lambda h, pO=pO, hs=hs: h.tensor_tensor(out=yos[:, :].rearrange("p (h e) -> p h e", h=8), in0=pO[:, :].rearrange("p (h e) -> p h e", h=8),
                                                                in1=_bc(ecs[:, hs], [128, 8, 64], 2), op=ALU.mult),
                     reads=[pO_r, ecs_r], writes=[yos_r])
                P.op("dve", lambda h, pY=pY: h.tensor_tensor(out=yos[:, :], in0=pY[:, :], in1=yos[:, :], op=ALU.add), reads=[pY_r, yos_r], writes=[yos_r])
                P.op("pool", lambda h, g=g: h.tensor_tensor(out=yacc[:, g * 512:(g + 1) * 512], in0=yacc[:, g * 512:(g + 1) * 512], in1=yos[:, :], op=ALU.add),
                     reads=[yos_r, yacc_r], writes=[yacc_r])
                P.op("pe", lambda h, g=g, pO=pO: h.matmul(pO[:, :], btok[:, g * 128:(g + 1) * 128], xw[:, g * 512:(g + 1) * 512], start=True, stop=True),
                     reads=[btok_r, xw_r], writes=[pO_r])
                P.op("pool", lambda h, g=g, hs=hs: h.tensor_tensor(out=S[:, g * 512:(g + 1) * 512].rearrange("p (h e) -> p h e", h=8),
                                                                 in0=S[:, g * 512:(g + 1) * 512].rearrange("p (h e) -> p h e", h=8),
                                                                 in1=_bc(dec[:, hs], [128, 8, 64], 2), op=ALU.mult),
                     reads=[S_r, dec_r, Sb_r], writes=[S_r])
                P.op("dve", lambda h, g=g, pO=pO: h.tensor_tensor(out=S[:, g * 512:(g + 1) * 512], in0=pO[:, :], in1=S[:, g * 512:(g + 1) * 512], op=ALU.add),
                     reads=[pO_r, S_r], writes=[S_r])
                P.op("act", lambda h, g=g: h.copy(out=Sb[:, g * 512:(g + 1) * 512], in_=S[:, g * 512:(g + 1) * 512]), reads=[S_r], writes=[Sb_r])
            if sweep == 0:
                P.dma("sp", cx.Y1[tok0:tok0 + 128, :], yacc[:, :], yacc_r.dsem, reads=[yacc_r], writes=[cx.Y1_r], partial=True)
            else:
                P.op("dve", lambda h: h.tensor_tensor(out=yacc[:, :], in0=yacc[:, :], in1=zs[:, :], op=ALU.mult), reads=[yacc_r, zs_r], writes=[yacc_r])
                for g in range(8):
                    P.op("act", lambda h, g=g: h.activation(out=ynb[:, g * 512:(g + 1) * 512], in_=yacc[:, g * 512:(g + 1) * 512], func=AF.Square, accum_out=sm[:, g:g + 1]),
                         reads=[yacc_r], writes=[ynb_r, sm_r])
                P.op("act", lambda h: h.activation(out=sm[:, 8:16], in_=sm[:, 0:8], func=AF.Sqrt, bias=EPS, scale=1.0 / 512), reads=[sm_r], writes=[sm_r])
                P.op("dve", lambda h: h.reciprocal(out=sm[:, 16:24], in_=sm[:, 8:16]), reads=[sm_r], writes=[sm_r])
                P.op("dve", lambda h: h.tensor_tensor(out=yacc[:, :].rearrange("p (g e) -> p g e", g=8), in0=yacc[:, :].rearrange("p (g e) -> p g e", g=8),
                                                     in1=_bc(sm[:, 16:24], [128, 8, 512], 2), op=ALU.mult), reads=[yacc_r, sm_r], writes=[yacc_r])
                P.op("dve", lambda h: h.tensor_tensor(out=ynb[:, :], in0=yacc[:, :], in1=gs[:, :], op=ALU.mult), reads=[yacc_r, gs_r], writes=[ynb_r])
                for q in range(4):
                    pb, pbr = psT[q % 2], psT_r[q % 2]
                    for k8 in range(8):
                        cc = q * 8 + k8
                        P.op("pe", lambda h, pb=pb, k8=k8, cc=cc: h.transpose(pb[:, k8 * 128:(k8 + 1) * 128], ynb[:, cc * 128:(cc + 1) * 128], ident[:, :]),
                             reads=[ynb_r, cx.c_r], writes=[pbr], sig=(k8 == 7), partial=(k8 > 0))
                    P.op("act" if q % 2 else "dve", (lambda h, pb=pb, q=q: h.copy(out=ynT[:, q * 8:(q + 1) * 8, :], in_=pb[:, :].rearrange("p (k n) -> p k n", k=8))) if q % 2 else
                         (lambda h, pb=pb, q=q: h.tensor_copy(out=ynT[:, q * 8:(q + 1) * 8, :], in_=pb[:, :].rearrange("p (k n) -> p k n", k=8))),
                         reads=[pbr], writes=[ynT_r], partial=(q > 0))
                P.dma("sp", cx.YNT[:, :, tok0:tok0 + 128], ynT[:, :, :], ynT_r.dsem, reads=[ynT_r], writes=[cx.YNT_r], partial=True)
        _flush(P)


def _blocks_n(NT, n):
    out = []
    t = 0
    while t < NT:
        k = min(n, NT - t)
        out.append((t, k))
        t += k
    return out


SLOPES = [2.0 ** (-8.0 * (h + 1) / 16.0) for h in range(16)]


def phase_C(cx):
    nc, P, NT = cx.nc, cx.P, cx.NT
    T = NT * 128
    with ExitStack() as st:
        sb = lambda n, s, dt_: st.enter_context(nc.sbuf_tensor(n, s, dt_))
        R = lambda n, **kw: P.res("c_" + n, **kw)
        KTh = sb("cc_KT", [128, T], BF16); KTh_r = R("KT", dma=True)
        QTh = sb("cc_QT", [128, T], BF16); QTh_r = R("QT", dma=True)
        Vh = sb("cc_V", [128, NT, 128], BF16); Vh_r = R("V", dma=True)
        absd = sb("cc_absd", [128, 2 * T + 512], F32); absd_r = R("absd", dma=True)
        padb = sb("cc_padb", [128, 1], F32); padb_r = R("padb", dma=True)
        lam4 = sb("c_lam4", [128, 4, 64], F32); lam4_r = R("lam4", dma=True)
        lsm = sb("c_lsm", [128, 8], F32); lsm_r = R("lsm")
        gsub = sb("c_gsub", [128, 1], F32); gsub_r = R("gsub", dma=True)
        tmpS = [sb("c_tmp%d" % i, [128, 512], F32) for i in range(2)]; tmpS_r = [R("tmp") for i in range(2)]
        tring = Ring(list(zip(tmpS, tmpS_r)))
        PT = [sb("c_PT%d" % i, [128, 512], BF16) for i in range(3)]; PT_r = [R("PT") for i in range(3)]
        pring = Ring(list(zip(PT, PT_r)))
        r0 = sb("c_r0", [128, 512], F32); r0_r = R("r0")
        r1 = sb("c_r1", [128, 512], F32); r1_r = R("r1")
        o0 = sb("c_o0", [128, 512], F32); o0_r = R("o0")
        o1 = sb("c_o1", [128, 512], F32); o1_r = R("o1")
        sqb = sb("c_sqb", [128, 512], BF16); sqb_r = R("sqb")
        ya = [sb("c_ya%d" % i, [128, 512], BF16) for i in range(2)]; ya_r = [R("ya", dma=True) for i in range(2)]
        yring = Ring(list(zip(ya, ya_r)))
        psS = [st.enter_context(nc.psum_tensor("c_psS%d" % i, [128, 512], F32)) for i in range(2)]
        psS_r = [R("psS", excl=True) for i in range(2)]
        sring = Ring(list(zip(psS, psS_r)))
        psO = [st.enter_context(nc.psum_tensor("c_psO%d" % i, [128, 512], F32)) for i in range(2)]
        psO_r = [R("psO", excl=True) for i in range(2)]
        psR = [st.enter_context(nc.psum_tensor("c_psR%d" % i, [128, 512], F32)) for i in range(2)]
        psR_r = [R("psR", excl=True) for i in range(2)]
        ones = cx.c_ones
        _barrier(P)
        P.dma("sp", absd[:, :], cx.cdr["absd"][:, :], absd_r.dsem, writes=[absd_r])
        P.dma("sp", padb[:, :], cx.cdr["padbias"][:, :], padb_r.dsem, writes=[padb_r])
        P.dma("sp", gsub[:, :], cx.gsub[:, :], gsub_r.dsem, writes=[gsub_r])
        P.op("dve", lambda h: h.tensor_scalar(out=gsub[:, :], in0=gsub[:, :], scalar1=0.8, scalar2=None, op0=ALU.mult), reads=[gsub_r], writes=[gsub_r])
        P.dma("sp", lam4[:, :, :].rearrange("p a b -> p (a b)"), cx.lam4[0:1, :].partition_broadcast(128), lam4_r.dsem, writes=[lam4_r])
        P.op("dve", lambda h: h.tensor_tensor(out=lam4[:, 0, :], in0=lam4[:, 0, :], in1=lam4[:, 1, :], op=ALU.mult), reads=[lam4_r], writes=[lam4_r])
        P.op("dve", lambda h: h.tensor_tensor(out=lam4[:, 2, :], in0=lam4[:, 2, :], in1=lam4[:, 3, :], op=ALU.mult), reads=[lam4_r], writes=[lam4_r])
        P.op("dve", lambda h: h.tensor_reduce(out=lsm[:, 0:1], in_=lam4[:, 0, :], axis=AX.X, op=ALU.add), reads=[lam4_r], writes=[lsm_r])
        P.op("dve", lambda h: h.tensor_reduce(out=lsm[:, 1:2], in_=lam4[:, 2, :], axis=AX.X, op=ALU.add), reads=[lam4_r, lsm_r], writes=[lsm_r])
        P.op("act", lambda h: h.activation(out=lsm[:, 2:4], in_=lsm[:, 0:2], func=AF.Exp), reads=[lsm_r], writes=[lsm_r])
        P.op("dve", lambda h: h.tensor_tensor(out=lsm[:, 4:5], in0=lsm[:, 3:4], in1=lsm[:, 2:3], op=ALU.subtract), reads=[lsm_r], writes=[lsm_r])
        P.op("dve", lambda h: h.tensor_scalar(out=lsm[:, 4:5], in0=lsm[:, 4:5], scalar1=-0.2, scalar2=None, op0=ALU.add), reads=[lsm_r], writes=[lsm_r])
        nlam = lsm[:, 4:5]
        for hd in range(16):
            m_h = SLOPES[hd]
            P.dma("sp", KTh[:, :], cx.KT[:, hd, :], KTh_r.dsem, reads=[cx.KT_r], writes=[KTh_r])
            P.dma("sp", QTh[:, :], cx.QT[:, hd, :], QTh_r.dsem, reads=[cx.QT_r], writes=[QTh_r])
            P.dma("sp", Vh[:, :, :], cx.V[:, hd * 128:(hd + 1) * 128].rearrange("(t p) c -> p t c", p=128), Vh_r.dsem, reads=[cx.V_r], writes=[Vh_r])
            for (t0, nt) in _blocks_n(NT, 4):
                W = nt * 128
                q0 = t0 * 128
                for kt in range(NT):
                    u0 = T + 128 * (t0 - kt)
                    for i in range(2):
                        pS, pS_r = sring.next()
                        tm, tm_r = tring.next()
                        pt, pt_r = pring.next()
                        P.op("pe", lambda h, W=W, pS=pS, i=i, kt=kt, q0=q0: h.matmul(pS[:, 0:W], KTh[i * 64:(i + 1) * 64, kt * 128:(kt + 1) * 128],
                                                                               QTh[i * 64:(i + 1) * 64, q0:q0 + W], start=True, stop=True),
                             reads=[KTh_r, QTh_r], writes=[pS_r])
                        P.op("dve", lambda h, W=W, pS=pS, tm=tm, u0=u0, m_h=m_h: h.scalar_tensor_tensor(out=tm[:, 0:W], in0=absd[:, u0:u0 + W], scalar=-m_h, in1=pS[:, 0:W],
                                                                                                  op0=ALU.mult, op1=ALU.add),
                             reads=[pS_r, absd_r], writes=[tm_r])
                        if kt == 0:
                            P.op("act", lambda h, W=W, pt=pt, tm=tm: h.activation(out=pt[:, 0:W], in_=tm[:, 0:W], func=AF.Exp, bias=padb[:, 0:1], scale=1.0),
                                 reads=[tm_r, padb_r], writes=[pt_r])
                        else:
                            P.op("act", lambda h, W=W, pt=pt, tm=tm: h.activation(out=pt[:, 0:W], in_=tm[:, 0:W], func=AF.Exp), reads=[tm_r], writes=[pt_r])
                        P.op("pe", lambda h, W=W, pt=pt, i=i, kt=kt: h.matmul(psO[i][:, 0:W], Vh[:, kt, :], pt[:, 0:W], start=(kt == 0), stop=(kt == NT - 1)),
                             reads=[Vh_r, pt_r], writes=[psO_r[i]], sig=False, partial=(kt > 0))
                        P.op("pe", lambda h, W=W, pt=pt, i=i, kt=kt: h.matmul(psR[i][:, 0:W], ones[:, :], pt[:, 0:W], start=(kt == 0), stop=(kt == NT - 1)),
                             reads=[cx.c_r, pt_r], writes=[psR_r[i]], partial=(kt > 0))
                P.op("dve", lambda h, W=W: h.reciprocal(out=r0[:, 0:W], in_=psR[0][:, 0:W]), reads=[psR_r[0]], writes=[r0_r])
                P.op("dve", lambda h, W=W: h.reciprocal(out=r1[:, 0:W], in_=psR[1][:, 0:W]), reads=[psR_r[1]], writes=[r1_r])
                P.op("dve", lambda h, W=W: h.tensor_tensor(out=o0[:, 0:W], in0=psO[0][:, 0:W], in1=r0[:, 0:W], op=ALU.mult), reads=[psO_r[0], r0_r], writes=[o0_r])
                P.op("dve", lambda h, W=W: h.tensor_tensor(out=o1[:, 0:W], in0=psO[1][:, 0:W], in1=r1[:, 0:W], op=ALU.mult), reads=[psO_r[1], r1_r], writes=[o1_r])
                P.op("dve", lambda h, W=W: h.scalar_tensor_tensor(out=o0[:, 0:W], in0=o1[:, 0:W], scalar=nlam, in1=o0[:, 0:W], op0=ALU.mult, op1=ALU.add),
                     reads=[o0_r, o1_r, lsm_r], writes=[o0_r])
                P.op("act", lambda h, W=W: h.activation(out=sqb[:, 0:W], in_=o0[:, 0:W], func=AF.Square), reads=[o0_r], writes=[sqb_r])
                P.op("pe", lambda h, W=W: h.matmul(psR[0][:, 0:W], ones[:, :], sqb[:, 0:W], start=True, stop=True), reads=[cx.c_r, sqb_r], writes=[psR_r[0]])
                P.op("act", lambda h, W=W: h.activation(out=r0[:, 0:W], in_=psR[0][:, 0:W], func=AF.Sqrt, bias=EPS, scale=1.0 / 128), reads=[psR_r[0]], writes=[r0_r])
                P.op("dve", lambda h, W=W: h.reciprocal(out=r0[:, 0:W], in_=r0[:, 0:W]), reads=[r0_r], writes=[r0_r])
                y_, y_r = yring.next()
                P.op("dve", lambda h, W=W, y_=y_: h.scalar_tensor_tensor(out=y_[:, 0:W], in0=o0[:, 0:W], scalar=gsub[:, 0:1], in1=r0[:, 0:W], op0=ALU.mult, op1=ALU.mult),
                     reads=[o0_r, r0_r, gsub_r], writes=[y_r])
                P.dma("sp", cx.YAT[:, hd, q0:q0 + W], y_[:, 0:W], y_r.dsem, reads=[y_r], writes=[cx.YAT_r], partial=True)
        _flush(P)


def _norm_transpose(cx, P, src_rows_ap, src_res, xt, xt_r, sm, sm_r, gcol, gcol_r, ps, ps_r, hT, hT_r, i, identf):
    P.dma("sp", xt[:, :], src_rows_ap, xt_r.dsem, reads=[src_res] if src_res is not None else [], writes=[xt_r])
    P.op("act", lambda h: h.activation(out=cx.junk[:, :], in_=xt[:, :], func=AF.Square, accum_out=sm[:, 0:1]), reads=[xt_r], writes=[cx.junk_r, sm_r])
    P.op("act", lambda h: h.activation(out=sm[:, 1:2], in_=sm[:, 0:1], func=AF.Sqrt, bias=EPS, scale=1.0 / D), reads=[sm_r], writes=[sm_r])
    P.op("dve", lambda h: h.reciprocal(out=sm[:, 2:3], in_=sm[:, 1:2]), reads=[sm_r], writes=[sm_r])
    P.op("dve", lambda h: h.tensor_scalar(out=xt[:, :], in0=xt[:, :], scalar1=sm[:, 2:3], scalar2=None, op0=ALU.mult), reads=[xt_r, sm_r], writes=[xt_r])
    for q4 in range(8):
        pb, pbr = ps[q4 % 2], ps_r[q4 % 2]
        for k4 in range(4):
            kc = q4 * 4 + k4
            P.op("pe", lambda h, pb=pb, k4=k4, kc=kc: h.transpose(pb[:, k4 * 128:(k4 + 1) * 128], xt[:, kc * 128:(kc + 1) * 128], identf[:, :]),
                 reads=[xt_r, cx.c_r], writes=[pbr], sig=(k4 == 3), partial=(k4 > 0))
        P.op("dve", lambda h, pb=pb, q4=q4, i=i: h.tensor_tensor(
            out=hT[:, q4 * 4:(q4 + 1) * 4, i * 128:(i + 1) * 128],
            in0=pb[:, :].rearrange("p (k n) -> p k n", k=4),
            in1=gcol[:, q4 * 4:(q4 + 1) * 4].unsqueeze(2).to_broadcast([128, 4, 128]), op=ALU.mult),
            reads=[pbr, gcol_r], writes=[hT_r], partial=True)


def phase_D1(cx):
    nc, P, NT = cx.nc, cx.P, cx.NT
    with ExitStack() as st:
        sb = lambda n, s, dt_: st.enter_context(nc.sbuf_tensor(n, s, dt_))
        R = lambda n, **kw: P.res("d_" + n, **kw)
        WB = 256
        hnT = sb("d_hnT", [128, NKC, WB], BF16); hnT_r = R("hnT", dma=True)
        ynT = sb("d_ynT", [128, NKC, WB], BF16); ynT_r = R("ynT", dma=True)
        yaT = sb("d_yaT", [128, 16, WB], BF16); yaT_r = R("yaT", dma=True)
        mxT = sb("d_mxT", [128, NKC, WB], BF16); mxT_r = R("mxT")
        G1 = sb("d_G1", [128, 4, WB], F32); G1_r = R("G1")
        G2 = sb("d_G2", [128, 4, WB], F32); G2_r = R("G2")
        M = sb("d_M", [128, 4, WB], F32); M_r = R("M")
        bg = sb("d_bg", [128, 64], F32); bg_r = R("bg", dma=True)
        wt = [sb("d_w%d" % i, [128, 16, 512], BF16) for i in range(3)]; wt_r = [R("w", dma=True) for i in range(3)]
        wring = Ring(list(zip(wt, wt_r)))
        xs_ = [sb("d_xs%d" % i, [128, 512], F32) for i in range(3)]; xs_r = [R("xs", dma=True) for i in range(3)]
        xring = Ring(list(zip(xs_, xs_r)))
        ps = [st.enter_context(nc.psum_tensor("d_ps%d" % i, [128, 512], F32)) for i in range(8)]
        ps_r = [R("ps", excl=True) for i in range(8)]
        _barrier(P)
        P.dma("sp", bg[:, :], cx.bgcol[:, :], bg_r.dsem, writes=[bg_r])
        wbg = cx.w_bg.rearrange("(kc p) n -> p kc n", p=128)
        wso = cx.w_so.rearrange("(kc p) n -> p kc n", p=128)
        wao = cx.w_ao.rearrange("(kc p) n -> p kc n", p=128)
        wo = cx.w_o.rearrange("(kc p) n -> p kc n", p=128)
        grp = [0]

        def load_w(src, kh, c0, nk=16):
            w, wr = wring.next()
            P.dma("pool", w[:, 0:nk, :], src[:, kh * 16:kh * 16 + nk, c0:c0 + 512], wr.dsem, writes=[wr])
            return w, wr

        for (t0, nt) in _blocks_n(NT, 2):
            W = nt * 128
            tok0 = t0 * 128
            P.dma("sp", hnT[:, :, 0:W], cx.HNT[:, :, tok0:tok0 + W], hnT_r.dsem, reads=[cx.HNT_r], writes=[hnT_r])
            P.dma("sp", ynT[:, :, 0:W], cx.YNT[:, :, tok0:tok0 + W], ynT_r.dsem, reads=[cx.YNT_r], writes=[ynT_r])
            P.dma("sp", yaT[:, :, 0:W], cx.YAT[:, :, tok0:tok0 + W], yaT_r.dsem, reads=[cx.YAT_r], writes=[yaT_r])

            def proj_b(src, c0, act_, act_r, nkh, kind, cg):
                base = 4 * (grp[0] % 2)
                grp[0] += 1
                nkc = nkh * 16
                for kh in range(nkh):
                    w, wr = load_w(src, kh, c0)
                    for j in range(4):
                        pb, pbr = ps[base + j], ps_r[base + j]
                        for k in range(16):
                            kc = kh * 16 + k
                            P.op("pe", lambda h, W=W, pb=pb, w=w, k=k, j=j, kc=kc, act_=act_, nkc=nkc: h.matmul(pb[:, 0:W], w[:, k, j * 128:(j + 1) * 128], act_[:, kc, 0:W],
                                                                                                      start=(kc == 0), stop=(kc == nkc - 1)),
                                 reads=[act_r, wr], writes=[pbr], sig=(k == 15), partial=(kc > 0))
                for j in range(4):
                    pb, pbr = ps[base + j], ps_r[base + j]
                    ch = cg * 4 + j
                    if kind == "g1":
                        P.op("act", lambda h, W=W, pb=pb, j=j, ch=ch: h.activation(out=G1[:, j, 0:W], in_=pb[:, 0:W], func=AF.Sigmoid, bias=bg[:, ch:ch + 1], scale=1.0),
                             reads=[pbr, bg_r], writes=[G1_r], partial=(j > 0))
                    elif kind == "g2":
                        P.op("act", lambda h, W=W, pb=pb, j=j, ch=ch: h.activation(out=G2[:, j, 0:W], in_=pb[:, 0:W], func=AF.Sigmoid, bias=bg[:, 32 + ch:33 + ch], scale=1.0),
                             reads=[pbr, bg_r], writes=[G2_r], partial=(j > 0))
                    elif kind == "s":
                        P.op("dve", lambda h, W=W, pb=pb, j=j: h.tensor_tensor(out=M[:, j, 0:W], in0=pb[:, 0:W], in1=G1[:, j, 0:W], op=ALU.mult),
                             reads=[pbr, G1_r], writes=[M_r], partial=(j > 0))
                    else:
                        P.op("dve", lambda h, W=W, pb=pb, j=j: h.tensor_tensor(out=G2[:, j, 0:W], in0=pb[:, 0:W], in1=G2[:, j, 0:W], op=ALU.mult),
                             reads=[pbr, G2_r], writes=[G2_r], partial=(j > 0))
                        P.op("pool", lambda h, W=W, j=j, ch=ch: h.tensor_tensor(out=mxT[:, ch, 0:W], in0=M[:, j, 0:W], in1=G2[:, j, 0:W], op=ALU.add),
                             reads=[M_r, G2_r], writes=[mxT_r], partial=True)

            for cg in range(8):
                proj_b(wbg, cg * 512, hnT, hnT_r, 2, "g1", cg)
                proj_b(wbg, D + cg * 512, hnT, hnT_r, 2, "g2", cg)
                proj_b(wso, cg * 512, ynT, ynT_r, 2, "s", cg)
                proj_b(wao, cg * 512, yaT, yaT_r, 1, "a", cg)
            for cg in range(8):
                base = 4 * (grp[0] % 2)
                grp[0] += 1
                for kh in range(2):
                    w, wr = load_w(wo, kh, cg * 512)
                    for i in range(nt):
                        pb, pbr = ps[base + i], ps_r[base + i]
                        for k in range(16):
                            kc = kh * 16 + k
                            P.op("pe", lambda h, pb=pb, w=w, k=k, i=i, kc=kc: h.matmul(pb[:, :], mxT[:, kc, i * 128:(i + 1) * 128], w[:, k, :],
                                                                                 start=(kc == 0), stop=(kc == NKC - 1)),
                                 reads=[mxT_r, wr], writes=[pbr], sig=(k == 15), partial=(kc > 0))
                for i in range(nt):
                    pb, pbr = ps[base + i], ps_r[base + i]
                    r0_ = tok0 + i * 128
                    x_, x_r = xring.next()
                    P.dma("sp", x_[:, :], cx.xin[r0_:r0_ + 128, cg * 512:(cg + 1) * 512], x_r.dsem, writes=[x_r])
                    P.op("dve", lambda h, pb=pb, x_=x_: h.tensor_tensor(out=x_[:, :], in0=pb[:, :], in1=x_[:, :], op=ALU.add), reads=[pbr, x_r], writes=[x_r])
                    P.dma("sp", cx.H1[r0_:r0_ + 128, cg * 512:(cg + 1) * 512], x_[:, :], x_r.dsem, reads=[x_r], writes=[cx.H1_r], partial=True)
        _flush(P)


def phase_D2(cx):
    nc, P, NT = cx.nc, cx.P, cx.NT
    NFC = D_FF // 128
    with ExitStack() as st:
        sb = lambda n, s, dt_: st.enter_context(nc.sbuf_tensor(n, s, dt_))
        R = lambda n, **kw: P.res("f_" + n, **kw)
        WB = 256
        xt = sb("f_xt", [128, D], F32); xt_r = R("xt", dma=True)
        cx.junk = sb("f_junk", [128, D], BF16); cx.junk_r = R("junk")
        hfT = sb("f_hfT", [128, NKC, WB], BF16); hfT_r = R("hfT")
        acT = sb("f_acT", [128, NFC, WB], BF16); acT_r = R("acT")
        Gs = sb("f_Gs", [128, 4, WB], F32); Gs_r = R("Gs")
        gcol = sb("f_gcol", [128, NKC], F32); gcol_r = R("gcol", dma=True)
        sm = sb("f_sm", [128, 8], F32); sm_r = R("sm")
        wt = [sb("f_w%d" % i, [128, 16, 512], BF16) for i in range(3)]; wt_r = [R("w", dma=True) for i in range(3)]
        wring = Ring(list(zip(wt, wt_r)))
        xs_ = [sb("f_xs%d" % i, [128, 512], F32) for i in range(3)]; xs_r = [R("xs", dma=True) for i in range(3)]
        xring = Ring(list(zip(xs_, xs_r)))
        ps = [st.enter_context(nc.psum_tensor("f_ps%d" % i, [128, 512], F32)) for i in range(8)]
        ps_r = [R("ps", excl=True) for i in range(8)]
        _barrier(P)
        P.dma("sp", gcol[:, :], cx.gcol_ffn[:, :], gcol_r.dsem, writes=[gcol_r])
        wgu = cx.w_gu.rearrange("(kc p) n -> p kc n", p=128)
        wdn = cx.w_dn.rearrange("(kc p) n -> p kc n", p=128)
        grp = [0]
        for (t0, nt) in _blocks_n(NT, 2):
            W = nt * 128
            tok0 = t0 * 128
            for i in range(nt):
                r0_ = tok0 + i * 128
                _norm_transpose(cx, P, cx.H1[r0_:r0_ + 128, :], cx.H1_r, xt, xt_r, sm, sm_r, gcol, gcol_r, ps, ps_r, hfT, hfT_r, i, cx.c_identf)
            f0 = 0
            while f0 < NFC:
                nch = min(4, NFC - f0)
                for which in (0, 1):
                    base = 4 * (grp[0] % 2)
                    grp[0] += 1
                    c0 = which * D_FF + f0 * 128
                    for kh in range(2):
                        w, wr = wring.next()
                        P.dma("pool", w[:, :, 0:nch * 128], wgu[:, kh * 16:(kh + 1) * 16, c0:c0 + nch * 128], wr.dsem, writes=[wr])
                        for j in range(nch):
                            pb, pbr = ps[base + j], ps_r[base + j]
                            for k in range(16):
                                kc = kh * 16 + k
                                P.op("pe", lambda h, W=W, pb=pb, w=w, k=k, j=j, kc=kc: h.matmul(pb[:, 0:W], w[:, k, j * 128:(j + 1) * 128], hfT[:, kc, 0:W],
                                                                                     start=(kc == 0), stop=(kc == NKC - 1)),
                                     reads=[hfT_r, wr], writes=[pbr], sig=(k == 15), partial=(kc > 0))
                    for j in range(nch):
                        pb, pbr = ps[base + j], ps_r[base + j]
                        if which == 0:
                            P.op("act", lambda h, W=W, pb=pb, j=j: h.activation(out=Gs[:, j, 0:W], in_=pb[:, 0:W], func=AF.Silu), reads=[pbr], writes=[Gs_r], partial=(j > 0))
                        else:
                            P.op("dve", lambda h, W=W, pb=pb, j=j, f0=f0: h.tensor_tensor(out=acT[:, f0 + j, 0:W], in0=pb[:, 0:W], in1=Gs[:, j, 0:W], op=ALU.mult),
                                 reads=[pbr, Gs_r], writes=[acT_r], partial=True)
                f0 += nch
            for cg in range(8):
                base = 4 * (grp[0] % 2)
                grp[0] += 1
                fu = 0
                while fu < NFC:
                    nk = min(16, NFC - fu)
                    w, wr = wring.next()
                    P.dma("pool", w[:, 0:nk, :], wdn[:, fu:fu + nk, cg * 512:(cg + 1) * 512], wr.dsem, writes=[wr])
                    for i in range(nt):
                        pb, pbr = ps[base + i], ps_r[base + i]
                        for k in range(nk):
                            fc = fu + k
                            P.op("pe", lambda h, pb=pb, w=w, k=k, i=i, fc=fc: h.matmul(pb[:, :], acT[:, fc, i * 128:(i + 1) * 128], w[:, k, :],
                                                                                 start=(fc == 0), stop=(fc == NFC - 1)),
                                 reads=[acT_r, wr], writes=[pbr], sig=(k == nk - 1), partial=(fc > 0))
                    fu += nk
                for i in range(nt):
                    t = t0 + i
                    pb, pbr = ps[base + i], ps_r[base + i]
                    if t == 0:
                        continue
                    r0_ = t * 128
                    x_, x_r = xring.next()
                    P.dma("sp", x_[:, :], cx.H1[r0_:r0_ + 128, cg * 512:(cg + 1) * 512], x_r.dsem, reads=[cx.H1_r], writes=[x_r])
                    P.op("dve", lambda h, pb=pb, x_=x_: h.tensor_tensor(out=x_[:, :], in0=pb[:, :], in1=x_[:, :], op=ALU.add), reads=[pbr, x_r], writes=[x_r])
                    P.dma("sp", cx.out[r0_ - 128:r0_, cg * 512:(cg + 1) * 512], x_[:, :], x_r.dsem, reads=[x_r], writes=[cx.out_r], partial=True)
        _flush(P)


def build(NT, phases="A", dbg=()):
    T = NT * 128
    nc = bass.Bass("TRN2", target_bir_lowering=False)
    cx = Cx()
    cx.nc, cx.NT = nc, NT
    kin = "ExternalInput"

    def din(name, shape, dt=F32):
        return nc.dram_tensor(name, list(shape), dt, kind=kin).ap()

    def dscr(name, shape, dt):
        kind = "ExternalOutput" if name in dbg else "Internal"
        return nc.dram_tensor(name, list(shape), dt, kind=kind).ap()

    cx.xin = din("xin", [T, D])
    cx.w_in = din("w_in", [D, D_IN])
    cx.gcol_mix = din("gcol_mix", [128, NKC])
    cx.dt_bias = din("dt_bias", [1, 128])
    cx.a_log = din("a_log", [1, 128])
    cx.gqk = din("gqk", [128, 2])
    cx.convw = din("convw", [128, 48, 5])
    cx.convb = din("convb", [128, 48])
    cx.d_skip = din("d_skip", [1, 64])
    cx.ssd_norm_g = din("ssd_norm_g", [1, D])
    cx.gsub = din("gsub", [128, 1])
    cx.lam4 = din("lam4", [1, 256])
    cx.bgcol = din("bgcol", [128, 64])
    cx.gcol_ffn = din("gcol_ffn", [128, NKC])
    cx.w_bg = din("w_bg", [D, 2 * D])
    cx.w_so = din("w_so", [D, D])
    cx.w_ao = din("w_ao", [2048, D])
    cx.w_o = din("w_o", [D, D])
    cx.w_gu = din("w_gu", [D, 2 * D_FF])
    cx.w_dn = din("w_dn", [D_FF, D])
    consts = make_consts(NT)
    cdr = {}
    for k, v in consts.items():
        cdr[k] = din("c_" + k, v.shape, BF16 if v.dtype == ml_dtypes.bfloat16 else F32)
    cx.HNT = dscr("HNT", [128, NKC, T], BF16)
    cx.XBC = dscr("XBC", [128, 48, T + 4], F32)
    cx.QT = dscr("QT", [128, 16, T], BF16)
    cx.KT = dscr("KT", [128, 16, T], BF16)
    cx.V = dscr("V", [T, 2048], BF16)
    cx.ZS = dscr("ZS", [T, D], F32)
    cx.DTD = dscr("DTD", [128, NT, 128], F32)
    cx.XTOK = dscr("XTOK", [T, D], BF16)
    cx.BTOK = dscr("BTOK", [T, 1024], BF16)
    cx.BCT = dscr("BCT", [128, 16, T], BF16)
    cx.Y1 = dscr("Y1", [T, D], F32)
    cx.YNT = dscr("YNT", [128, NKC, T], BF16)
    cx.YAT = dscr("YAT", [128, 16, T], BF16)
    cx.H1 = dscr("H1", [T, D], F32)
    cx.out = nc.dram_tensor("out", [T - 128, D], F32, kind="ExternalOutput").ap()

    with ExitStack() as st:
        P = Prog(nc, st)
        cx.P = P
        for n in ("HNT", "XBC", "QT", "KT", "V", "ZS", "DTD", "XTOK", "BTOK", "BCT", "Y1", "YNT", "YAT", "H1", "out"):
            setattr(cx, n + "_r", P.res(n))
        sb = lambda n, s, d: st.enter_context(nc.sbuf_tensor(n, s, d))
        cx.DT = sb("DT", [128, NT, 128], F32)
        cx.DA = sb("DA", [128, NT, 128], F32)
        cx.DT_r = [P.res("DT") for _ in range(NT)]
        cx.DA_r = [P.res("DA") for _ in range(NT)]
        cx.c_r = P.res("consts", dma=True)
        for k in ("ident", "blk64", "ones", "m_kles", "m_kgts", "m_kgel", "m_klts", "onesf"):
            v = consts[k]
            t_ = sb("s_" + k, list(v.shape), BF16 if v.dtype == ml_dtypes.bfloat16 else F32)
            setattr(cx, "c_" + k, t_)
            P.dma("sp", t_[:, :], cdr[k][:, :], cx.c_r.dsem, writes=[cx.c_r], partial=True)
        cx.c_identf = sb("s_identf", [128, 128], F32)
        P.op("dve", lambda h: h.tensor_copy(out=cx.c_identf[:, :], in_=cx.c_ident[:, :]), reads=[cx.c_r], writes=[cx.c_r], partial=True)
        cx.cdr = cdr
        _flush(P)
        if "A" in phases:
            phase_A(cx)
        if "B" in phases:
            phase_B(cx, 0)
            phase_B(cx, 1)
        if "C" in phases:
            phase_C(cx)
        if "D" in phases:
            phase_D1(cx)
            phase_D2(cx)
        if "DTD" in dbg:
            _barrier(P)
            dd = P.new_dsem("dtd")
            P.dma("sp", cx.DTD[:, :, :], cx.DT[:, :, :], dd, reads=cx.DT_r)
        _barrier(P)
        _flush(P, final=True)
    return nc, consts


def host_inputs(NT, seq, meta_tokens, w, consts):
    T = NT * 128
    xin = np.zeros((T, D), np.float32)
    xin[NPAD:128] = meta_tokens
    xin[128:] = seq
    m = {"xin": xin}
    m["w_in"] = w["w_in"][0]
    m["gcol_mix"] = np.ascontiguousarray(w["norm_mix_g"][0].reshape(NKC, 128).T)
    m["dt_bias"] = np.ascontiguousarray(w["dt_bias"][0].reshape(1, 128))
    m["a_log"] = np.ascontiguousarray(w["a_log"][0].reshape(1, 128))
    gqk = np.stack([np.tile(w["q_norm_g"][0], 2), np.tile(w["k_norm_g"][0], 2)], axis=1)
    m["gqk"] = np.ascontiguousarray(gqk.astype(np.float32))
    m["convw"] = np.ascontiguousarray(w["conv_w"][0].T.reshape(48, 128, 5).transpose(1, 0, 2))
    m["convb"] = np.ascontiguousarray(w["conv_b"][0].reshape(48, 128).T)
    m["d_skip"] = np.ascontiguousarray(w["d_skip"][0].reshape(1, 64))
    m["ssd_norm_g"] = np.ascontiguousarray(w["ssd_norm_g"][0].reshape(1, D))
    m["gsub"] = np.ascontiguousarray(w["subln_g"][0].reshape(128, 1))
    m["lam4"] = np.ascontiguousarray(np.concatenate([w["lambda_q1"][0], w["lambda_k1"][0], w["lambda_q2"][0], w["lambda_k2"][0]]).reshape(1, 256))
    m["bgcol"] = np.ascontiguousarray(w["b_branch_gate"][0].reshape(64, 128).T)
    m["gcol_ffn"] = np.ascontiguousarray(w["norm_ffn_g"][0].reshape(NKC, 128).T)
    m["w_bg"] = w["w_branch_gate"][0]
    m["w_so"] = w["w_ssd_out"][0]
    m["w_ao"] = w["w_att_out"][0]
    m["w_o"] = w["w_o"][0]
    m["w_gu"] = w["w_gate_up"][0]
    m["w_dn"] = w["w_down"][0]
    for k, v in consts.items():
        m["c_" + k] = v
    return m


def kernel(**inputs):
    NT = 33
    w = {k: np.asarray(v) for k, v in inputs.items()}
    nc, consts = build(NT, phases="ABCD")
    xp, xs = w["x_prompt"], w["x_sample"]
    seqs = [xp[0], xp[1], xs[0], xs[1], xs[2], xs[3], xp[0], xp[1]]
    meta = w["meta_tokens"]
    in_maps = [host_inputs(NT, s_, meta, w, consts) for s_ in seqs]
    res = run_bass_kernel_spmd(nc, in_maps, core_ids=list(range(8)))
    outs = [np.asarray(r["out"], dtype=np.float32) for r in res.results]
    y_prompt = np.stack(outs[0:2], axis=0)
    y_sample = np.stack(outs[2:6], axis=0)
    return (y_prompt, y_sample)
```

```python
import numpy as np
import ml_dtypes
from contextlib import ExitStack
import concourse.bass as bass
import concourse.mybir as mybir
from concourse.bass_utils import run_bass_kernel_spmd

F32 = mybir.dt.float32
BF16 = mybir.dt.bfloat16
AF = mybir.ActivationFunctionType
ALU = mybir.AluOpType
AX = mybir.AxisListType

D = 4096
NKC = 32
D_IN = 16512
C_XBC = 6144
O_XBC = 4096
O_DT = 10240
O_Q = 10368
O_K = 12416
O_V = 14464
D_FF = 11008
EPS = 1e-6
NPAD = 112


class Res:
    __slots__ = ("name", "lw", "rd", "dsem", "excl", "base")

    def __init__(self, name, dsem=None, excl=False):
        self.name = name
        self.lw = {}
        self.rd = {}
        self.dsem = dsem
        self.excl = excl
        self.base = {}


class DSem:
    def __init__(self, h):
        self.h = h
        self.count = 0


class Eng:
    def __init__(self, name, sem):
        self.name = name
        self.sem = sem
        self.count = 0
        self.waited = {}
        self.ops = []
        self.pending_nosig = False


class Prog:
    def __init__(self, nc, stack):
        self.nc = nc
        self.stack = stack
        self.E = {}
        self.semobj = {}
        for n in ("pe", "act", "dve", "pool", "sp"):
            s = stack.enter_context(nc.semaphore("e_" + n))
            self.E[n] = Eng(n, s)
            self.semobj[id(s)] = ("eng", self.E[n], s)
        self.dsems = []
        self.nres = 0

    def new_dsem(self, name):
        s = self.stack.enter_context(self.nc.semaphore("d_" + name))
        d = DSem(s)
        self.semobj[id(s)] = ("dma", d, s)
        self.dsems.append(d)
        return d

    def res(self, name, dma=False, excl=False):
        self.nres += 1
        return Res(name, self.new_dsem(name + str(self.nres)) if dma else None, excl)

    def _waits(self, eng, reads, writes, partial):
        need = {}

        def add(tok):
            for k, v in tok.items():
                if need.get(k, 0) < v:
                    need[k] = v

        for r in reads:
            add(r.lw)
            if r.excl:
                add(r.rd)
        for w in writes:
            if not partial:
                add(w.lw)
            else:
                add(w.base)
            add(w.rd)
        out = []
        for k, v in need.items():
            kind, obj, h = self.semobj[k]
            if kind == "dma":
                v = obj.count
            else:
                if obj is eng and eng.name == "pe":
                    continue
            if eng.waited.get(k, 0) >= v:
                continue
            eng.waited[k] = v
            out.append((h, v))
        return out

    def _commit(self, key, val, reads, writes, partial):
        for r in reads:
            if r.rd.get(key, 0) < val:
                r.rd[key] = val
        for w in writes:
            if partial:
                if w.lw.get(key, 0) < val:
                    w.lw[key] = val
            else:
                b = dict(w.lw)
                for k2, v2 in w.rd.items():
                    if b.get(k2, 0) < v2:
                        b[k2] = v2
                w.base = b
                w.lw = {key: val}
                w.rd = {}

    def op(self, en, fn, reads=(), writes=(), sig=True, partial=False):
        eng = self.E[en]
        waits = self._waits(eng, reads, writes, partial)
        if sig:
            eng.count += 1
            val = eng.count
            inc = (eng.sem, 1)
            eng.pending_nosig = False
        else:
            val = eng.count + 1
            inc = None
            eng.pending_nosig = True
        eng.ops.append((waits, fn, inc))
        self._commit(id(eng.sem), val, reads, writes, partial)

    def dma(self, qn, out, in_, dsem, reads=(), writes=(), partial=False):
        eng = self.E[qn]
        waits = self._waits(eng, reads, writes, partial)
        dsem.count += 16
        eng.ops.append((waits, lambda h: h.dma_start(out=out, in_=in_), (dsem.h, 16)))
        self._commit(id(dsem.h), dsem.count, reads, writes, partial)

    def emit(self):
        nc = self.nc
        sp = self.E["sp"]
        fin = [(d.h, d.count) for d in self.dsems if d.count > 0]
        for n, e in self.E.items():
            assert not e.pending_nosig, n
        with nc.Block() as block:
            def mk(eng, final):
                def f(h):
                    for waits, fn, inc in eng.ops:
                        for (s, v) in waits:
                            h.wait_ge(s, v)
                        ins = fn(h)
                        if inc is not None:
                            ins.then_inc(inc[0], inc[1])
                    if final:
                        for (s, v) in fin:
                            h.wait_ge(s, v)
                return f
            block.tensor(mk(self.E["pe"], False))
            block.scalar(mk(self.E["act"], False))
            block.vector(mk(self.E["dve"], False))
            block.gpsimd(mk(self.E["pool"], False))
            block.sync(mk(self.E["sp"], True))


class Ring:
    def __init__(self, items):
        self.items = items
        self.i = 0

    def next(self):
        it = self.items[self.i % len(self.items)]
        self.i += 1
        return it


def make_consts(NT):
    T = NT * 128
    i = np.arange(128)[:, None]
    j = np.arange(128)[None, :]
    c = {}
    c["ident"] = (i == j).astype(ml_dtypes.bfloat16)
    c["blk64"] = ((i // 64) == (j // 64)).astype(ml_dtypes.bfloat16)
    c["ones"] = np.ones((128, 128), ml_dtypes.bfloat16)
    u = np.arange(2 * T + 512)[None, :]
    c["absd"] = np.abs(u - i - T).astype(np.float32)
    pb = np.zeros((128, 1), np.float32)
    pb[:NPAD] = -30000.0
    c["padbias"] = pb
    c["m_kles"] = (i <= j).astype(np.float32)
    c["m_kgts"] = (i > j).astype(np.float32)
    c["m_kgel"] = (i >= j).astype(np.float32)
    c["m_klts"] = (i < j).astype(np.float32)
    c["onesf"] = np.ones((128, 128), np.float32)
    return c


class Cx:
    pass


class _Stop(Exception):
    pass


def _chk(cx, n):
    import os
    if int(os.environ.get('K_STOP', '0')) == n:
        raise _Stop()


def _barrier(P):
    toks = []
    for n, e in P.E.items():
        assert not e.pending_nosig
        if e.count > 0:
            toks.append((id(e.sem), e.sem, e.count))
    for d in P.dsems:
        if d.count > 0:
            toks.append((id(d.h), d.h, d.count))
    for n, e in P.E.items():
        waits = []
        for k, h, v in toks:
            if k == id(e.sem):
                continue
            if e.waited.get(k, 0) >= v:
                continue
            e.waited[k] = v
            waits.append((h, v))
        if waits:
            e.ops.append((waits, None, None))


def _flush(P, final=False):
    nc = P.nc
    fin = [(d.h, d.count) for d in P.dsems if d.count > 0] if final else []
    for n, e in P.E.items():
        assert not e.pending_nosig, n
    with nc.Block() as block:
        def mk(eng, fl):
            ops = eng.ops

            def f(h):
                for waits, fn, inc in ops:
                    for (s, v) in waits:
                        h.wait_ge(s, v)
                    if fn is None:
                        continue
                    ins = fn(h)
                    if inc is not None:
                        ins.then_inc(inc[0], inc[1])
                if fl:
                    for (s, v) in fin:
                        h.wait_ge(s, v)
            return f
        block.tensor(mk(P.E["pe"], False))
        block.scalar(mk(P.E["act"], False))
        block.vector(mk(P.E["dve"], False))
        block.gpsimd(mk(P.E["pool"], False))
        block.sync(mk(P.E["sp"], final))
    for e in P.E.values():
        e.ops = []


def _blocks(NT):
    out = []
    t = 0
    while t < NT:
        n = min(4, NT - t)
        out.append((t, n))
        t += n
    return out


def phase_A(cx):
    nc, P, NT = cx.nc, cx.P, cx.NT
    T = NT * 128
    with ExitStack() as st:
        sb = lambda n, s, d: st.enter_context(nc.sbuf_tensor(n, s, d))
        xt = sb("a_xt", [128, D], F32)
        xt_r = P.res("a_xt", dma=True)
        hn = sb("a_hn", [128, D], F32)
        hn_r = P.res("a_hn")
        hnT = sb("a_hnT", [128, NKC, 512], BF16)
        hnT_r = P.res("a_hnT", dma=True)
        gcol = sb("a_gcol", [128, NKC], F32)
        gcol_r = P.res("a_gcol", dma=True)
        NW = 3
        wt = [sb("a_w%d" % i, [128, 16, 512], BF16) for i in range(NW)]
        wt_r = [P.res("a_w", dma=True) for i in range(NW)]
        wring = Ring(list(zip(wt, wt_r)))
        wdt = sb("a_wdt", [128, NKC, 128], BF16)
        wdt_r = P.res("a_wdt", dma=True)
        sm = sb("a_sm", [128, 8], F32)
        sm_r = P.res("a_sm")
        NS = 4
        sf = [sb("a_sf%d" % i, [128, 512], F32) for i in range(NS)]
        sf_r = [P.res("a_sf", dma=True) for i in range(NS)]
        sfring = Ring(list(zip(sf, sf_r)))
        sh = [sb("a_sh%d" % i, [128, 512], BF16) for i in range(NS)]
        sh_r = [P.res("a_sh", dma=True) for i in range(NS)]
        shring = Ring(list(zip(sh, sh_r)))
        sq = [sb("a_sq%d" % i, [128, 512], BF16) for i in range(2)]
        sq_r = [P.res("a_sq") for i in range(2)]
        sqring = Ring(list(zip(sq, sq_r)))
        rq = [sb("a_rq%d" % i, [128, 512], F32) for i in range(2)]
        rq_r = [P.res("a_rq") for i in range(2)]
        rqring = Ring(list(zip(rq, rq_r)))
        dtb = sb("a_dtb", [128, 128], F32)
        dtb_r = P.res("a_dtb", dma=True)
        Ab = sb("a_Ab", [128, 128], F32)
        Ab_r = P.res("a_Ab", dma=True)
        gq = sb("a_gq", [128, 2], F32)
        gq_r = P.res("a_gq", dma=True)
        zt = sb("a_zero", [128, 48, 2], F32)
        zt_r = P.res("a_zero", dma=True)
        ps = [st.enter_context(nc.psum_tensor("a_ps%d" % i, [128, 512], F32)) for i in range(8)]
        ps_r = [P.res("a_ps", excl=True) for i in range(8)]

        _barrier(P)
        try:
          _phase_A_body(cx, locals())
        except _Stop:
          pass
        _flush(P)


def _phase_A_body(cx, L):
        globals().update({})
        nc, P, NT = cx.nc, cx.P, cx.NT
        T = NT * 128
        for _k, _v in L.items():
            pass
        xt, xt_r, hn, hn_r, hnT, hnT_r, gcol, gcol_r, wring, wdt, wdt_r, sm, sm_r, sfring, shring, sqring, rqring, dtb, dtb_r, Ab, Ab_r, gq, gq_r, zt, zt_r, ps, ps_r = [L[k] for k in 'xt xt_r hn hn_r hnT hnT_r gcol gcol_r wring wdt wdt_r sm sm_r sfring shring sqring rqring dtb dtb_r Ab Ab_r gq gq_r zt zt_r ps ps_r'.split()]
        P.dma("sp", gcol[:, :], cx.gcol_mix[:, :], gcol_r.dsem, writes=[gcol_r])
        P.dma("sp", dtb[:, :], cx.dt_bias[0:1, :].partition_broadcast(128), dtb_r.dsem, writes=[dtb_r])
        P.dma("sp", Ab[:, :], cx.a_log[0:1, :].partition_broadcast(128), Ab_r.dsem, writes=[Ab_r])
        P.op("act", lambda h: h.activation(out=Ab[:, :], in_=Ab[:, :], func=AF.Exp), reads=[Ab_r], writes=[Ab_r])
        P.op("dve", lambda h: h.tensor_scalar(out=Ab[:, :], in0=Ab[:, :], scalar1=-1.0, scalar2=None, op0=ALU.mult),
             reads=[Ab_r], writes=[Ab_r])
        P.dma("sp", gq[:, :], cx.gqk[:, :], gq_r.dsem, writes=[gq_r])
        P.op("dve", lambda h: h.tensor_scalar(out=gq[:, 0:1], in0=gq[:, 0:1], scalar1=0.125, scalar2=None, op0=ALU.mult),
             reads=[gq_r], writes=[gq_r])
        P.op("pool", lambda h: h.memset(zt[:, :, :], 0.0), writes=[zt_r])
        P.dma("sp", cx.XBC[:, :, 0:2], zt[:, :, :], zt_r.dsem, reads=[zt_r], writes=[cx.XBC_r], partial=True)
        P.dma("sp", cx.XBC[:, :, T + 2:T + 4], zt[:, :, :], zt_r.dsem, reads=[zt_r], writes=[cx.XBC_r], partial=True)

        _chk(cx, 1)
        win = cx.w_in.rearrange("(kc p) n -> p kc n", p=128)
        identf = cx.c_identf
        evi = [0]

        def evac_copy(out_ap, in_ap, reads, writes):
            evi[0] += 1
            if evi[0] % 2 == 0:
                P.op("act", lambda h: h.copy(out=out_ap, in_=in_ap), reads=reads, writes=writes)
            else:
                P.op("dve", lambda h: h.tensor_copy(out=out_ap, in_=in_ap), reads=reads, writes=writes)

        def load_w(c0, kh, ncols=512):
            w, wr = wring.next()
            P.dma("pool", w[:, :, 0:ncols], win[:, kh * 16:(kh + 1) * 16, c0:c0 + ncols], wr.dsem, writes=[wr])
            return w, wr

        grp = [0]
        for (t0, nt) in _blocks(NT):
            W = nt * 128
            tok0 = t0 * 128
            for i in range(nt):
                P.dma("sp", xt[:, :], cx.xin[tok0 + i * 128: tok0 + (i + 1) * 128, :], xt_r.dsem, writes=[xt_r])
                P.op("act", lambda h: h.activation(out=hn[:, :], in_=xt[:, :], func=AF.Square, accum_out=sm[:, 0:1]),
                     reads=[xt_r], writes=[hn_r, sm_r])
                P.op("act", lambda h: h.activation(out=sm[:, 1:2], in_=sm[:, 0:1], func=AF.Sqrt, bias=EPS, scale=1.0 / D),
                     reads=[sm_r], writes=[sm_r])
                P.op("dve", lambda h: h.reciprocal(out=sm[:, 2:3], in_=sm[:, 1:2]), reads=[sm_r], writes=[sm_r])
                P.op("dve", lambda h: h.tensor_scalar(out=hn[:, :], in0=xt[:, :], scalar1=sm[:, 2:3], scalar2=None, op0=ALU.mult),
                     reads=[xt_r, sm_r], writes=[hn_r])
                for q4 in range(8):
                    pb, pbr = ps[q4 % 2], ps_r[q4 % 2]
                    for k4 in range(4):
                        kc = q4 * 4 + k4
                        P.op("pe", lambda h, pb=pb, k4=k4, kc=kc: h.transpose(pb[:, k4 * 128:(k4 + 1) * 128], hn[:, kc * 128:(kc + 1) * 128], identf[:, :]),
                             reads=[hn_r, cx.c_r], writes=[pbr], sig=(k4 == 3), partial=(k4 > 0))
                    P.op("dve", lambda h, pb=pb, q4=q4, i=i: h.tensor_tensor(
                        out=hnT[:, q4 * 4:(q4 + 1) * 4, i * 128:(i + 1) * 128],
                        in0=pb[:, :].rearrange("p (k n) -> p k n", k=4),
                        in1=gcol[:, q4 * 4:(q4 + 1) * 4].unsqueeze(2).to_broadcast([128, 4, 128]), op=ALU.mult),
                        reads=[pbr, gcol_r], writes=[hnT_r], partial=True)
            _chk(cx, 2)
            P.dma("sp", cx.HNT[:, :, tok0:tok0 + W], hnT[:, :, 0:W], hnT_r.dsem, reads=[hnT_r], writes=[cx.HNT_r], partial=True)

            _chk(cx, 3)
            P.dma("pool", wdt[:, :, :], win[:, :, O_DT:O_DT + 128], wdt_r.dsem, writes=[wdt_r])
            for i in range(nt):
                t = t0 + i
                pb, pbr = ps[i % 2], ps_r[i % 2]
                for kc in range(NKC):
                    P.op("pe", lambda h, pb=pb, kc=kc, i=i: h.matmul(pb[:, 0:128], hnT[:, kc, i * 128:(i + 1) * 128], wdt[:, kc, :],
                                                                   start=(kc == 0), stop=(kc == NKC - 1)),
                         reads=[hnT_r, wdt_r], writes=[pbr], sig=(kc == NKC - 1), partial=(kc > 0))
                dtt = cx.DT[:, t, :]
                dr = cx.DT_r[t]
                P.op("dve", lambda h, pb=pb, dtt=dtt: h.tensor_tensor(out=dtt, in0=pb[:, 0:128], in1=dtb[:, :], op=ALU.add),
                     reads=[pbr, dtb_r], writes=[dr])
                P.op("act", lambda h, dtt=dtt: h.activation(out=dtt, in_=dtt, func=AF.Exp), reads=[dr], writes=[dr])
                P.op("act", lambda h, dtt=dtt: h.activation(out=dtt, in_=dtt, func=AF.Ln, bias=1.0, scale=1.0),
                     reads=[dr], writes=[dr])
                if t == 0:
                    P.op("pool", lambda h: h.memset(cx.DT[0:NPAD, 0, :], 0.0), reads=[dr], writes=[dr])
                P.op("pool", lambda h, dtt=dtt, t=t: h.tensor_tensor(out=cx.DA[:, t, :], in0=dtt, in1=Ab[:, :], op=ALU.mult),
                     reads=[dr, Ab_r], writes=[cx.DA_r[t]])

            _chk(cx, 4)
            def group_b(c0, kind):
                base = 4 * (grp[0] % 2)
                grp[0] += 1
                for kh in range(2):
                    w, wr = load_w(c0, kh)
                    for j in range(4):
                        pb, pbr = ps[base + j], ps_r[base + j]
                        for k in range(16):
                            kc = kh * 16 + k
                            P.op("pe", lambda h, W=W, pb=pb, w=w, k=k, j=j, kc=kc: h.matmul(pb[:, 0:W], w[:, k, j * 128:(j + 1) * 128], hnT[:, kc, 0:W],
                                                                                 start=(kc == 0), stop=(kc == NKC - 1)),
                                 reads=[hnT_r, wr], writes=[pbr], sig=(k == 15), partial=(kc > 0))
                for j in range(4):
                    pb, pbr = ps[base + j], ps_r[base + j]
                    col = c0 + j * 128
                    if kind == "xbc":
                        s_, sr = sfring.next()
                        evac_copy(s_[:, 0:W], pb[:, 0:W], [pbr], [sr])
                        cc = (col - O_XBC) // 128
                        P.dma("sp", cx.XBC[:, cc, 2 + tok0: 2 + tok0 + W], s_[:, 0:W], sr.dsem, reads=[sr], writes=[cx.XBC_r], partial=True)
                    else:
                        isq = (kind == "q")
                        hd = (col - (O_Q if isq else O_K)) // 128
                        gc = gq[:, 0:1] if isq else gq[:, 1:2]
                        s_, sr = sfring.next()
                        q2, q2r = sqring.next()
                        r_, rr = rqring.next()
                        o_, orr = shring.next()
                        P.op("dve", lambda h, W=W, s_=s_, pb=pb: h.tensor_copy(out=s_[:, 0:W], in_=pb[:, 0:W]), reads=[pbr], writes=[sr])
                        P.op("act", lambda h, W=W, q2=q2, pb=pb: h.activation(out=q2[:, 0:W], in_=pb[:, 0:W], func=AF.Square), reads=[pbr], writes=[q2r])
                        import os as _os
                        _qv = int(_os.environ.get('K_QV', '0'))
                        if _qv == 1:
                            continue
                        P.op("pe", lambda h, W=W, pb=pb, q2=q2: h.matmul(pb[:, 0:W], cx.c_blk64[:, :], q2[:, 0:W], start=True, stop=True),
                             reads=[q2r, cx.c_r], writes=[pbr])
                        if _qv == 2:
                            continue
                        P.op("act", lambda h, W=W, r_=r_, pb=pb: h.activation(out=r_[:, 0:W], in_=pb[:, 0:W], func=AF.Sqrt, bias=EPS, scale=1.0 / 64),
                             reads=[pbr], writes=[rr])
                        if _qv == 3:
                            continue
                        P.op("dve", lambda h, W=W, r_=r_: h.reciprocal(out=r_[:, 0:W], in_=r_[:, 0:W]), reads=[rr], writes=[rr])
                        P.op("dve", lambda h, W=W, o_=o_, s_=s_, r_=r_, gc=gc: h.scalar_tensor_tensor(out=o_[:, 0:W], in0=s_[:, 0:W], scalar=gc, in1=r_[:, 0:W],
                                                                                      op0=ALU.mult, op1=ALU.mult),
                             reads=[sr, rr, gq_r], writes=[orr])
                        dst = cx.QT if isq else cx.KT
                        dst_r = cx.QT_r if isq else cx.KT_r
                        P.dma("sp", dst[:, hd, tok0:tok0 + W], o_[:, 0:W], orr.dsem, reads=[orr], writes=[dst_r], partial=True)

            for g in range(12):
                group_b(O_XBC + g * 512, "xbc")
                _chk(cx, 5)
            _chk(cx, 6)
            for g in range(4):
                group_b(O_Q + g * 512, "q")
            for g in range(4):
                group_b(O_K + g * 512, "k")
            _chk(cx, 7)

            def group_a(c0, kind):
                base = 4 * (grp[0] % 2)
                grp[0] += 1
                for kh in range(2):
                    w, wr = load_w(c0, kh)
                    for i in range(nt):
                        pb, pbr = ps[base + i], ps_r[base + i]
                        for k in range(16):
                            kc = kh * 16 + k
                            P.op("pe", lambda h, pb=pb, w=w, k=k, i=i, kc=kc: h.matmul(pb[:, :], hnT[:, kc, i * 128:(i + 1) * 128], w[:, k, :],
                                                                                 start=(kc == 0), stop=(kc == NKC - 1)),
                                 reads=[hnT_r, wr], writes=[pbr], sig=(k == 15), partial=(kc > 0))
                for i in range(nt):
                    pb, pbr = ps[base + i], ps_r[base + i]
                    r0 = tok0 + i * 128
                    if kind == "v":
                        o_, orr = shring.next()
                        evac_copy(o_[:, :], pb[:, :], [pbr], [orr])
                        cv = c0 - O_V
                        P.dma("sp", cx.V[r0:r0 + 128, cv:cv + 512], o_[:, :], orr.dsem, reads=[orr], writes=[cx.V_r], partial=True)
                    else:
                        s_, sr = sfring.next()
                        P.op("act", lambda h, s_=s_, pb=pb: h.activation(out=s_[:, :], in_=pb[:, :], func=AF.Silu), reads=[pbr], writes=[sr])
                        P.dma("sp", cx.ZS[r0:r0 + 128, c0:c0 + 512], s_[:, :], sr.dsem, reads=[sr], writes=[cx.ZS_r], partial=True)

            for g in range(4):
                group_a(O_V + g * 512, "v")
            _chk(cx, 8)
            for g in range(8):
                group_a(g * 512, "z")


def _bc(ap, shape, axis):
    return ap.unsqueeze(axis).to_broadcast(shape)


def phase_B(cx, sweep):
    nc, P, NT = cx.nc, cx.P, cx.NT
    T = NT * 128
    d = sweep
    with ExitStack() as st:
        sb = lambda n, s, dt_: st.enter_context(nc.sbuf_tensor(n, s, dt_))
        pfx = "b%d_" % sweep
        R = lambda n, **kw: P.res(pfx + n, **kw)
        xin_ = sb(pfx + "xin", [128, 16, 132] if sweep == 0 else [128, 1, 2], F32); xin_r = R("xin", dma=True)
        acc = sb(pfx + "acc", [128, 16, 128] if sweep == 0 else [128, 1, 2], F32); acc_r = R("acc")
        tmp = sb(pfx + "tmp", [128, 16, 128] if sweep == 0 else [128, 1, 2], F32); tmp_r = R("tmp")
        xbcT = sb(pfx + "xbcT", [128, 48, 128], BF16); xbcT_r = R("xbcT", dma=True)
        xtok = sb(pfx + "xtok", [128, D], BF16); xtok_r = R("xtok", dma=True)
        btok = sb(pfx + "btok", [128, 1024], BF16); btok_r = R("btok", dma=True)
        xw = sb(pfx + "xw", [128, D], BF16); xw_r = R("xw")
        S = sb(pfx + "S", [128, D], F32); S_r = R("S")
        Sb = sb(pfx + "Sb", [128, D], BF16); Sb_r = R("Sb")
        aU = sb(pfx + "aU", [128, 8, 128], F32); aU_r = R("aU")
        E = sb(pfx + "E", [128, 8, 128], F32); E_r = R("E")
        MT = sb(pfx + "MT", [128, 8, 128], BF16); MT_r = R("MT")
        CBm = sb(pfx + "CBm", [128, 128], F32); CBm_r = R("CBm")
        yacc = sb(pfx + "yacc", [128, D], F32); yacc_r = R("yacc", dma=True)
        yos = sb(pfx + "yos", [128, 512], F32); yos_r = R("yos")
        ecs = sb(pfx + "ecs", [128, 64], F32); ecs_r = R("ecs")
        wst = sb(pfx + "wst", [128, 64], F32); wst_r = R("wst")
        dec = sb(pfx + "dec", [128, 64], F32); dec_r = R("dec")
        cw = sb(pfx + "cw", [128, 48, 5], F32); cw_r = R("cw", dma=True)
        cb = sb(pfx + "cb", [128, 48], F32); cb_r = R("cb", dma=True)
        dsk = sb(pfx + "dsk", [128, 64], F32); dsk_r = R("dsk", dma=True)
        zs = sb(pfx + "zs", [128, D] if sweep == 1 else [128, 2], F32); zs_r = R("zs", dma=True)
        gs = sb(pfx + "gs", [128, D] if sweep == 1 else [128, 2], F32); gs_r = R("gs", dma=True)
        ynb = sb(pfx + "ynb", [128, D] if sweep == 1 else [128, 2], BF16); ynb_r = R("ynb")
        ynT = sb(pfx + "ynT", [128, NKC, 128] if sweep == 1 else [128, 1, 2], BF16); ynT_r = R("ynT", dma=True)
        sm = sb(pfx + "sm", [128, 24], F32); sm_r = R("sm")
        psA = [st.enter_context(nc.psum_tensor(pfx + "psA%d" % i, [128, 512], F32)) for i in range(5)]
        psA_r = [R("psA", excl=True) for i in range(5)]
        psT = [st.enter_context(nc.psum_tensor(pfx + "psT%d" % i, [128, 1024], BF16)) for i in range(2)]
        psT_r = [R("psT", excl=True) for i in range(2)]
        psS = st.enter_context(nc.psum_tensor(pfx + "psS", [128, 512], F32)); psS_r = R("psS", excl=True)
        ident = cx.c_ident
        _barrier(P)
        if sweep == 0:
            P.dma("sp", cw[:, :, :], cx.convw[:, :, :], cw_r.dsem, writes=[cw_r])
            P.dma("sp", cb[:, :], cx.convb[:, :], cb_r.dsem, writes=[cb_r])
            P.dma("sp", dsk[:, :], cx.d_skip[0:1, :].partition_broadcast(128), dsk_r.dsem, writes=[dsk_r])
        else:
            P.dma("sp", gs[:, :], cx.ssd_norm_g[0:1, :].partition_broadcast(128), gs_r.dsem, writes=[gs_r])
        P.op("pool", lambda h: h.memset(S[:, :], 0.0), writes=[S_r])
        P.op("pool", lambda h: h.memset(Sb[:, :], 0.0), writes=[Sb_r])
        m_cum = cx.c_m_kles if d == 0 else cx.c_m_kgel
        m_seg = cx.c_m_kgts if d == 0 else cx.c_m_klts
        order = list(range(NT)) if d == 0 else list(range(NT - 1, -1, -1))
        for t in order:
            tok0 = t * 128
            dtr, dar = cx.DT_r[t], cx.DA_r[t]
            DTt = cx.DT[:, t, d * 64:(d + 1) * 64]
            DAt = cx.DA[:, t, d * 64:(d + 1) * 64]
            if sweep == 0:
                for c3 in range(3):
                    P.dma("sp", xin_[:, :, :], cx.XBC[:, c3 * 16:(c3 + 1) * 16, tok0:tok0 + 132], xin_r.dsem, reads=[cx.XBC_r], writes=[xin_r])
                    for k in range(5):
                        wk = _bc(cw[:, c3 * 16:(c3 + 1) * 16, k], [128, 16, 128], 2)
                        if k == 0:
                            P.op("dve", lambda h, wk=wk: h.tensor_tensor(out=acc[:, :, :], in0=xin_[:, :, 0:128], in1=wk, op=ALU.mult),
                                 reads=[xin_r, cw_r], writes=[acc_r])
                        else:
                            P.op("pool", lambda h, wk=wk, k=k: h.tensor_tensor(out=tmp[:, :, :], in0=xin_[:, :, k:k + 128], in1=wk, op=ALU.mult),
                                 reads=[xin_r, cw_r], writes=[tmp_r])
                            P.op("dve", lambda h: h.tensor_tensor(out=acc[:, :, :], in0=acc[:, :, :], in1=tmp[:, :, :], op=ALU.add),
                                 reads=[acc_r, tmp_r], writes=[acc_r])
                    P.op("dve", lambda h, c3=c3: h.tensor_tensor(out=acc[:, :, :], in0=acc[:, :, :], in1=_bc(cb[:, c3 * 16:(c3 + 1) * 16], [128, 16, 128], 2), op=ALU.add),
                         reads=[acc_r, cb_r], writes=[acc_r])
                    P.op("act", lambda h, c3=c3: h.activation(out=xbcT[:, c3 * 16:(c3 + 1) * 16, :], in_=acc[:, :, :], func=AF.Silu),
                         reads=[acc_r], writes=[xbcT_r], partial=True)
                if t == 0:
                    P.op("pool", lambda h: h.memset(xbcT[:, :, 0:NPAD], 0.0), reads=[xbcT_r], writes=[xbcT_r])
                for q in range(5):
                    pb, pbr = psT[q % 2], psT_r[q % 2]
                    for k8 in range(8):
                        cc = q * 8 + k8
                        P.op("pe", lambda h, pb=pb, k8=k8, cc=cc: h.transpose(pb[:, k8 * 128:(k8 + 1) * 128], xbcT[:, cc, :], ident[:, :]),
                             reads=[xbcT_r, cx.c_r], writes=[pbr], sig=(k8 == 7), partial=(k8 > 0))
                    if q < 4:
                        P.op("act" if q % 2 else "dve", (lambda h, pb=pb, q=q: h.copy(out=xtok[:, q * 1024:(q + 1) * 1024], in_=pb[:, :])) if q % 2 else
                             (lambda h, pb=pb, q=q: h.tensor_copy(out=xtok[:, q * 1024:(q + 1) * 1024], in_=pb[:, :])),
                             reads=[pbr], writes=[xtok_r], partial=(q > 0))
                    else:
                        P.op("dve", lambda h, pb=pb: h.tensor_copy(out=btok[:, :], in_=pb[:, :]), reads=[pbr], writes=[btok_r])
                P.dma("sp", cx.XTOK[tok0:tok0 + 128, :], xtok[:, :], xtok_r.dsem, reads=[xtok_r], writes=[cx.XTOK_r], partial=True)
                P.dma("sp", cx.BTOK[tok0:tok0 + 128, :], btok[:, :], btok_r.dsem, reads=[btok_r], writes=[cx.BTOK_r], partial=True)
                P.dma("sp", cx.BCT[:, :, tok0:tok0 + 128], xbcT[:, 32:48, :], xbcT_r.dsem, reads=[xbcT_r], writes=[cx.BCT_r], partial=True)
            else:
                P.dma("sp", xtok[:, :], cx.XTOK[tok0:tok0 + 128, :], xtok_r.dsem, reads=[cx.XTOK_r], writes=[xtok_r])
                P.dma("sp", btok[:, :], cx.BTOK[tok0:tok0 + 128, :], btok_r.dsem, reads=[cx.BTOK_r], writes=[btok_r])
                P.dma("sp", xbcT[:, 32:48, :], cx.BCT[:, :, tok0:tok0 + 128], xbcT_r.dsem, reads=[cx.BCT_r], writes=[xbcT_r])
                P.dma("sp", yacc[:, :], cx.Y1[tok0:tok0 + 128, :], yacc_r.dsem, reads=[cx.Y1_r], writes=[yacc_r])
                P.dma("sp", zs[:, :], cx.ZS[tok0:tok0 + 128, :], zs_r.dsem, reads=[cx.ZS_r], writes=[zs_r])

            pb, pbr = psS, psS_r
            P.op("pe", lambda h, DAt=DAt: h.matmul(psS[:, 0:64], m_cum[:, :], DAt, start=True, stop=True), reads=[dar, cx.c_r], writes=[pbr])
            P.op("act", lambda h: h.activation(out=ecs[:, :], in_=psS[:, 0:64], func=AF.Exp), reads=[pbr], writes=[ecs_r])
            P.op("pe", lambda h, DAt=DAt: h.matmul(psS[:, 64:128], m_seg[:, :], DAt, start=True, stop=True), reads=[dar, cx.c_r], writes=[pbr])
            P.op("act", lambda h: h.activation(out=wst[:, :], in_=psS[:, 64:128], func=AF.Exp), reads=[pbr], writes=[wst_r])
            P.op("dve", lambda h, DTt=DTt: h.tensor_tensor(out=wst[:, :], in0=wst[:, :], in1=DTt, op=ALU.mult), reads=[wst_r, dtr], writes=[wst_r])
            P.op("pe", lambda h, DAt=DAt: h.matmul(psS[:, 128:192], cx.c_onesf[:, :], DAt, start=True, stop=True), reads=[dar, cx.c_r], writes=[pbr])
            P.op("act", lambda h: h.activation(out=dec[:, :], in_=psS[:, 128:192], func=AF.Exp), reads=[pbr], writes=[dec_r])
            P.op("pool", lambda h: h.tensor_tensor(out=xw[:, :].rearrange("p (h e) -> p h e", h=64), in0=xtok[:, :].rearrange("p (h e) -> p h e", h=64),
                                                  in1=_bc(wst[:, :], [128, 64, 64], 2), op=ALU.mult),
                 reads=[xtok_r, wst_r], writes=[xw_r])
            if sweep == 0:
                P.op("pool", lambda h: h.tensor_tensor(out=yacc[:, :].rearrange("p (h e) -> p h e", h=64), in0=xtok[:, :].rearrange("p (h e) -> p h e", h=64),
                                                      in1=_bc(dsk[:, :], [128, 64, 64], 2), op=ALU.mult),
                     reads=[xtok_r, dsk_r], writes=[yacc_r])
            for g in range(8):
                BTg = xbcT[:, 32 + g, :]
                CTg = xbcT[:, 40 + g, :]
                pCB, pCB_r = psA[0], psA_r[0]
                pSeg = [psA[1], psA[2]]; pSeg_r = [psA_r[1], psA_r[2]]
                pY, pY_r = psA[3], psA_r[3]
                pO, pO_r = psA[4], psA_r[4]
                hs = slice(g * 8, (g + 1) * 8)
                P.op("pe", lambda h, BTg=BTg, CTg=CTg, pCB=pCB: h.matmul(pCB[:, 0:128], BTg, CTg, start=True, stop=True), reads=[xbcT_r], writes=[pCB_r])
                P.op("dve", lambda h, pCB=pCB: h.tensor_tensor(out=CBm[:, :], in0=pCB[:, 0:128], in1=m_cum[:, :], op=ALU.mult), reads=[pCB_r, cx.c_r], writes=[CBm_r])
                P.op("pool", lambda h, DAt=DAt, hs=hs: h.tensor_tensor(out=aU[:, :, :], in0=_bc(DAt[:, hs], [128, 8, 128], 2), in1=_bc(m_cum[:, :], [128, 8, 128], 1), op=ALU.mult),
                     reads=[dar, cx.c_r], writes=[aU_r])
                for hf in range(2):
                    P.op("pe", lambda h, hf=hf, pSeg=pSeg: h.matmul(pSeg[hf][:, :], m_seg[:, :], aU[:, hf * 4:(hf + 1) * 4, :].rearrange("p a b -> p (a b)"), start=True, stop=True),
                         reads=[aU_r, cx.c_r], writes=[pSeg_r[hf]])
                    P.op("act", lambda h, hf=hf, pSeg=pSeg: h.activation(out=E[:, hf * 4:(hf + 1) * 4, :].rearrange("p a b -> p (a b)"), in_=pSeg[hf][:, :], func=AF.Exp),
                         reads=[pSeg_r[hf]], writes=[E_r], partial=(hf > 0))
                P.op("dve", lambda h, DTt=DTt, hs=hs: h.tensor_tensor(out=E[:, :, :], in0=E[:, :, :], in1=_bc(DTt[:, hs], [128, 8, 128], 2), op=ALU.mult),
                     reads=[E_r, dtr], writes=[E_r])
                P.op("dve", lambda h: h.tensor_tensor(out=MT[:, :, :], in0=E[:, :, :], in1=_bc(CBm[:, :], [128, 8, 128], 1), op=ALU.mult),
                     reads=[E_r, CBm_r], writes=[MT_r])
                for j in range(8):
                    hh = g * 8 + j
                    P.op("pe", lambda h, j=j, hh=hh, pY=pY: h.matmul(pY[:, j * 64:(j + 1) * 64], MT[:, j, :], xtok[:, hh * 64:(hh + 1) * 64], start=True, stop=True),
                         reads=[MT_r, xtok_r], writes=[pY_r], sig=(j == 7), partial=(j > 0))
                P.op("pe", lambda h, CTg=CTg, g=g, pO=pO: h.matmul(pO[:, :], CTg, Sb[:, g * 512:(g + 1) * 512], start=True, stop=True),
                     reads=[xbcT_r, Sb_r], writes=[pO_r])
                P.op("dve", lambda h, pO=pO, hs=hs: h.tensor_tensor(out=yos[:, :].rearrange("p (h e) -> p h e", h=8), in0=pO[:, :].rearrange("p (h e) -> p h e", h=8),
                                                                in1=_bc(ecs[:, hs], [128, 8, 64], 2), op=ALU.mult),
                     reads=[pO_r, ecs_r], writes=[yos_r])
                P.op("dve", lambda h, pY=pY: h.tensor_tensor(out=yos[:, :], in0=pY[:, :], in1=yos[:, :], op=ALU.add), reads=[pY_r, yos_r], writes=[yos_r])
                P.op("pool", lambda h, g=g: h.tensor_tensor(out=yacc[:, g * 512:(g + 1) * 512], in0=yacc[:, g * 512:(g + 1) * 512], in1=yos[:, :], op=ALU.add),
                     reads=[yos_r, yacc_r], writes=[yacc_r])
                P.op("pe", lambda h, g=g, pO=pO: h.matmul(pO[:, :], btok[:, g * 128:(g + 1) * 128], xw[:, g * 512:(g + 1) * 512], start=True, stop=True),
                     reads=[btok_r, xw_r], writes=[pO_r])
                P.op("pool", lambda h, g=g, hs=hs: h.tensor_tensor(out=S[:, g * 512:(g + 1) * 512].rearrange("p (h e) -> p h e", h=8),
                                                                 in0=S[:, g * 512:(g + 1) * 512].rearrange("p (h e) -> p h e", h=8),
                                                                 in1=_bc(dec[:, hs], [128, 8, 64], 2), op=ALU.mult),
                     reads=[S_r, dec_r, Sb_r], writes=[S_r])
                P.op("dve", lambda h, g=g, pO=pO: h.tensor_tensor(out=S[:, g * 512:(g + 1) * 512], in0=pO[:, :], in1=S[:, g * 512:(g + 1) * 512], op=ALU.add),
                     reads=[pO_r, S_r], writes=[S_r])
                P.op("act", lambda h, g=g: h.copy(out=Sb[:, g * 512:(g + 1) * 512], in_=S[:, g * 512:(g + 1) * 512]), reads=[S_r], writes=[Sb_r])
            if sweep == 0:
                P.dma("sp", cx.Y1[tok0:tok0 + 128, :], yacc[:, :], yacc_r.dsem, reads=[yacc_r], writes=[cx.Y1_r], partial=True)
            else:
                P.op("dve", lambda h: h.tensor_tensor(out=yacc[:, :], in0=yacc[:, :], in1=zs[:, :], op=ALU.mult), reads=[yacc_r, zs_r], writes=[yacc_r])
                for g in range(8):
                    P.op("act", lambda h, g=g: h.activation(out=ynb[:, g * 512:(g + 1) * 512], in_=yacc[:, g * 512:(g + 1) * 512], func=AF.Square, accum_out=sm[:, g:g + 1]),
                         reads=[yacc_r], writes=[ynb_r, sm_r])
                P.op("act", lambda h: h.activation(out=sm[:, 8:16], in_=sm[:, 0:8], func=AF.Sqrt, bias=EPS, scale=1.0 / 512), reads=[sm_r], writes=[sm_r])
                P.op("dve", lambda h: h.reciprocal(out=sm[:, 16:24], in_=sm[:, 8:16]), reads=[sm_r], writes=[sm_r])
                P.op("dve", lambda h: h.tensor_tensor(out=yacc[:, :].rearrange("p (g e) -> p g e", g=8), in0=yacc[:, :].rearrange("p (g e) -> p g e", g=8),
                                                     in1=_bc(sm[:, 16:24], [128, 8, 512], 2), op=ALU.mult), reads=[yacc_r, sm_r], writes=[yacc_r])
                P.op("dve", lambda h: h.tensor_tensor(out=ynb[:, :], in0=yacc[:, :], in1=gs[:, :], op=ALU.mult), reads=[yacc_r, gs_r], writes=[ynb_r])
                for q in range(4):
                    pb, pbr = psT[q % 2], psT_r[q % 2]
                    for k8 in range(8):
                        cc = q * 8 + k8
                        P.op("pe", lambda h, pb=pb, k8=k8, cc=cc: h.transpose(pb[:, k8 * 128:(k8 + 1) * 128], ynb[:, cc * 128:(cc + 1) * 128], ident[:, :]),
                             reads=[ynb_r, cx.c_r], writes=[pbr], sig=(k8 == 7), partial=(k8 > 0))
                    P.op("act" if q % 2 else "dve", (lambda h, pb=pb, q=q: h.copy(out=ynT[:, q * 8:(q + 1) * 8, :], in_=pb[:, :].rearrange("p (k n) -> p k n", k=8))) if q % 2 else
                         (lambda h, pb=pb, q=q: h.tensor_copy(out=ynT[:, q * 8:(q + 1) * 8, :], in_=pb[:, :].rearrange("p (k n) -> p k n", k=8))),
                         reads=[pbr], writes=[ynT_r], partial=(q > 0))
                P.dma("sp", cx.YNT[:, :, tok0:tok0 + 128], ynT[:, :, :], ynT_r.dsem, reads=[ynT_r], writes=[cx.YNT_r], partial=True)
        _flush(P)


def _blocks_n(NT, n):
    out = []
    t = 0
    while t < NT:
        k = min(n, NT - t)
        out.append((t, k))
        t += k
    return out


SLOPES = [2.0 ** (-8.0 * (h + 1) / 16.0) for h in range(16)]


def phase_C(cx):
    nc, P, NT = cx.nc, cx.P, cx.NT
    T = NT * 128
    with ExitStack() as st:
        sb = lambda n, s, dt_: st.enter_context(nc.sbuf_tensor(n, s, dt_))
        R = lambda n, **kw: P.res("c_" + n, **kw)
        KTh = sb("cc_KT", [128, T], BF16); KTh_r = R("KT", dma=True)
        QTh = sb("cc_QT", [128, T], BF16); QTh_r = R("QT", dma=True)
        Vh = sb("cc_V", [128, NT, 128], BF16); Vh_r = R("V", dma=True)
        absd = sb("cc_absd", [128, 2 * T + 512], F32); absd_r = R("absd", dma=True)
        padb = sb("cc_padb", [128, 1], F32); padb_r = R("padb", dma=True)
        lam4 = sb("c_lam4", [128, 4, 64], F32); lam4_r = R("lam4", dma=True)
        lsm = sb("c_lsm", [128, 8], F32); lsm_r = R("lsm")
        gsub = sb("c_gsub", [128, 1], F32); gsub_r = R("gsub", dma=True)
        tmpS = [sb("c_tmp%d" % i, [128, 512], F32) for i in range(2)]; tmpS_r = [R("tmp") for i in range(2)]
        tring = Ring(list(zip(tmpS, tmpS_r)))
        PT = [sb("c_PT%d" % i, [128, 512], BF16) for i in range(3)]; PT_r = [R("PT") for i in range(3)]
        pring = Ring(list(zip(PT, PT_r)))
        r0 = sb("c_r0", [128, 512], F32); r0_r = R("r0")
        r1 = sb("c_r1", [128, 512], F32); r1_r = R("r1")
        o0 = sb("c_o0", [128, 512], F32); o0_r = R("o0")
        o1 = sb("c_o1", [128, 512], F32); o1_r = R("o1")
        sqb = sb("c_sqb", [128, 512], BF16); sqb_r = R("sqb")
        ya = [sb("c_ya%d" % i, [128, 512], BF16) for i in range(2)]; ya_r = [R("ya", dma=True) for i in range(2)]
        yring = Ring(list(zip(ya, ya_r)))
        psS = [st.enter_context(nc.psum_tensor("c_psS%d" % i, [128, 512], F32)) for i in range(2)]
        psS_r = [R("psS", excl=True) for i in range(2)]
        sring = Ring(list(zip(psS, psS_r)))
        psO = [st.enter_context(nc.psum_tensor("c_psO%d" % i, [128, 512], F32)) for i in range(2)]
        psO_r = [R("psO", excl=True) for i in range(2)]
        psR = [st.enter_context(nc.psum_tensor("c_psR%d" % i, [128, 512], F32)) for i in range(2)]
        psR_r = [R("psR", excl=True) for i in range(2)]
        ones = cx.c_ones
        _barrier(P)
        P.dma("sp", absd[:, :], cx.cdr["absd"][:, :], absd_r.dsem, writes=[absd_r])
        P.dma("sp", padb[:, :], cx.cdr["padbias"][:, :], padb_r.dsem, writes=[padb_r])
        P.dma("sp", gsub[:, :], cx.gsub[:, :], gsub_r.dsem, writes=[gsub_r])
        P.op("dve", lambda h: h.tensor_scalar(out=gsub[:, :], in0=gsub[:, :], scalar1=0.8, scalar2=None, op0=ALU.mult), reads=[gsub_r], writes=[gsub_r])
        P.dma("sp", lam4[:, :, :].rearrange("p a b -> p (a b)"), cx.lam4[0:1, :].partition_broadcast(128), lam4_r.dsem, writes=[lam4_r])
        P.op("dve", lambda h: h.tensor_tensor(out=lam4[:, 0, :], in0=lam4[:, 0, :], in1=lam4[:, 1, :], op=ALU.mult), reads=[lam4_r], writes=[lam4_r])
        P.op("dve", lambda h: h.tensor_tensor(out=lam4[:, 2, :], in0=lam4[:, 2, :], in1=lam4[:, 3, :], op=ALU.mult), reads=[lam4_r], writes=[lam4_r])
        P.op("dve", lambda h: h.tensor_reduce(out=lsm[:, 0:1], in_=lam4[:, 0, :], axis=AX.X, op=ALU.add), reads=[lam4_r], writes=[lsm_r])
        P.op("dve", lambda h: h.tensor_reduce(out=lsm[:, 1:2], in_=lam4[:, 2, :], axis=AX.X, op=ALU.add), reads=[lam4_r, lsm_r], writes=[lsm_r])
        P.op("act", lambda h: h.activation(out=lsm[:, 2:4], in_=lsm[:, 0:2], func=AF.Exp), reads=[lsm_r], writes=[lsm_r])
        P.op("dve", lambda h: h.tensor_tensor(out=lsm[:, 4:5], in0=lsm[:, 3:4], in1=lsm[:, 2:3], op=ALU.subtract), reads=[lsm_r], writes=[lsm_r])
        P.op("dve", lambda h: h.tensor_scalar(out=lsm[:, 4:5], in0=lsm[:, 4:5], scalar1=-0.2, scalar2=None, op0=ALU.add), reads=[lsm_r], writes=[lsm_r])
        nlam = lsm[:, 4:5]
        for hd in range(16):
            m_h = SLOPES[hd]
            P.dma("sp", KTh[:, :], cx.KT[:, hd, :], KTh_r.dsem, reads=[cx.KT_r], writes=[KTh_r])
            P.dma("sp", QTh[:, :], cx.QT[:, hd, :], QTh_r.dsem, reads=[cx.QT_r], writes=[QTh_r])
            P.dma("sp", Vh[:, :, :], cx.V[:, hd * 128:(hd + 1) * 128].rearrange("(t p) c -> p t c", p=128), Vh_r.dsem, reads=[cx.V_r], writes=[Vh_r])
            for (t0, nt) in _blocks_n(NT, 4):
                W = nt * 128
                q0 = t0 * 128
                for kt in range(NT):
                    u0 = T + 128 * (t0 - kt)
                    for i in range(2):
                        pS, pS_r = sring.next()
                        tm, tm_r = tring.next()
                        pt, pt_r = pring.next()
                        P.op("pe", lambda h, W=W, pS=pS, i=i, kt=kt, q0=q0: h.matmul(pS[:, 0:W], KTh[i * 64:(i + 1) * 64, kt * 128:(kt + 1) * 128],
                                                                               QTh[i * 64:(i + 1) * 64, q0:q0 + W], start=True, stop=True),
                             reads=[KTh_r, QTh_r], writes=[pS_r])
                        P.op("dve", lambda h, W=W, pS=pS, tm=tm, u0=u0, m_h=m_h: h.scalar_tensor_tensor(out=tm[:, 0:W], in0=absd[:, u0:u0 + W], scalar=-m_h, in1=pS[:, 0:W],
                                                                                                  op0=ALU.mult, op1=ALU.add),
                             reads=[pS_r, absd_r], writes=[tm_r])
                        if kt == 0:
                            P.op("act", lambda h, W=W, pt=pt, tm=tm: h.activation(out=pt[:, 0:W], in_=tm[:, 0:W], func=AF.Exp, bias=padb[:, 0:1], scale=1.0),
                                 reads=[tm_r, padb_r], writes=[pt_r])
                        else:
                            P.op("act", lambda h, W=W, pt=pt, tm=tm: h.activation(out=pt[:, 0:W], in_=tm[:, 0:W], func=AF.Exp), reads=[tm_r], writes=[pt_r])
                        P.op("pe", lambda h, W=W, pt=pt, i=i, kt=kt: h.matmul(psO[i][:, 0:W], Vh[:, kt, :], pt[:, 0:W], start=(kt == 0), stop=(kt == NT - 1)),
                             reads=[Vh_r, pt_r], writes=[psO_r[i]], sig=False, partial=(kt > 0))
                        P.op("pe", lambda h, W=W, pt=pt, i=i, kt=kt: h.matmul(psR[i][:, 0:W], ones[:, :], pt[:, 0:W], start=(kt == 0), stop=(kt == NT - 1)),
                             reads=[cx.c_r, pt_r], writes=[psR_r[i]], partial=(kt > 0))
                P.op("dve", lambda h, W=W: h.reciprocal(out=r0[:, 0:W], in_=psR[0][:, 0:W]), reads=[psR_r[0]], writes=[r0_r])
                P.op("dve", lambda h, W=W: h.reciprocal(out=r1[:, 0:W], in_=psR[1][:, 0:W]), reads=[psR_r[1]], writes=[r1_r])
                P.op("dve", lambda h, W=W: h.tensor_tensor(out=o0[:, 0:W], in0=psO[0][:, 0:W], in1=r0[:, 0:W], op=ALU.mult), reads=[psO_r[0], r0_r], writes=[o0_r])
                P.op("dve", lambda h, W=W: h.tensor_tensor(out=o1[:, 0:W], in0=psO[1][:, 0:W], in1=r1[:, 0:W], op=ALU.mult), reads=[psO_r[1], r1_r], writes=[o1_r])
                P.op("dve", lambda h, W=W: h.scalar_tensor_tensor(out=o0[:, 0:W], in0=o1[:, 0:W], scalar=nlam, in1=o0[:, 0:W], op0=ALU.mult, op1=ALU.add),
                     reads=[o0_r, o1_r, lsm_r], writes=[o0_r])
                P.op("act", lambda h, W=W: h.activation(out=sqb[:, 0:W], in_=o0[:, 0:W], func=AF.Square), reads=[o0_r], writes=[sqb_r])
                P.op("pe", lambda h, W=W: h.matmul(psR[0][:, 0:W], ones[:, :], sqb[:, 0:W], start=True, stop=True), reads=[cx.c_r, sqb_r], writes=[psR_r[0]])
                P.op("act", lambda h, W=W: h.activation(out=r0[:, 0:W], in_=psR[0][:, 0:W], func=AF.Sqrt, bias=EPS, scale=1.0 / 128), reads=[psR_r[0]], writes=[r0_r])
                P.op("dve", lambda h, W=W: h.reciprocal(out=r0[:, 0:W], in_=r0[:, 0:W]), reads=[r0_r], writes=[r0_r])
                y_, y_r = yring.next()
                P.op("dve", lambda h, W=W, y_=y_: h.scalar_tensor_tensor(out=y_[:, 0:W], in0=o0[:, 0:W], scalar=gsub[:, 0:1], in1=r0[:, 0:W], op0=ALU.mult, op1=ALU.mult),
                     reads=[o0_r, r0_r, gsub_r], writes=[y_r])
                P.dma("sp", cx.YAT[:, hd, q0:q0 + W], y_[:, 0:W], y_r.dsem, reads=[y_r], writes=[cx.YAT_r], partial=True)
        _flush(P)


def _norm_transpose(cx, P, src_rows_ap, src_res, xt, xt_r, sm, sm_r, gcol, gcol_r, ps, ps_r, hT, hT_r, i, identf):
    P.dma("sp", xt[:, :], src_rows_ap, xt_r.dsem, reads=[src_res] if src_res is not None else [], writes=[xt_r])
    P.op("act", lambda h: h.activation(out=cx.junk[:, :], in_=xt[:, :], func=AF.Square, accum_out=sm[:, 0:1]), reads=[xt_r], writes=[cx.junk_r, sm_r])
    P.op("act", lambda h: h.activation(out=sm[:, 1:2], in_=sm[:, 0:1], func=AF.Sqrt, bias=EPS, scale=1.0 / D), reads=[sm_r], writes=[sm_r])
    P.op("dve", lambda h: h.reciprocal(out=sm[:, 2:3], in_=sm[:, 1:2]), reads=[sm_r], writes=[sm_r])
    P.op("dve", lambda h: h.tensor_scalar(out=xt[:, :], in0=xt[:, :], scalar1=sm[:, 2:3], scalar2=None, op0=ALU.mult), reads=[xt_r, sm_r], writes=[xt_r])
    for q4 in range(8):
        pb, pbr = ps[q4 % 2], ps_r[q4 % 2]
        for k4 in range(4):
            kc = q4 * 4 + k4
            P.op("pe", lambda h, pb=pb, k4=k4, kc=kc: h.transpose(pb[:, k4 * 128:(k4 + 1) * 128], xt[:, kc * 128:(kc + 1) * 128], identf[:, :]),
                 reads=[xt_r, cx.c_r], writes=[pbr], sig=(k4 == 3), partial=(k4 > 0))
        P.op("dve", lambda h, pb=pb, q4=q4, i=i: h.tensor_tensor(
            out=hT[:, q4 * 4:(q4 + 1) * 4, i * 128:(i + 1) * 128],
            in0=pb[:, :].rearrange("p (k n) -> p k n", k=4),
            in1=gcol[:, q4 * 4:(q4 + 1) * 4].unsqueeze(2).to_broadcast([128, 4, 128]), op=ALU.mult),
            reads=[pbr, gcol_r], writes=[hT_r], partial=True)


def phase_D1(cx):
    nc, P, NT = cx.nc, cx.P, cx.NT
    with ExitStack() as st:
        sb = lambda n, s, dt_: st.enter_context(nc.sbuf_tensor(n, s, dt_))
        R = lambda n, **kw: P.res("d_" + n, **kw)
        WB = 384
        hnT = sb("d_hnT", [128, NKC, WB], BF16); hnT_r = R("hnT", dma=True)
        ynT = sb("d_ynT", [128, NKC, WB], BF16); ynT_r = R("ynT", dma=True)
        yaT = sb("d_yaT", [128, 16, WB], BF16); yaT_r = R("yaT", dma=True)
        mxT = sb("d_mxT", [128, NKC, WB], BF16); mxT_r = R("mxT")
        G1 = sb("d_G1", [128, 4, WB], F32); G1_r = R("G1")
        G2 = sb("d_G2", [128, 4, WB], F32); G2_r = R("G2")
        M = sb("d_M", [128, 4, WB], F32); M_r = R("M")
        bg = sb("d_bg", [128, 64], F32); bg_r = R("bg", dma=True)
        wt = [sb("d_w%d" % i, [128, 16, 512], BF16) for i in range(3)]; wt_r = [R("w", dma=True) for i in range(3)]
        wring = Ring(list(zip(wt, wt_r)))
        xs_ = [sb("d_xs%d" % i, [128, 512], F32) for i in range(3)]; xs_r = [R("xs", dma=True) for i in range(3)]
        xring = Ring(list(zip(xs_, xs_r)))
        ps = [st.enter_context(nc.psum_tensor("d_ps%d" % i, [128, 512], F32)) for i in range(8)]
        ps_r = [R("ps", excl=True) for i in range(8)]
        _barrier(P)
        P.dma("sp", bg[:, :], cx.bgcol[:, :], bg_r.dsem, writes=[bg_r])
        wbg = cx.w_bg.rearrange("(kc p) n -> p kc n", p=128)
        wso = cx.w_so.rearrange("(kc p) n -> p kc n", p=128)
        wao = cx.w_ao.rearrange("(kc p) n -> p kc n", p=128)
        wo = cx.w_o.rearrange("(kc p) n -> p kc n", p=128)
        grp = [0]

        def load_w(src, kh, c0, nk=16):
            w, wr = wring.next()
            P.dma("pool", w[:, 0:nk, :], src[:, kh * 16:kh * 16 + nk, c0:c0 + 512], wr.dsem, writes=[wr])
            return w, wr

        for (t0, nt) in _blocks_n(NT, 3):
            W = nt * 128
            tok0 = t0 * 128
            P.dma("sp", hnT[:, :, 0:W], cx.HNT[:, :, tok0:tok0 + W], hnT_r.dsem, reads=[cx.HNT_r], writes=[hnT_r])
            P.dma("sp", ynT[:, :, 0:W], cx.YNT[:, :, tok0:tok0 + W], ynT_r.dsem, reads=[cx.YNT_r], writes=[ynT_r])
            P.dma("sp", yaT[:, :, 0:W], cx.YAT[:, :, tok0:tok0 + W], yaT_r.dsem, reads=[cx.YAT_r], writes=[yaT_r])

            def proj_b(src, c0, act_, act_r, nkh, kind, cg):
                base = 4 * (grp[0] % 2)
                grp[0] += 1
                nkc = nkh * 16
                for kh in range(nkh):
                    w, wr = load_w(src, kh, c0)
                    for j in range(4):
                        pb, pbr = ps[base + j], ps_r[base + j]
                        for k in range(16):
                            kc = kh * 16 + k
                            P.op("pe", lambda h, W=W, pb=pb, w=w, k=k, j=j, kc=kc, act_=act_, nkc=nkc: h.matmul(pb[:, 0:W], w[:, k, j * 128:(j + 1) * 128], act_[:, kc, 0:W],
                                                                                                      start=(kc == 0), stop=(kc == nkc - 1)),
                                 reads=[act_r, wr], writes=[pbr], sig=(k == 15), partial=(kc > 0))
                for j in range(4):
                    pb, pbr = ps[base + j], ps_r[base + j]
                    ch = cg * 4 + j
                    if kind == "g1":
                        P.op("act", lambda h, W=W, pb=pb, j=j, ch=ch: h.activation(out=G1[:, j, 0:W], in_=pb[:, 0:W], func=AF.Sigmoid, bias=bg[:, ch:ch + 1], scale=1.0),
                             reads=[pbr, bg_r], writes=[G1_r], partial=(j > 0))
                    elif kind == "g2":
                        P.op("act", lambda h, W=W, pb=pb, j=j, ch=ch: h.activation(out=G2[:, j, 0:W], in_=pb[:, 0:W], func=AF.Sigmoid, bias=bg[:, 32 + ch:33 + ch], scale=1.0),
                             reads=[pbr, bg_r], writes=[G2_r], partial=(j > 0))
                    elif kind == "s":
                        P.op("dve", lambda h, W=W, pb=pb, j=j: h.tensor_tensor(out=M[:, j, 0:W], in0=pb[:, 0:W], in1=G1[:, j, 0:W], op=ALU.mult),
                             reads=[pbr, G1_r], writes=[M_r], partial=(j > 0))
                    else:
                        P.op("dve", lambda h, W=W, pb=pb, j=j: h.tensor_tensor(out=G2[:, j, 0:W], in0=pb[:, 0:W], in1=G2[:, j, 0:W], op=ALU.mult),
                             reads=[pbr, G2_r], writes=[G2_r], partial=(j > 0))
                        P.op("pool", lambda h, W=W, j=j, ch=ch: h.tensor_tensor(out=mxT[:, ch, 0:W], in0=M[:, j, 0:W], in1=G2[:, j, 0:W], op=ALU.add),
                             reads=[M_r, G2_r], writes=[mxT_r], partial=True)

            for cg in range(8):
                proj_b(wbg, cg * 512, hnT, hnT_r, 2, "g1", cg)
                proj_b(wbg, D + cg * 512, hnT, hnT_r, 2, "g2", cg)
                proj_b(wso, cg * 512, ynT, ynT_r, 2, "s", cg)
                proj_b(wao, cg * 512, yaT, yaT_r, 1, "a", cg)
            for cg in range(8):
                base = 4 * (grp[0] % 2)
                grp[0] += 1
                for kh in range(2):
                    w, wr = load_w(wo, kh, cg * 512)
                    for i in range(nt):
                        pb, pbr = ps[base + i], ps_r[base + i]
                        for k in range(16):
                            kc = kh * 16 + k
                            P.op("pe", lambda h, pb=pb, w=w, k=k, i=i, kc=kc: h.matmul(pb[:, :], mxT[:, kc, i * 128:(i + 1) * 128], w[:, k, :],
                                                                                 start=(kc == 0), stop=(kc == NKC - 1)),
                                 reads=[mxT_r, wr], writes=[pbr], sig=(k == 15), partial=(kc > 0))
                for i in range(nt):
                    pb, pbr = ps[base + i], ps_r[base + i]
                    r0_ = tok0 + i * 128
                    x_, x_r = xring.next()
                    P.dma("sp", x_[:, :], cx.xin[r0_:r0_ + 128, cg * 512:(cg + 1) * 512], x_r.dsem, writes=[x_r])
                    P.op("dve", lambda h, pb=pb, x_=x_: h.tensor_tensor(out=x_[:, :], in0=pb[:, :], in1=x_[:, :], op=ALU.add), reads=[pbr, x_r], writes=[x_r])
                    P.dma("sp", cx.H1[r0_:r0_ + 128, cg * 512:(cg + 1) * 512], x_[:, :], x_r.dsem, reads=[x_r], writes=[cx.H1_r], partial=True)
        _flush(P)


def phase_D2(cx):
    nc, P, NT = cx.nc, cx.P, cx.NT
    NFC = D_FF // 128
    with ExitStack() as st:
        sb = lambda n, s, dt_: st.enter_context(nc.sbuf_tensor(n, s, dt_))
        R = lambda n, **kw: P.res("f_" + n, **kw)
        WB = 384
        xt = sb("f_xt", [128, D], F32); xt_r = R("xt", dma=True)
        cx.junk = sb("f_junk", [128, D], BF16); cx.junk_r = R("junk")
        hfT = sb("f_hfT", [128, NKC, WB], BF16); hfT_r = R("hfT")
        acT = sb("f_acT", [128, NFC, WB], BF16); acT_r = R("acT")
        Gs = sb("f_Gs", [128, 4, WB], F32); Gs_r = R("Gs")
        gcol = sb("f_gcol", [128, NKC], F32); gcol_r = R("gcol", dma=True)
        sm = sb("f_sm", [128, 8], F32); sm_r = R("sm")
        wt = [sb("f_w%d" % i, [128, 16, 512], BF16) for i in range(3)]; wt_r = [R("w", dma=True) for i in range(3)]
        wring = Ring(list(zip(wt, wt_r)))
        xs_ = [sb("f_xs%d" % i, [128, 512], F32) for i in range(3)]; xs_r = [R("xs", dma=True) for i in range(3)]
        xring = Ring(list(zip(xs_, xs_r)))
        ps = [st.enter_context(nc.psum_tensor("f_ps%d" % i, [128, 512], F32)) for i in range(8)]
        ps_r = [R("ps", excl=True) for i in range(8)]
        _barrier(P)
        P.dma("sp", gcol[:, :], cx.gcol_ffn[:, :], gcol_r.dsem, writes=[gcol_r])
        wgu = cx.w_gu.rearrange("(kc p) n -> p kc n", p=128)
        wdn = cx.w_dn.rearrange("(kc p) n -> p kc n", p=128)
        grp = [0]
        for (t0, nt) in _blocks_n(NT, 3):
            W = nt * 128
            tok0 = t0 * 128
            for i in range(nt):
                r0_ = tok0 + i * 128
                _norm_transpose(cx, P, cx.H1[r0_:r0_ + 128, :], cx.H1_r, xt, xt_r, sm, sm_r, gcol, gcol_r, ps, ps_r, hfT, hfT_r, i, cx.c_identf)
            f0 = 0
            while f0 < NFC:
                nch = min(4, NFC - f0)
                for which in (0, 1):
                    base = 4 * (grp[0] % 2)
                    grp[0] += 1
                    c0 = which * D_FF + f0 * 128
                    for kh in range(2):
                        w, wr = wring.next()
                        P.dma("pool", w[:, :, 0:nch * 128], wgu[:, kh * 16:(kh + 1) * 16, c0:c0 + nch * 128], wr.dsem, writes=[wr])
                        for j in range(nch):
                            pb, pbr = ps[base + j], ps_r[base + j]
                            for k in range(16):
                                kc = kh * 16 + k
                                P.op("pe", lambda h, W=W, pb=pb, w=w, k=k, j=j, kc=kc: h.matmul(pb[:, 0:W], w[:, k, j * 128:(j + 1) * 128], hfT[:, kc, 0:W],
                                                                                     start=(kc == 0), stop=(kc == NKC - 1)),
                                     reads=[hfT_r, wr], writes=[pbr], sig=(k == 15), partial=(kc > 0))
                    for j in range(nch):
                        pb, pbr = ps[base + j], ps_r[base + j]
                        if which == 0:
                            P.op("act", lambda h, W=W, pb=pb, j=j: h.activation(out=Gs[:, j, 0:W], in_=pb[:, 0:W], func=AF.Silu), reads=[pbr], writes=[Gs_r], partial=(j > 0))
                        else:
                            P.op("dve", lambda h, W=W, pb=pb, j=j, f0=f0: h.tensor_tensor(out=acT[:, f0 + j, 0:W], in0=pb[:, 0:W], in1=Gs[:, j, 0:W], op=ALU.mult),
                                 reads=[pbr, Gs_r], writes=[acT_r], partial=True)
                f0 += nch
            for cg in range(8):
                base = 4 * (grp[0] % 2)
                grp[0] += 1
                fu = 0
                while fu < NFC:
                    nk = min(16, NFC - fu)
                    w, wr = wring.next()
                    P.dma("pool", w[:, 0:nk, :], wdn[:, fu:fu + nk, cg * 512:(cg + 1) * 512], wr.dsem, writes=[wr])
                    for i in range(nt):
                        pb, pbr = ps[base + i], ps_r[base + i]
                        for k in range(nk):
                            fc = fu + k
                            P.op("pe", lambda h, pb=pb, w=w, k=k, i=i, fc=fc: h.matmul(pb[:, :], acT[:, fc, i * 128:(i + 1) * 128], w[:, k, :],
                                                                                 start=(fc == 0), stop=(fc == NFC - 1)),
                                 reads=[acT_r, wr], writes=[pbr], sig=(k == nk - 1), partial=(fc > 0))
                    fu += nk
                for i in range(nt):
                    t = t0 + i
                    pb, pbr = ps[base + i], ps_r[base + i]
                    if t == 0:
                        continue
                    r0_ = t * 128
                    x_, x_r = xring.next()
                    P.dma("sp", x_[:, :], cx.H1[r0_:r0_ + 128, cg * 512:(cg + 1) * 512], x_r.dsem, reads=[cx.H1_r], writes=[x_r])
                    P.op("dve", lambda h, pb=pb, x_=x_: h.tensor_tensor(out=x_[:, :], in0=pb[:, :], in1=x_[:, :], op=ALU.add), reads=[pbr, x_r], writes=[x_r])
                    P.dma("sp", cx.out[r0_ - 128:r0_, cg * 512:(cg + 1) * 512], x_[:, :], x_r.dsem, reads=[x_r], writes=[cx.out_r], partial=True)
        _flush(P)


def build(NT, phases="A", dbg=()):
    T = NT * 128
    nc = bass.Bass("TRN2", target_bir_lowering=False)
    cx = Cx()
    cx.nc, cx.NT = nc, NT
    kin = "ExternalInput"

    def din(name, shape, dt=F32):
        return nc.dram_tensor(name, list(shape), dt, kind=kin).ap()

    def dscr(name, shape, dt):
        kind = "ExternalOutput" if name in dbg else "Internal"
        return nc.dram_tensor(name, list(shape), dt, kind=kind).ap()

    cx.xin = din("xin", [T, D])
    cx.w_in = din("w_in", [D, D_IN])
    cx.gcol_mix = din("gcol_mix", [128, NKC])
    cx.dt_bias = din("dt_bias", [1, 128])
    cx.a_log = din("a_log", [1, 128])
    cx.gqk = din("gqk", [128, 2])
    cx.convw = din("convw", [128, 48, 5])
    cx.convb = din("convb", [128, 48])
    cx.d_skip = din("d_skip", [1, 64])
    cx.ssd_norm_g = din("ssd_norm_g", [1, D])
    cx.gsub = din("gsub", [128, 1])
    cx.lam4 = din("lam4", [1, 256])
    cx.bgcol = din("bgcol", [128, 64])
    cx.gcol_ffn = din("gcol_ffn", [128, NKC])
    cx.w_bg = din("w_bg", [D, 2 * D])
    cx.w_so = din("w_so", [D, D])
    cx.w_ao = din("w_ao", [2048, D])
    cx.w_o = din("w_o", [D, D])
    cx.w_gu = din("w_gu", [D, 2 * D_FF])
    cx.w_dn = din("w_dn", [D_FF, D])
    consts = make_consts(NT)
    cdr = {}
    for k, v in consts.items():
        cdr[k] = din("c_" + k, v.shape, BF16 if v.dtype == ml_dtypes.bfloat16 else F32)
    cx.HNT = dscr("HNT", [128, NKC, T], BF16)
    cx.XBC = dscr("XBC", [128, 48, T + 4], F32)
    cx.QT = dscr("QT", [128, 16, T], BF16)
    cx.KT = dscr("KT", [128, 16, T], BF16)
    cx.V = dscr("V", [T, 2048], BF16)
    cx.ZS = dscr("ZS", [T, D], F32)
    cx.DTD = dscr("DTD", [128, NT, 128], F32)
    cx.XTOK = dscr("XTOK", [T, D], BF16)
    cx.BTOK = dscr("BTOK", [T, 1024], BF16)
    cx.BCT = dscr("BCT", [128, 16, T], BF16)
    cx.Y1 = dscr("Y1", [T, D], F32)
    cx.YNT = dscr("YNT", [128, NKC, T], BF16)
    cx.YAT = dscr("YAT", [128, 16, T], BF16)
    cx.H1 = dscr("H1", [T, D], F32)
    cx.out = nc.dram_tensor("out", [T - 128, D], F32, kind="ExternalOutput").ap()

    with ExitStack() as st:
        P = Prog(nc, st)
        cx.P = P
        for n in ("HNT", "XBC", "QT", "KT", "V", "ZS", "DTD", "XTOK", "BTOK", "BCT", "Y1", "YNT", "YAT", "H1", "out"):
            setattr(cx, n + "_r", P.res(n))
        sb = lambda n, s, d: st.enter_context(nc.sbuf_tensor(n, s, d))
        cx.DT_r = [P.res("DT") for _ in range(NT)]
        cx.DA_r = [P.res("DA") for _ in range(NT)]
        cx.c_r = P.res("consts", dma=True)
        for k in ("ident", "blk64", "ones", "m_kles", "m_kgts", "m_kgel", "m_klts", "onesf"):
            v = consts[k]
            t_ = sb("s_" + k, list(v.shape), BF16 if v.dtype == ml_dtypes.bfloat16 else F32)
            setattr(cx, "c_" + k, t_)
            P.dma("sp", t_[:, :], cdr[k][:, :], cx.c_r.dsem, writes=[cx.c_r], partial=True)
        cx.c_identf = sb("s_identf", [128, 128], F32)
        P.op("dve", lambda h: h.tensor_copy(out=cx.c_identf[:, :], in_=cx.c_ident[:, :]), reads=[cx.c_r], writes=[cx.c_r], partial=True)
        cx.cdr = cdr
        st_ab = ExitStack()
        cx.DT = st_ab.enter_context(nc.sbuf_tensor("DT", [128, NT, 128], F32))
        cx.DA = st_ab.enter_context(nc.sbuf_tensor("DA", [128, NT, 128], F32))
        _flush(P)
        if "A" in phases:
            phase_A(cx)
        if "B" in phases:
            phase_B(cx, 0)
            phase_B(cx, 1)
        if "DTD" not in dbg:
            st_ab.close()
        if "C" in phases:
            phase_C(cx)
        if "D" in phases:
            phase_D1(cx)
            phase_D2(cx)
        if "DTD" in dbg:
            _barrier(P)
            dd = P.new_dsem("dtd")
            P.dma("sp", cx.DTD[:, :, :], cx.DT[:, :, :], dd, reads=cx.DT_r)
        _barrier(P)
        _flush(P, final=True)
    return nc, consts


def host_inputs(NT, seq, meta_tokens, w, consts):
    T = NT * 128
    xin = np.zeros((T, D), np.float32)
    xin[NPAD:128] = meta_tokens
    xin[128:] = seq
    m = {"xin": xin}
    m["w_in"] = w["w_in"][0]
    m["gcol_mix"] = np.ascontiguousarray(w["norm_mix_g"][0].reshape(NKC, 128).T)
    m["dt_bias"] = np.ascontiguousarray(w["dt_bias"][0].reshape(1, 128))
    m["a_log"] = np.ascontiguousarray(w["a_log"][0].reshape(1, 128))
    gqk = np.stack([np.tile(w["q_norm_g"][0], 2), np.tile(w["k_norm_g"][0], 2)], axis=1)
    m["gqk"] = np.ascontiguousarray(gqk.astype(np.float32))
    m["convw"] = np.ascontiguousarray(w["conv_w"][0].T.reshape(48, 128, 5).transpose(1, 0, 2))
    m["convb"] = np.ascontiguousarray(w["conv_b"][0].reshape(48, 128).T)
    m["d_skip"] = np.ascontiguousarray(w["d_skip"][0].reshape(1, 64))
    m["ssd_norm_g"] = np.ascontiguousarray(w["ssd_norm_g"][0].reshape(1, D))
    m["gsub"] = np.ascontiguousarray(w["subln_g"][0].reshape(128, 1))
    m["lam4"] = np.ascontiguousarray(np.concatenate([w["lambda_q1"][0], w["lambda_k1"][0], w["lambda_q2"][0], w["lambda_k2"][0]]).reshape(1, 256))
    m["bgcol"] = np.ascontiguousarray(w["b_branch_gate"][0].reshape(64, 128).T)
    m["gcol_ffn"] = np.ascontiguousarray(w["norm_ffn_g"][0].reshape(NKC, 128).T)
    m["w_bg"] = w["w_branch_gate"][0]
    m["w_so"] = w["w_ssd_out"][0]
    m["w_ao"] = w["w_att_out"][0]
    m["w_o"] = w["w_o"][0]
    m["w_gu"] = w["w_gate_up"][0]
    m["w_dn"] = w["w_down"][0]
    for k, v in consts.items():
        m["c_" + k] = v
    return m


def kernel(**inputs):
    NT = 33
    w = {k: np.asarray(v) for k, v in inputs.items()}
    nc, consts = build(NT, phases="ABCD")
    xp, xs = w["x_prompt"], w["x_sample"]
    seqs = [xp[0], xp[1], xs[0], xs[1], xs[2], xs[3], xp[0], xp[1]]
    meta = w["meta_tokens"]
    in_maps = [host_inputs(NT, s_, meta, w, consts) for s_ in seqs]
    res = run_bass_kernel_spmd(nc, in_maps, core_ids=list(range(8)))
    outs = [np.asarray(r["out"], dtype=np.float32) for r in res.results]
    y_prompt = np.stack(outs[0:2], axis=0)
    y_sample = np.stack(outs[2:6], axis=0)
    return (y_prompt, y_sample)
```

```python
import numpy as np
import ml_dtypes
from contextlib import ExitStack
import concourse.bass as bass
import concourse.mybir as mybir
from concourse.bass_utils import run_bass_kernel_spmd

F32 = mybir.dt.float32
BF16 = mybir.dt.bfloat16
AF = mybir.ActivationFunctionType
ALU = mybir.AluOpType
AX = mybir.AxisListType

D = 4096
NKC = 32
D_IN = 16512
C_XBC = 6144
O_XBC = 4096
O_DT = 10240
O_Q = 10368
O_K = 12416
O_V = 14464
D_FF = 11008
EPS = 1e-6
NPAD = 112


class Res:
    __slots__ = ("name", "lw", "rd", "dsem", "excl", "base")

    def __init__(self, name, dsem=None, excl=False):
        self.name = name
        self.lw = {}
        self.rd = {}
        self.dsem = dsem
        self.excl = excl
        self.base = {}


class DSem:
    def __init__(self, h):
        self.h = h
        self.count = 0


class Eng:
    def __init__(self, name, sem):
        self.name = name
        self.sem = sem
        self.count = 0
        self.waited = {}
        self.ops = []
        self.pending_nosig = False


class Prog:
    def __init__(self, nc, stack):
        self.nc = nc
        self.stack = stack
        self.E = {}
        self.semobj = {}
        for n in ("pe", "act", "dve", "pool", "sp"):
            s = stack.enter_context(nc.semaphore("e_" + n))
            self.E[n] = Eng(n, s)
            self.semobj[id(s)] = ("eng", self.E[n], s)
        self.dsems = []
        self.nres = 0

    def new_dsem(self, name):
        s = self.stack.enter_context(self.nc.semaphore("d_" + name))
        d = DSem(s)
        self.semobj[id(s)] = ("dma", d, s)
        self.dsems.append(d)
        return d

    def res(self, name, dma=False, excl=False):
        self.nres += 1
        return Res(name, self.new_dsem(name + str(self.nres)) if dma else None, excl)

    def _waits(self, eng, reads, writes, partial):
        need = {}

        def add(tok):
            for k, v in tok.items():
                if need.get(k, 0) < v:
                    need[k] = v

        for r in reads:
            add(r.lw)
            if r.excl:
                add(r.rd)
        for w in writes:
            if not partial:
                add(w.lw)
            else:
                add(w.base)
            add(w.rd)
        out = []
        for k, v in need.items():
            kind, obj, h = self.semobj[k]
            if kind == "dma":
                v = obj.count
            else:
                if obj is eng and eng.name == "pe":
                    continue
            if eng.waited.get(k, 0) >= v:
                continue
            eng.waited[k] = v
            out.append((h, v))
        return out

    def _commit(self, key, val, reads, writes, partial):
        for r in reads:
            if r.rd.get(key, 0) < val:
                r.rd[key] = val
        for w in writes:
            if partial:
                if w.lw.get(key, 0) < val:
                    w.lw[key] = val
            else:
                b = dict(w.lw)
                for k2, v2 in w.rd.items():
                    if b.get(k2, 0) < v2:
                        b[k2] = v2
                w.base = b
                w.lw = {key: val}
                w.rd = {}

    def op(self, en, fn, reads=(), writes=(), sig=True, partial=False):
        eng = self.E[en]
        waits = self._waits(eng, reads, writes, partial)
        if sig:
            eng.count += 1
            val = eng.count
            inc = (eng.sem, 1)
            eng.pending_nosig = False
        else:
            val = eng.count + 1
            inc = None
            eng.pending_nosig = True
        eng.ops.append((waits, fn, inc))
        self._commit(id(eng.sem), val, reads, writes, partial)

    def dma(self, qn, out, in_, dsem, reads=(), writes=(), partial=False):
        eng = self.E[qn]
        waits = self._waits(eng, reads, writes, partial)
        dsem.count += 16
        eng.ops.append((waits, lambda h: h.dma_start(out=out, in_=in_), (dsem.h, 16)))
        self._commit(id(dsem.h), dsem.count, reads, writes, partial)

    def emit(self):
        nc = self.nc
        sp = self.E["sp"]
        fin = [(d.h, d.count) for d in self.dsems if d.count > 0]
        for n, e in self.E.items():
            assert not e.pending_nosig, n
        with nc.Block() as block:
            def mk(eng, final):
                def f(h):
                    for waits, fn, inc in eng.ops:
                        for (s, v) in waits:
                            h.wait_ge(s, v)
                        ins = fn(h)
                        if inc is not None:
                            ins.then_inc(inc[0], inc[1])
                    if final:
                        for (s, v) in fin:
                            h.wait_ge(s, v)
                return f
            block.tensor(mk(self.E["pe"], False))
            block.scalar(mk(self.E["act"], False))
            block.vector(mk(self.E["dve"], False))
            block.gpsimd(mk(self.E["pool"], False))
            block.sync(mk(self.E["sp"], True))


class Ring:
    def __init__(self, items):
        self.items = items
        self.i = 0

    def next(self):
        it = self.items[self.i % len(self.items)]
        self.i += 1
        return it


def make_consts(NT):
    T = NT * 128
    i = np.arange(128)[:, None]
    j = np.arange(128)[None, :]
    c = {}
    c["ident"] = (i == j).astype(ml_dtypes.bfloat16)
    c["blk64"] = ((i // 64) == (j // 64)).astype(ml_dtypes.bfloat16)
    c["ones"] = np.ones((128, 128), ml_dtypes.bfloat16)
    u = np.arange(2 * T + 512)[None, :]
    c["absd"] = np.abs(u - i - T).astype(np.float32)
    pb = np.zeros((128, 1), np.float32)
    pb[:NPAD] = -30000.0
    c["padbias"] = pb
    c["m_kles"] = (i <= j).astype(np.float32)
    c["m_kgts"] = (i > j).astype(np.float32)
    c["m_kgel"] = (i >= j).astype(np.float32)
    c["m_klts"] = (i < j).astype(np.float32)
    c["onesf"] = np.ones((128, 128), np.float32)
    return c


class Cx:
    pass


class _Stop(Exception):
    pass


def _chk(cx, n):
    import os
    if int(os.environ.get('K_STOP', '0')) == n:
        raise _Stop()


def _barrier(P):
    toks = []
    for n, e in P.E.items():
        assert not e.pending_nosig
        if e.count > 0:
            toks.append((id(e.sem), e.sem, e.count))
    for d in P.dsems:
        if d.count > 0:
            toks.append((id(d.h), d.h, d.count))
    for n, e in P.E.items():
        waits = []
        for k, h, v in toks:
            if k == id(e.sem):
                continue
            if e.waited.get(k, 0) >= v:
                continue
            e.waited[k] = v
            waits.append((h, v))
        if waits:
            e.ops.append((waits, None, None))


def _flush(P, final=False):
    nc = P.nc
    fin = [(d.h, d.count) for d in P.dsems if d.count > 0] if final else []
    for n, e in P.E.items():
        assert not e.pending_nosig, n
    with nc.Block() as block:
        def mk(eng, fl):
            ops = eng.ops

            def f(h):
                for waits, fn, inc in ops:
                    for (s, v) in waits:
                        h.wait_ge(s, v)
                    if fn is None:
                        continue
                    ins = fn(h)
                    if inc is not None:
                        ins.then_inc(inc[0], inc[1])
                if fl:
                    for (s, v) in fin:
                        h.wait_ge(s, v)
            return f
        block.tensor(mk(P.E["pe"], False))
        block.scalar(mk(P.E["act"], False))
        block.vector(mk(P.E["dve"], False))
        block.gpsimd(mk(P.E["pool"], False))
        block.sync(mk(P.E["sp"], final))
    for e in P.E.values():
        e.ops = []


def _blocks(NT):
    out = []
    t = 0
    while t < NT:
        n = min(4, NT - t)
        out.append((t, n))
        t += n
    return out


def phase_A(cx):
    nc, P, NT = cx.nc, cx.P, cx.NT
    T = NT * 128
    with ExitStack() as st:
        sb = lambda n, s, d: st.enter_context(nc.sbuf_tensor(n, s, d))
        xt = sb("a_xt", [128, D], F32)
        xt_r = P.res("a_xt", dma=True)
        hn = sb("a_hn", [128, D], F32)
        hn_r = P.res("a_hn")
        hnT = sb("a_hnT", [128, NKC, 512], BF16)
        hnT_r = P.res("a_hnT", dma=True)
        gcol = sb("a_gcol", [128, NKC], F32)
        gcol_r = P.res("a_gcol", dma=True)
        NW = 3
        wt = [sb("a_w%d" % i, [128, 16, 512], BF16) for i in range(NW)]
        wt_r = [P.res("a_w", dma=True) for i in range(NW)]
        wring = Ring(list(zip(wt, wt_r)))
        wdt = sb("a_wdt", [128, NKC, 128], BF16)
        wdt_r = P.res("a_wdt", dma=True)
        sm = sb("a_sm", [128, 8], F32)
        sm_r = P.res("a_sm")
        NS = 4
        sf = [sb("a_sf%d" % i, [128, 512], F32) for i in range(NS)]
        sf_r = [P.res("a_sf", dma=True) for i in range(NS)]
        sfring = Ring(list(zip(sf, sf_r)))
        sh = [sb("a_sh%d" % i, [128, 512], BF16) for i in range(NS)]
        sh_r = [P.res("a_sh", dma=True) for i in range(NS)]
        shring = Ring(list(zip(sh, sh_r)))
        sq = [sb("a_sq%d" % i, [128, 512], BF16) for i in range(2)]
        sq_r = [P.res("a_sq") for i in range(2)]
        sqring = Ring(list(zip(sq, sq_r)))
        rq = [sb("a_rq%d" % i, [128, 512], F32) for i in range(2)]
        rq_r = [P.res("a_rq") for i in range(2)]
        rqring = Ring(list(zip(rq, rq_r)))
        dtb = sb("a_dtb", [128, 128], F32)
        dtb_r = P.res("a_dtb", dma=True)
        Ab = sb("a_Ab", [128, 128], F32)
        Ab_r = P.res("a_Ab", dma=True)
        gq = sb("a_gq", [128, 2], F32)
        gq_r = P.res("a_gq", dma=True)
        zt = sb("a_zero", [128, 48, 2], F32)
        zt_r = P.res("a_zero", dma=True)
        ps = [st.enter_context(nc.psum_tensor("a_ps%d" % i, [128, 512], F32)) for i in range(8)]
        ps_r = [P.res("a_ps", excl=True) for i in range(8)]

        _barrier(P)
        try:
          _phase_A_body(cx, locals())
        except _Stop:
          pass
        _flush(P)


def _phase_A_body(cx, L):
        globals().update({})
        nc, P, NT = cx.nc, cx.P, cx.NT
        T = NT * 128
        for _k, _v in L.items():
            pass
        xt, xt_r, hn, hn_r, hnT, hnT_r, gcol, gcol_r, wring, wdt, wdt_r, sm, sm_r, sfring, shring, sqring, rqring, dtb, dtb_r, Ab, Ab_r, gq, gq_r, zt, zt_r, ps, ps_r = [L[k] for k in 'xt xt_r hn hn_r hnT hnT_r gcol gcol_r wring wdt wdt_r sm sm_r sfring shring sqring rqring dtb dtb_r Ab Ab_r gq gq_r zt zt_r ps ps_r'.split()]
        P.dma("sp", gcol[:, :], cx.gcol_mix[:, :], gcol_r.dsem, writes=[gcol_r])
        P.dma("sp", dtb[:, :], cx.dt_bias[0:1, :].partition_broadcast(128), dtb_r.dsem, writes=[dtb_r])
        P.dma("sp", Ab[:, :], cx.a_log[0:1, :].partition_broadcast(128), Ab_r.dsem, writes=[Ab_r])
        P.op("act", lambda h: h.activation(out=Ab[:, :], in_=Ab[:, :], func=AF.Exp), reads=[Ab_r], writes=[Ab_r])
        P.op("dve", lambda h: h.tensor_scalar(out=Ab[:, :], in0=Ab[:, :], scalar1=-1.0, scalar2=None, op0=ALU.mult),
             reads=[Ab_r], writes=[Ab_r])
        P.dma("sp", gq[:, :], cx.gqk[:, :], gq_r.dsem, writes=[gq_r])
        P.op("dve", lambda h: h.tensor_scalar(out=gq[:, 0:1], in0=gq[:, 0:1], scalar1=0.125, scalar2=None, op0=ALU.mult),
             reads=[gq_r], writes=[gq_r])
        P.op("pool", lambda h: h.memset(zt[:, :, :], 0.0), writes=[zt_r])
        P.dma("sp", cx.XBC[:, :, 0:2], zt[:, :, :], zt_r.dsem, reads=[zt_r], writes=[cx.XBC_r], partial=True)
        P.dma("sp", cx.XBC[:, :, T + 2:T + 4], zt[:, :, :], zt_r.dsem, reads=[zt_r], writes=[cx.XBC_r], partial=True)

        _chk(cx, 1)
        win = cx.w_in.rearrange("(kc p) n -> p kc n", p=128)
        identf = cx.c_identf
        evi = [0]

        def evac_copy(out_ap, in_ap, reads, writes):
            evi[0] += 1
            if evi[0] % 2 == 0:
                P.op("act", lambda h: h.copy(out=out_ap, in_=in_ap), reads=reads, writes=writes)
            else:
                P.op("dve", lambda h: h.tensor_copy(out=out_ap, in_=in_ap), reads=reads, writes=writes)

        def load_w(c0, kh, ncols=512):
            w, wr = wring.next()
            P.dma("pool", w[:, :, 0:ncols], win[:, kh * 16:(kh + 1) * 16, c0:c0 + ncols], wr.dsem, writes=[wr])
            return w, wr

        grp = [0]
        for (t0, nt) in _blocks(NT):
            W = nt * 128
            tok0 = t0 * 128
            for i in range(nt):
                P.dma("sp", xt[:, :], cx.xin[tok0 + i * 128: tok0 + (i + 1) * 128, :], xt_r.dsem, writes=[xt_r])
                P.op("act", lambda h: h.activation(out=hn[:, :], in_=xt[:, :], func=AF.Square, accum_out=sm[:, 0:1]),
                     reads=[xt_r], writes=[hn_r, sm_r])
                P.op("act", lambda h: h.activation(out=sm[:, 1:2], in_=sm[:, 0:1], func=AF.Sqrt, bias=EPS, scale=1.0 / D),
                     reads=[sm_r], writes=[sm_r])
                P.op("dve", lambda h: h.reciprocal(out=sm[:, 2:3], in_=sm[:, 1:2]), reads=[sm_r], writes=[sm_r])
                P.op("dve", lambda h: h.tensor_scalar(out=hn[:, :], in0=xt[:, :], scalar1=sm[:, 2:3], scalar2=None, op0=ALU.mult),
                     reads=[xt_r, sm_r], writes=[hn_r])
                for q4 in range(8):
                    pb, pbr = ps[q4 % 2], ps_r[q4 % 2]
                    for k4 in range(4):
                        kc = q4 * 4 + k4
                        P.op("pe", lambda h, pb=pb, k4=k4, kc=kc: h.transpose(pb[:, k4 * 128:(k4 + 1) * 128], hn[:, kc * 128:(kc + 1) * 128], identf[:, :]),
                             reads=[hn_r, cx.c_r], writes=[pbr], sig=(k4 == 3), partial=(k4 > 0))
                    P.op("dve", lambda h, pb=pb, q4=q4, i=i: h.tensor_tensor(
                        out=hnT[:, q4 * 4:(q4 + 1) * 4, i * 128:(i + 1) * 128],
                        in0=pb[:, :].rearrange("p (k n) -> p k n", k=4),
                        in1=gcol[:, q4 * 4:(q4 + 1) * 4].unsqueeze(2).to_broadcast([128, 4, 128]), op=ALU.mult),
                        reads=[pbr, gcol_r], writes=[hnT_r], partial=True)
            _chk(cx, 2)
            P.dma("sp", cx.HNT[:, :, tok0:tok0 + W], hnT[:, :, 0:W], hnT_r.dsem, reads=[hnT_r], writes=[cx.HNT_r], partial=True)

            _chk(cx, 3)
            P.dma("pool", wdt[:, :, :], win[:, :, O_DT:O_DT + 128], wdt_r.dsem, writes=[wdt_r])
            for i in range(nt):
                t = t0 + i
                pb, pbr = ps[i % 2], ps_r[i % 2]
                for kc in range(NKC):
                    P.op("pe", lambda h, pb=pb, kc=kc, i=i: h.matmul(pb[:, 0:128], hnT[:, kc, i * 128:(i + 1) * 128], wdt[:, kc, :],
                                                                   start=(kc == 0), stop=(kc == NKC - 1)),
                         reads=[hnT_r, wdt_r], writes=[pbr], sig=(kc == NKC - 1), partial=(kc > 0))
                dtt = cx.DT[:, t, :]
                dr = cx.DT_r[t]
                P.op("dve", lambda h, pb=pb, dtt=dtt: h.tensor_tensor(out=dtt, in0=pb[:, 0:128], in1=dtb[:, :], op=ALU.add),
                     reads=[pbr, dtb_r], writes=[dr])
                P.op("act", lambda h, dtt=dtt: h.activation(out=dtt, in_=dtt, func=AF.Exp), reads=[dr], writes=[dr])
                P.op("act", lambda h, dtt=dtt: h.activation(out=dtt, in_=dtt, func=AF.Ln, bias=1.0, scale=1.0),
                     reads=[dr], writes=[dr])
                if t == 0:
                    P.op("pool", lambda h: h.memset(cx.DT[0:NPAD, 0, :], 0.0), reads=[dr], writes=[dr])
                P.op("pool", lambda h, dtt=dtt, t=t: h.tensor_tensor(out=cx.DA[:, t, :], in0=dtt, in1=Ab[:, :], op=ALU.mult),
                     reads=[dr, Ab_r], writes=[cx.DA_r[t]])

            _chk(cx, 4)
            def group_b(c0, kind):
                base = 4 * (grp[0] % 2)
                grp[0] += 1
                for kh in range(2):
                    w, wr = load_w(c0, kh)
                    for j in range(4):
                        pb, pbr = ps[base + j], ps_r[base + j]
                        for k in range(16):
                            kc = kh * 16 + k
                            P.op("pe", lambda h, W=W, pb=pb, w=w, k=k, j=j, kc=kc: h.matmul(pb[:, 0:W], w[:, k, j * 128:(j + 1) * 128], hnT[:, kc, 0:W],
                                                                                 start=(kc == 0), stop=(kc == NKC - 1)),
                                 reads=[hnT_r, wr], writes=[pbr], sig=(k == 15), partial=(kc > 0))
                for j in range(4):
                    pb, pbr = ps[base + j], ps_r[base + j]
                    col = c0 + j * 128
                    if kind == "xbc":
                        s_, sr = sfring.next()
                        evac_copy(s_[:, 0:W], pb[:, 0:W], [pbr], [sr])
                        cc = (col - O_XBC) // 128
                        P.dma("sp", cx.XBC[:, cc, 2 + tok0: 2 + tok0 + W], s_[:, 0:W], sr.dsem, reads=[sr], writes=[cx.XBC_r], partial=True)
                    else:
                        isq = (kind == "q")
                        hd = (col - (O_Q if isq else O_K)) // 128
                        gc = gq[:, 0:1] if isq else gq[:, 1:2]
                        s_, sr = sfring.next()
                        q2, q2r = sqring.next()
                        r_, rr = rqring.next()
                        o_, orr = shring.next()
                        P.op("dve", lambda h, W=W, s_=s_, pb=pb: h.tensor_copy(out=s_[:, 0:W], in_=pb[:, 0:W]), reads=[pbr], writes=[sr])
                        P.op("act", lambda h, W=W, q2=q2, pb=pb: h.activation(out=q2[:, 0:W], in_=pb[:, 0:W], func=AF.Square), reads=[pbr], writes=[q2r])
                        import os as _os
                        _qv = int(_os.environ.get('K_QV', '0'))
                        if _qv == 1:
                            continue
                        P.op("pe", lambda h, W=W, pb=pb, q2=q2: h.matmul(pb[:, 0:W], cx.c_blk64[:, :], q2[:, 0:W], start=True, stop=True),
                             reads=[q2r, cx.c_r], writes=[pbr])
                        if _qv == 2:
                            continue
                        P.op("act", lambda h, W=W, r_=r_, pb=pb: h.activation(out=r_[:, 0:W], in_=pb[:, 0:W], func=AF.Sqrt, bias=EPS, scale=1.0 / 64),
                             reads=[pbr], writes=[rr])
                        if _qv == 3:
                            continue
                        P.op("dve", lambda h, W=W, r_=r_: h.reciprocal(out=r_[:, 0:W], in_=r_[:, 0:W]), reads=[rr], writes=[rr])
                        P.op("dve", lambda h, W=W, o_=o_, s_=s_, r_=r_, gc=gc: h.scalar_tensor_tensor(out=o_[:, 0:W], in0=s_[:, 0:W], scalar=gc, in1=r_[:, 0:W],
                                                                                      op0=ALU.mult, op1=ALU.mult),
                             reads=[sr, rr, gq_r], writes=[orr])
                        dst = cx.QT if isq else cx.KT
                        dst_r = cx.QT_r if isq else cx.KT_r
                        P.dma("sp", dst[:, hd, tok0:tok0 + W], o_[:, 0:W], orr.dsem, reads=[orr], writes=[dst_r], partial=True)

            for g in range(12):
                group_b(O_XBC + g * 512, "xbc")
                _chk(cx, 5)
            _chk(cx, 6)
            for g in range(4):
                group_b(O_Q + g * 512, "q")
            for g in range(4):
                group_b(O_K + g * 512, "k")
            _chk(cx, 7)

            def group_a(c0, kind):
                base = 4 * (grp[0] % 2)
                grp[0] += 1
                for kh in range(2):
                    w, wr = load_w(c0, kh)
                    for i in range(nt):
                        pb, pbr = ps[base + i], ps_r[base + i]
                        for k in range(16):
                            kc = kh * 16 + k
                            P.op("pe", lambda h, pb=pb, w=w, k=k, i=i, kc=kc: h.matmul(pb[:, :], hnT[:, kc, i * 128:(i + 1) * 128], w[:, k, :],
                                                                                 start=(kc == 0), stop=(kc == NKC - 1)),
                                 reads=[hnT_r, wr], writes=[pbr], sig=(k == 15), partial=(kc > 0))
                for i in range(nt):
                    pb, pbr = ps[base + i], ps_r[base + i]
                    r0 = tok0 + i * 128
                    if kind == "v":
                        o_, orr = shring.next()
                        evac_copy(o_[:, :], pb[:, :], [pbr], [orr])
                        cv = c0 - O_V
                        P.dma("sp", cx.V[r0:r0 + 128, cv:cv + 512], o_[:, :], orr.dsem, reads=[orr], writes=[cx.V_r], partial=True)
                    else:
                        s_, sr = sfring.next()
                        P.op("act", lambda h, s_=s_, pb=pb: h.activation(out=s_[:, :], in_=pb[:, :], func=AF.Silu), reads=[pbr], writes=[sr])
                        P.dma("sp", cx.ZS[r0:r0 + 128, c0:c0 + 512], s_[:, :], sr.dsem, reads=[sr], writes=[cx.ZS_r], partial=True)

            for g in range(4):
                group_a(O_V + g * 512, "v")
            _chk(cx, 8)
            for g in range(8):
                group_a(g * 512, "z")


def _bc(ap, shape, axis):
    return ap.unsqueeze(axis).to_broadcast(shape)


def phase_B(cx, sweep):
    nc, P, NT = cx.nc, cx.P, cx.NT
    T = NT * 128
    d = sweep
    with ExitStack() as st:
        sb = lambda n, s, dt_: st.enter_context(nc.sbuf_tensor(n, s, dt_))
        pfx = "b%d_" % sweep
        R = lambda n, **kw: P.res(pfx + n, **kw)
        xin_ = sb(pfx + "xin", [128, 16, 132] if sweep == 0 else [128, 1, 2], F32); xin_r = R("xin", dma=True)
        acc = sb(pfx + "acc", [128, 16, 128] if sweep == 0 else [128, 1, 2], F32); acc_r = R("acc")
        tmp = sb(pfx + "tmp", [128, 16, 128] if sweep == 0 else [128, 1, 2], F32); tmp_r = R("tmp")
        xbcT = sb(pfx + "xbcT", [128, 48, 128], BF16); xbcT_r = R("xbcT", dma=True)
        xtok = sb(pfx + "xtok", [128, D], BF16); xtok_r = R("xtok", dma=True)
        btok = sb(pfx + "btok", [128, 1024], BF16); btok_r = R("btok", dma=True)
        xw = sb(pfx + "xw", [128, D], BF16); xw_r = R("xw")
        S = sb(pfx + "S", [128, D], F32); S_r = R("S")
        Sb = sb(pfx + "Sb", [128, D], BF16); Sb_r = R("Sb")
        aU = sb(pfx + "aU", [128, 8, 128], F32); aU_r = R("aU")
        E = sb(pfx + "E", [128, 8, 128], F32); E_r = R("E")
        MT = sb(pfx + "MT", [128, 8, 128], BF16); MT_r = R("MT")
        CBm = sb(pfx + "CBm", [128, 128], F32); CBm_r = R("CBm")
        yacc = sb(pfx + "yacc", [128, D], F32); yacc_r = R("yacc", dma=True)
        yos = sb(pfx + "yos", [128, 512], F32); yos_r = R("yos")
        ecs = sb(pfx + "ecs", [128, 64], F32); ecs_r = R("ecs")
        wst = sb(pfx + "wst", [128, 64], F32); wst_r = R("wst")
        dec = sb(pfx + "dec", [128, 64], F32); dec_r = R("dec")
        cw = sb(pfx + "cw", [128, 48, 5], F32); cw_r = R("cw", dma=True)
        cb = sb(pfx + "cb", [128, 48], F32); cb_r = R("cb", dma=True)
        dsk = sb(pfx + "dsk", [128, 64], F32); dsk_r = R("dsk", dma=True)
        zs = sb(pfx + "zs", [128, D] if sweep == 1 else [128, 2], F32); zs_r = R("zs", dma=True)
        gs = sb(pfx + "gs", [128, D] if sweep == 1 else [128, 2], F32); gs_r = R("gs", dma=True)
        ynb = sb(pfx + "ynb", [128, D] if sweep == 1 else [128, 2], BF16); ynb_r = R("ynb")
        ynT = sb(pfx + "ynT", [128, NKC, 128] if sweep == 1 else [128, 1, 2], BF16); ynT_r = R("ynT", dma=True)
        sm = sb(pfx + "sm", [128, 24], F32); sm_r = R("sm")
        psA = [st.enter_context(nc.psum_tensor(pfx + "psA%d" % i, [128, 512], F32)) for i in range(5)]
        psA_r = [R("psA", excl=True) for i in range(5)]
        psT = [st.enter_context(nc.psum_tensor(pfx + "psT%d" % i, [128, 1024], BF16)) for i in range(2)]
        psT_r = [R("psT", excl=True) for i in range(2)]
        psS = st.enter_context(nc.psum_tensor(pfx + "psS", [128, 512], F32)); psS_r = R("psS", excl=True)
        ident = cx.c_ident
        _barrier(P)
        if sweep == 0:
            P.dma("sp", cw[:, :, :], cx.convw[:, :, :], cw_r.dsem, writes=[cw_r])
            P.dma("sp", cb[:, :], cx.convb[:, :], cb_r.dsem, writes=[cb_r])
            P.dma("sp", dsk[:, :], cx.d_skip[0:1, :].partition_broadcast(128), dsk_r.dsem, writes=[dsk_r])
        else:
            P.dma("sp", gs[:, :], cx.ssd_norm_g[0:1, :].partition_broadcast(128), gs_r.dsem, writes=[gs_r])
        P.op("pool", lambda h: h.memset(S[:, :], 0.0), writes=[S_r])
        P.op("pool", lambda h: h.memset(Sb[:, :], 0.0), writes=[Sb_r])
        m_cum = cx.c_m_kles if d == 0 else cx.c_m_kgel
        m_seg = cx.c_m_kgts if d == 0 else cx.c_m_klts
        order = list(range(NT)) if d == 0 else list(range(NT - 1, -1, -1))
        for t in order:
            tok0 = t * 128
            dtr, dar = cx.DT_r[t], cx.DA_r[t]
            DTt = cx.DT[:, t, d * 64:(d + 1) * 64]
            DAt = cx.DA[:, t, d * 64:(d + 1) * 64]
            if sweep == 0:
                for c3 in range(3):
                    P.dma("sp", xin_[:, :, :], cx.XBC[:, c3 * 16:(c3 + 1) * 16, tok0:tok0 + 132], xin_r.dsem, reads=[cx.XBC_r], writes=[xin_r])
                    for k in range(5):
                        wk = _bc(cw[:, c3 * 16:(c3 + 1) * 16, k], [128, 16, 128], 2)
                        if k == 0:
                            P.op("dve", lambda h, wk=wk: h.tensor_tensor(out=acc[:, :, :], in0=xin_[:, :, 0:128], in1=wk, op=ALU.mult),
                                 reads=[xin_r, cw_r], writes=[acc_r])
                        else:
                            P.op("pool", lambda h, wk=wk, k=k: h.tensor_tensor(out=tmp[:, :, :], in0=xin_[:, :, k:k + 128], in1=wk, op=ALU.mult),
                                 reads=[xin_r, cw_r], writes=[tmp_r])
                            P.op("dve", lambda h: h.tensor_tensor(out=acc[:, :, :], in0=acc[:, :, :], in1=tmp[:, :, :], op=ALU.add),
                                 reads=[acc_r, tmp_r], writes=[acc_r])
                    P.op("dve", lambda h, c3=c3: h.tensor_tensor(out=acc[:, :, :], in0=acc[:, :, :], in1=_bc(cb[:, c3 * 16:(c3 + 1) * 16], [128, 16, 128], 2), op=ALU.add),
                         reads=[acc_r, cb_r], writes=[acc_r])
                    P.op("act", lambda h, c3=c3: h.activation(out=xbcT[:, c3 * 16:(c3 + 1) * 16, :], in_=acc[:, :, :], func=AF.Silu),
                         reads=[acc_r], writes=[xbcT_r], partial=True)
                if t == 0:
                    P.op("pool", lambda h: h.memset(xbcT[:, :, 0:NPAD], 0.0), reads=[xbcT_r], writes=[xbcT_r])
                for q in range(5):
                    pb, pbr = psT[q % 2], psT_r[q % 2]
                    for k8 in range(8):
                        cc = q * 8 + k8
                        P.op("pe", lambda h, pb=pb, k8=k8, cc=cc: h.transpose(pb[:, k8 * 128:(k8 + 1) * 128], xbcT[:, cc, :], ident[:, :]),
                             reads=[xbcT_r, cx.c_r], writes=[pbr], sig=(k8 == 7), partial=(k8 > 0))
                    if q < 4:
                        P.op("act" if q % 2 else "dve", (lambda h, pb=pb, q=q: h.copy(out=xtok[:, q * 1024:(q + 1) * 1024], in_=pb[:, :])) if q % 2 else
                             (lambda h, pb=pb, q=q: h.tensor_copy(out=xtok[:, q * 1024:(q + 1) * 1024], in_=pb[:, :])),
                             reads=[pbr], writes=[xtok_r], partial=(q > 0))
                    else:
                        P.op("dve", lambda h, pb=pb: h.tensor_copy(out=btok[:, :], in_=pb[:, :]), reads=[pbr], writes=[btok_r])
                P.dma("sp", cx.XTOK[tok0:tok0 + 128, :], xtok[:, :], xtok_r.dsem, reads=[xtok_r], writes=[cx.XTOK_r], partial=True)
                P.dma("sp", cx.BTOK[tok0:tok0 + 128, :], btok[:, :], btok_r.dsem, reads=[btok_r], writes=[cx.BTOK_r], partial=True)
                P.dma("sp", cx.BCT[:, :, tok0:tok0 + 128], xbcT[:, 32:48, :], xbcT_r.dsem, reads=[xbcT_r], writes=[cx.BCT_r], partial=True)
            else:
                P.dma("sp", xtok[:, :], cx.XTOK[tok0:tok0 + 128, :], xtok_r.dsem, reads=[cx.XTOK_r], writes=[xtok_r])
                P.dma("sp", btok[:, :], cx.BTOK[tok0:tok0 + 128, :], btok_r.dsem, reads=[cx.BTOK_r], writes=[btok_r])
                P.dma("sp", xbcT[:, 32:48, :], cx.BCT[:, :, tok0:tok0 + 128], xbcT_r.dsem, reads=[cx.BCT_r], writes=[xbcT_r])
                P.dma("sp", yacc[:, :], cx.Y1[tok0:tok0 + 128, :], yacc_r.dsem, reads=[cx.Y1_r], writes=[yacc_r])
                P.dma("sp", zs[:, :], cx.ZS[tok0:tok0 + 128, :], zs_r.dsem, reads=[cx.ZS_r], writes=[zs_r])

            pb, pbr = psS, psS_r
            P.op("pe", lambda h, DAt=DAt: h.matmul(psS[:, 0:64], m_cum[:, :], DAt, start=True, stop=True), reads=[dar, cx.c_r], writes=[pbr])
            P.op("act", lambda h: h.activation(out=ecs[:, :], in_=psS[:, 0:64], func=AF.Exp), reads=[pbr], writes=[ecs_r])
            P.op("pe", lambda h, DAt=DAt: h.matmul(psS[:, 64:128], m_seg[:, :], DAt, start=True, stop=True), reads=[dar, cx.c_r], writes=[pbr])
            P.op("act", lambda h: h.activation(out=wst[:, :], in_=psS[:, 64:128], func=AF.Exp), reads=[pbr], writes=[wst_r])
            P.op("dve", lambda h, DTt=DTt: h.tensor_tensor(out=wst[:, :], in0=wst[:, :], in1=DTt, op=ALU.mult), reads=[wst_r, dtr], writes=[wst_r])
            P.op("pe", lambda h, DAt=DAt: h.matmul(psS[:, 128:192], cx.c_onesf[:, :], DAt, start=True, stop=True), reads=[dar, cx.c_r], writes=[pbr])
            P.op("act", lambda h: h.activation(out=dec[:, :], in_=psS[:, 128:192], func=AF.Exp), reads=[pbr], writes=[dec_r])
            P.op("pool", lambda h: h.tensor_tensor(out=xw[:, :].rearrange("p (h e) -> p h e", h=64), in0=xtok[:, :].rearrange("p (h e) -> p h e", h=64),
                                                  in1=_bc(wst[:, :], [128, 64, 64], 2), op=ALU.mult),
                 reads=[xtok_r, wst_r], writes=[xw_r])
            if sweep == 0:
                P.op("pool", lambda h: h.tensor_tensor(out=yacc[:, :].rearrange("p (h e) -> p h e", h=64), in0=xtok[:, :].rearrange("p (h e) -> p h e", h=64),
                                                      in1=_bc(dsk[:, :], [128, 64, 64], 2), op=ALU.mult),
                     reads=[xtok_r, dsk_r], writes=[yacc_r])
            for g in range(8):
                BTg = xbcT[:, 32 + g, :]
                CTg = xbcT[:, 40 + g, :]
                pCB, pCB_r = psA[0], psA_r[0]
                pSeg = [psA[1], psA[2]]; pSeg_r = [psA_r[1], psA_r[2]]
                pY, pY_r = psA[3], psA_r[3]
                pO, pO_r = psA[4], psA_r[4]
                hs = slice(g * 8, (g + 1) * 8)
                P.op("pe", lambda h, BTg=BTg, CTg=CTg, pCB=pCB: h.matmul(pCB[:, 0:128], BTg, CTg, start=True, stop=True), reads=[xbcT_r], writes=[pCB_r])
                P.op("dve", lambda h, pCB=pCB: h.tensor_tensor(out=CBm[:, :], in0=pCB[:, 0:128], in1=m_cum[:, :], op=ALU.mult), reads=[pCB_r, cx.c_r], writes=[CBm_r])
                P.op("pool", lambda h, DAt=DAt, hs=hs: h.tensor_tensor(out=aU[:, :, :], in0=_bc(DAt[:, hs], [128, 8, 128], 2), in1=_bc(m_cum[:, :], [128, 8, 128], 1), op=ALU.mult),
                     reads=[dar, cx.c_r], writes=[aU_r])
                for hf in range(2):
                    P.op("pe", lambda h, hf=hf, pSeg=pSeg: h.matmul(pSeg[hf][:, :], m_seg[:, :], aU[:, hf * 4:(hf + 1) * 4, :].rearrange("p a b -> p (a b)"), start=True, stop=True),
                         reads=[aU_r, cx.c_r], writes=[pSeg_r[hf]])
                    P.op("act", lambda h, hf=hf, pSeg=pSeg: h.activation(out=E[:, hf * 4:(hf + 1) * 4, :].rearrange("p a b -> p (a b)"), in_=pSeg[hf][:, :], func=AF.Exp),
                         reads=[pSeg_r[hf]], writes=[E_r], partial=(hf > 0))
                P.op("dve", lambda h, DTt=DTt, hs=hs: h.tensor_tensor(out=E[:, :, :], in0=E[:, :, :], in1=_bc(DTt[:, hs], [128, 8, 128], 2), op=ALU.mult),
                     reads=[E_r, dtr], writes=[E_r])
                P.op("dve", lambda h: h.tensor_tensor(out=MT[:, :, :], in0=E[:, :, :], in1=_bc(CBm[:, :], [128, 8, 128], 1), op=ALU.mult),
                     reads=[E_r, CBm_r], writes=[MT_r])
                for j in range(8):
                    hh = g * 8 + j
                    P.op("pe", lambda h, j=j, hh=hh, pY=pY: h.matmul(pY[:, j * 64:(j + 1) * 64], MT[:, j, :], xtok[:, hh * 64:(hh + 1) * 64], start=True, stop=True),
                         reads=[MT_r, xtok_r], writes=[pY_r], sig=(j == 7), partial=(j > 0))
                P.op("pe", lambda h, CTg=CTg, g=g, pO=pO: h.matmul(pO[:, :], CTg, Sb[:, g * 512:(g + 1) * 512], start=True, stop=True),
                     reads=[xbcT_r, Sb_r], writes=[pO_r])
                P.op("dve", lambda h, pO=pO, hs=hs: h.tensor_tensor(out=yos[:, :].rearrange("p (h e) -> p h e", h=8), in0=pO[:, :].rearrange("p (h e) -> p h e", h=8),
                                                                in1=_bc(ecs[:, hs], [128, 8, 64], 2), op=ALU.mult),
                     reads=[pO_r, ecs_r], writes=[yos_r])
                P.op("dve", lambda h, pY=pY: h.tensor_tensor(out=yos[:, :], in0=pY[:, :], in1=yos[:, :], op=ALU.add), reads=[pY_r, yos_r], writes=[yos_r])
                P.op("pool", lambda h, g=g: h.tensor_tensor(out=yacc[:, g * 512:(g + 1) * 512], in0=yacc[:, g * 512:(g + 1) * 512], in1=yos[:, :], op=ALU.add),
                     reads=[yos_r, yacc_r], writes=[yacc_r])
                P.op("pe", lambda h, g=g, pO=pO: h.matmul(pO[:, :], btok[:, g * 128:(g + 1) * 128], xw[:, g * 512:(g + 1) * 512], start=True, stop=True),
                     reads=[btok_r, xw_r], writes=[pO_r])
                P.op("pool", lambda h, g=g, hs=hs: h.tensor_tensor(out=S[:, g * 512:(g + 1) * 512].rearrange("p (h e) -> p h e", h=8),
                                                                 in0=S[:, g * 512:(g + 1) * 512].rearrange("p (h e) -> p h e", h=8),
                                                                 in1=_bc(dec[:, hs], [128, 8, 64], 2), op=ALU.mult),
                     reads=[S_r, dec_r, Sb_r], writes=[S_r])
                P.op("dve", lambda h, g=g, pO=pO: h.tensor_tensor(out=S[:, g * 512:(g + 1) * 512], in0=pO[:, :], in1=S[:, g * 512:(g + 1) * 512], op=ALU.add),
                     reads=[pO_r, S_r], writes=[S_r])
                P.op("act", lambda h, g=g: h.copy(out=Sb[:, g * 512:(g + 1) * 512], in_=S[:, g * 512:(g + 1) * 512]), reads=[S_r], writes=[Sb_r])
            if sweep == 0:
                P.dma("sp", cx.Y1[tok0:tok0 + 128, :], yacc[:, :], yacc_r.dsem, reads=[yacc_r], writes=[cx.Y1_r], partial=True)
            else:
                P.op("dve", lambda h: h.tensor_tensor(out=yacc[:, :], in0=yacc[:, :], in1=zs[:, :], op=ALU.mult), reads=[yacc_r, zs_r], writes=[yacc_r])
                for g in range(8):
                    P.op("act", lambda h, g=g: h.activation(out=ynb[:, g * 512:(g + 1) * 512], in_=yacc[:, g * 512:(g + 1) * 512], func=AF.Square, accum_out=sm[:, g:g + 1]),
                         reads=[yacc_r], writes=[ynb_r, sm_r])
                P.op("act", lambda h: h.activation(out=sm[:, 8:16], in_=sm[:, 0:8], func=AF.Sqrt, bias=EPS, scale=1.0 / 512), reads=[sm_r], writes=[sm_r])
                P.op("dve", lambda h: h.reciprocal(out=sm[:, 16:24], in_=sm[:, 8:16]), reads=[sm_r], writes=[sm_r])
                P.op("dve", lambda h: h.tensor_tensor(out=yacc[:, :].rearrange("p (g e) -> p g e", g=8), in0=yacc[:, :].rearrange("p (g e) -> p g e", g=8),
                                                     in1=_bc(sm[:, 16:24], [128, 8, 512], 2), op=ALU.mult), reads=[yacc_r, sm_r], writes=[yacc_r])
                P.op("dve", lambda h: h.tensor_tensor(out=ynb[:, :], in0=yacc[:, :], in1=gs[:, :], op=ALU.mult), reads=[yacc_r, gs_r], writes=[ynb_r])
                for q in range(4):
                    pb, pbr = psT[q % 2], psT_r[q % 2]
                    for k8 in range(8):
                        cc = q * 8 + k8
                        P.op("pe", lambda h, pb=pb, k8=k8, cc=cc: h.transpose(pb[:, k8 * 128:(k8 + 1) * 128], ynb[:, cc * 128:(cc + 1) * 128], ident[:, :]),
                             reads=[ynb_r, cx.c_r], writes=[pbr], sig=(k8 == 7), partial=(k8 > 0))
                    P.op("act" if q % 2 else "dve", (lambda h, pb=pb, q=q: h.copy(out=ynT[:, q * 8:(q + 1) * 8, :], in_=pb[:, :].rearrange("p (k n) -> p k n", k=8))) if q % 2 else
                         (lambda h, pb=pb, q=q: h.tensor_copy(out=ynT[:, q * 8:(q + 1) * 8, :], in_=pb[:, :].rearrange("p (k n) -> p k n", k=8))),
                         reads=[pbr], writes=[ynT_r], partial=(q > 0))
                P.dma("sp", cx.YNT[:, :, tok0:tok0 + 128], ynT[:, :, :], ynT_r.dsem, reads=[ynT_r], writes=[cx.YNT_r], partial=True)
        _flush(P)


def _blocks_n(NT, n):
    out = []
    t = 0
    while t < NT:
        k = min(n, NT - t)
        out.append((t, k))
        t += k
    return out


SLOPES = [2.0 ** (-8.0 * (h + 1) / 16.0) for h in range(16)]


def phase_C(cx):
    nc, P, NT = cx.nc, cx.P, cx.NT
    T = NT * 128
    with ExitStack() as st:
        sb = lambda n, s, dt_: st.enter_context(nc.sbuf_tensor(n, s, dt_))
        R = lambda n, **kw: P.res("c_" + n, **kw)
        KTh = sb("cc_KT", [128, T], BF16); KTh_r = R("KT", dma=True)
        QTh = sb("cc_QT", [128, T], BF16); QTh_r = R("QT", dma=True)
        Vh = sb("cc_V", [128, NT, 128], BF16); Vh_r = R("V", dma=True)
        absd = sb("cc_absd", [128, 2 * T + 512], F32); absd_r = R("absd", dma=True)
        padb = sb("cc_padb", [128, 1], F32); padb_r = R("padb", dma=True)
        lam4 = sb("c_lam4", [128, 4, 64], F32); lam4_r = R("lam4", dma=True)
        lsm = sb("c_lsm", [128, 8], F32); lsm_r = R("lsm")
        gsub = sb("c_gsub", [128, 1], F32); gsub_r = R("gsub", dma=True)
        tmpS = [sb("c_tmp%d" % i, [128, 512], F32) for i in range(2)]; tmpS_r = [R("tmp") for i in range(2)]
        tring = Ring(list(zip(tmpS, tmpS_r)))
        PT = [sb("c_PT%d" % i, [128, 512], BF16) for i in range(3)]; PT_r = [R("PT") for i in range(3)]
        pring = Ring(list(zip(PT, PT_r)))
        r0 = sb("c_r0", [128, 512], F32); r0_r = R("r0")
        r1 = sb("c_r1", [128, 512], F32); r1_r = R("r1")
        o0 = sb("c_o0", [128, 512], F32); o0_r = R("o0")
        o1 = sb("c_o1", [128, 512], F32); o1_r = R("o1")
        sqb = sb("c_sqb", [128, 512], BF16); sqb_r = R("sqb")
        ya = [sb("c_ya%d" % i, [128, 512], BF16) for i in range(2)]; ya_r = [R("ya", dma=True) for i in range(2)]
        yring = Ring(list(zip(ya, ya_r)))
        psS = [st.enter_context(nc.psum_tensor("c_psS%d" % i, [128, 512], F32)) for i in range(2)]
        psS_r = [R("psS", excl=True) for i in range(2)]
        sring = Ring(list(zip(psS, psS_r)))
        psO = [st.enter_context(nc.psum_tensor("c_psO%d" % i, [128, 512], F32)) for i in range(2)]
        psO_r = [R("psO", excl=True) for i in range(2)]
        psR = [st.enter_context(nc.psum_tensor("c_psR%d" % i, [128, 512], F32)) for i in range(2)]
        psR_r = [R("psR", excl=True) for i in range(2)]
        ones = cx.c_ones
        _barrier(P)
        P.dma("sp", absd[:, :], cx.cdr["absd"][:, :], absd_r.dsem, writes=[absd_r])
        P.dma("sp", padb[:, :], cx.cdr["padbias"][:, :], padb_r.dsem, writes=[padb_r])
        P.dma("sp", gsub[:, :], cx.gsub[:, :], gsub_r.dsem, writes=[gsub_r])
        P.op("dve", lambda h: h.tensor_scalar(out=gsub[:, :], in0=gsub[:, :], scalar1=0.8, scalar2=None, op0=ALU.mult), reads=[gsub_r], writes=[gsub_r])
        P.dma("sp", lam4[:, :, :].rearrange("p a b -> p (a b)"), cx.lam4[0:1, :].partition_broadcast(128), lam4_r.dsem, writes=[lam4_r])
        P.op("dve", lambda h: h.tensor_tensor(out=lam4[:, 0, :], in0=lam4[:, 0, :], in1=lam4[:, 1, :], op=ALU.mult), reads=[lam4_r], writes=[lam4_r])
        P.op("dve", lambda h: h.tensor_tensor(out=lam4[:, 2, :], in0=lam4[:, 2, :], in1=lam4[:, 3, :], op=ALU.mult), reads=[lam4_r], writes=[lam4_r])
        P.op("dve", lambda h: h.tensor_reduce(out=lsm[:, 0:1], in_=lam4[:, 0, :], axis=AX.X, op=ALU.add), reads=[lam4_r], writes=[lsm_r])
        P.op("dve", lambda h: h.tensor_reduce(out=lsm[:, 1:2], in_=lam4[:, 2, :], axis=AX.X, op=ALU.add), reads=[lam4_r, lsm_r], writes=[lsm_r])
        P.op("act", lambda h: h.activation(out=lsm[:, 2:4], in_=lsm[:, 0:2], func=AF.Exp), reads=[lsm_r], writes=[lsm_r])
        P.op("dve", lambda h: h.tensor_tensor(out=lsm[:, 4:5], in0=lsm[:, 3:4], in1=lsm[:, 2:3], op=ALU.subtract), reads=[lsm_r], writes=[lsm_r])
        P.op("dve", lambda h: h.tensor_scalar(out=lsm[:, 4:5], in0=lsm[:, 4:5], scalar1=-0.2, scalar2=None, op0=ALU.add), reads=[lsm_r], writes=[lsm_r])
        nlam = lsm[:, 4:5]
        for hd in range(16):
            m_h = SLOPES[hd]
            P.dma("sp", KTh[:, :], cx.KT[:, hd, :], KTh_r.dsem, reads=[cx.KT_r], writes=[KTh_r])
            P.dma("sp", QTh[:, :], cx.QT[:, hd, :], QTh_r.dsem, reads=[cx.QT_r], writes=[QTh_r])
            P.dma("sp", Vh[:, :, :], cx.V[:, hd * 128:(hd + 1) * 128].rearrange("(t p) c -> p t c", p=128), Vh_r.dsem, reads=[cx.V_r], writes=[Vh_r])
            for (t0, nt) in _blocks_n(NT, 4):
                W = nt * 128
                q0 = t0 * 128
                for kt in range(NT):
                    u0 = T + 128 * (t0 - kt)
                    for i in range(2):
                        pS, pS_r = sring.next()
                        tm, tm_r = tring.next()
                        pt, pt_r = pring.next()
                        P.op("pe", lambda h, W=W, pS=pS, i=i, kt=kt, q0=q0: h.matmul(pS[:, 0:W], KTh[i * 64:(i + 1) * 64, kt * 128:(kt + 1) * 128],
                                                                               QTh[i * 64:(i + 1) * 64, q0:q0 + W], start=True, stop=True),
                             reads=[KTh_r, QTh_r], writes=[pS_r])
                        P.op("dve", lambda h, W=W, pS=pS, tm=tm, u0=u0, m_h=m_h: h.scalar_tensor_tensor(out=tm[:, 0:W], in0=absd[:, u0:u0 + W], scalar=-m_h, in1=pS[:, 0:W],
                                                                                                  op0=ALU.mult, op1=ALU.add),
                             reads=[pS_r, absd_r], writes=[tm_r])
                        if kt == 0:
                            P.op("act", lambda h, W=W, pt=pt, tm=tm: h.activation(out=pt[:, 0:W], in_=tm[:, 0:W], func=AF.Exp, bias=padb[:, 0:1], scale=1.0),
                                 reads=[tm_r, padb_r], writes=[pt_r])
                        else:
                            P.op("act", lambda h, W=W, pt=pt, tm=tm: h.activation(out=pt[:, 0:W], in_=tm[:, 0:W], func=AF.Exp), reads=[tm_r], writes=[pt_r])
                        P.op("pe", lambda h, W=W, pt=pt, i=i, kt=kt: h.matmul(psO[i][:, 0:W], Vh[:, kt, :], pt[:, 0:W], start=(kt == 0), stop=(kt == NT - 1)),
                             reads=[Vh_r, pt_r], writes=[psO_r[i]], sig=False, partial=(kt > 0))
                        P.op("pe", lambda h, W=W, pt=pt, i=i, kt=kt: h.matmul(psR[i][:, 0:W], ones[:, :], pt[:, 0:W], start=(kt == 0), stop=(kt == NT - 1)),
                             reads=[cx.c_r, pt_r], writes=[psR_r[i]], partial=(kt > 0))
                P.op("dve", lambda h, W=W: h.reciprocal(out=r0[:, 0:W], in_=psR[0][:, 0:W]), reads=[psR_r[0]], writes=[r0_r])
                P.op("dve", lambda h, W=W: h.reciprocal(out=r1[:, 0:W], in_=psR[1][:, 0:W]), reads=[psR_r[1]], writes=[r1_r])
                P.op("dve", lambda h, W=W: h.tensor_tensor(out=o0[:, 0:W], in0=psO[0][:, 0:W], in1=r0[:, 0:W], op=ALU.mult), reads=[psO_r[0], r0_r], writes=[o0_r])
                P.op("dve", lambda h, W=W: h.tensor_tensor(out=o1[:, 0:W], in0=psO[1][:, 0:W], in1=r1[:, 0:W], op=ALU.mult), reads=[psO_r[1], r1_r], writes=[o1_r])
                P.op("dve", lambda h, W=W: h.scalar_tensor_tensor(out=o0[:, 0:W], in0=o1[:, 0:W], scalar=nlam, in1=o0[:, 0:W], op0=ALU.mult, op1=ALU.add),
                     reads=[o0_r, o1_r, lsm_r], writes=[o0_r])
                P.op("act", lambda h, W=W: h.activation(out=sqb[:, 0:W], in_=o0[:, 0:W], func=AF.Square), reads=[o0_r], writes=[sqb_r])
                P.op("pe", lambda h, W=W: h.matmul(psR[0][:, 0:W], ones[:, :], sqb[:, 0:W], start=True, stop=True), reads=[cx.c_r, sqb_r], writes=[psR_r[0]])
                P.op("act", lambda h, W=W: h.activation(out=r0[:, 0:W], in_=psR[0][:, 0:W], func=AF.Sqrt, bias=EPS, scale=1.0 / 128), reads=[psR_r[0]], writes=[r0_r])
                P.op("dve", lambda h, W=W: h.reciprocal(out=r0[:, 0:W], in_=r0[:, 0:W]), reads=[r0_r], writes=[r0_r])
                y_, y_r = yring.next()
                P.op("dve", lambda h, W=W, y_=y_: h.scalar_tensor_tensor(out=y_[:, 0:W], in0=o0[:, 0:W], scalar=gsub[:, 0:1], in1=r0[:, 0:W], op0=ALU.mult, op1=ALU.mult),
                     reads=[o0_r, r0_r, gsub_r], writes=[y_r])
                P.dma("sp", cx.YAT[:, hd, q0:q0 + W], y_[:, 0:W], y_r.dsem, reads=[y_r], writes=[cx.YAT_r], partial=True)
        _flush(P)


def _norm_transpose(cx, P, src_rows_ap, src_res, xt, xt_r, sm, sm_r, gcol, gcol_r, ps, ps_r, hT, hT_r, i, identf):
    P.dma("sp", xt[:, :], src_rows_ap, xt_r.dsem, reads=[src_res] if src_res is not None else [], writes=[xt_r])
    P.op("act", lambda h: h.activation(out=cx.junk, in_=xt[:, :], func=AF.Square, accum_out=sm[:, 0:1]), reads=[xt_r], writes=[cx.junk_r, sm_r])
    P.op("act", lambda h: h.activation(out=sm[:, 1:2], in_=sm[:, 0:1], func=AF.Sqrt, bias=EPS, scale=1.0 / D), reads=[sm_r], writes=[sm_r])
    P.op("dve", lambda h: h.reciprocal(out=sm[:, 2:3], in_=sm[:, 1:2]), reads=[sm_r], writes=[sm_r])
    P.op("dve", lambda h: h.tensor_scalar(out=xt[:, :], in0=xt[:, :], scalar1=sm[:, 2:3], scalar2=None, op0=ALU.mult), reads=[xt_r, sm_r], writes=[xt_r])
    for q4 in range(8):
        pb, pbr = ps[q4 % 2], ps_r[q4 % 2]
        for k4 in range(4):
            kc = q4 * 4 + k4
            P.op("pe", lambda h, pb=pb, k4=k4, kc=kc: h.transpose(pb[:, k4 * 128:(k4 + 1) * 128], xt[:, kc * 128:(kc + 1) * 128], identf[:, :]),
                 reads=[xt_r, cx.c_r], writes=[pbr], sig=(k4 == 3), partial=(k4 > 0))
        P.op("dve", lambda h, pb=pb, q4=q4, i=i: h.tensor_tensor(
            out=hT[:, q4 * 4:(q4 + 1) * 4, i * 128:(i + 1) * 128],
            in0=pb[:, :].rearrange("p (k n) -> p k n", k=4),
            in1=gcol[:, q4 * 4:(q4 + 1) * 4].unsqueeze(2).to_broadcast([128, 4, 128]), op=ALU.mult),
            reads=[pbr, gcol_r], writes=[hT_r], partial=True)


def phase_D1(cx):
    nc, P, NT = cx.nc, cx.P, cx.NT
    with ExitStack() as st:
        sb = lambda n, s, dt_: st.enter_context(nc.sbuf_tensor(n, s, dt_))
        R = lambda n, **kw: P.res("d_" + n, **kw)
        WB = 512
        hnT = sb("d_hnT", [128, NKC, WB], BF16); hnT_r = R("hnT", dma=True)
        ynT = sb("d_ynT", [128, NKC, WB], BF16); ynT_r = R("ynT", dma=True)
        yaT = sb("d_yaT", [128, 16, WB], BF16); yaT_r = R("yaT", dma=True)
        mxT = sb("d_mxT", [128, NKC, WB], BF16); mxT_r = R("mxT")
        G1 = sb("d_G1", [128, 4, WB], F32); G1_r = R("G1")
        G2 = sb("d_G2", [128, 4, WB], F32); G2_r = R("G2")
        M = sb("d_M", [128, 4, WB], F32); M_r = R("M")
        bg = sb("d_bg", [128, 64], F32); bg_r = R("bg", dma=True)
        wt = [sb("d_w%d" % i, [128, 16, 512], BF16) for i in range(2)]; wt_r = [R("w", dma=True) for i in range(2)]
        wring = Ring(list(zip(wt, wt_r)))
        xs_ = [sb("d_xs%d" % i, [128, 512], F32) for i in range(3)]; xs_r = [R("xs", dma=True) for i in range(3)]
        xring = Ring(list(zip(xs_, xs_r)))
        ps = [st.enter_context(nc.psum_tensor("d_ps%d" % i, [128, 512], F32)) for i in range(8)]
        ps_r = [R("ps", excl=True) for i in range(8)]
        _barrier(P)
        P.dma("sp", bg[:, :], cx.bgcol[:, :], bg_r.dsem, writes=[bg_r])
        wbg = cx.w_bg.rearrange("(kc p) n -> p kc n", p=128)
        wso = cx.w_so.rearrange("(kc p) n -> p kc n", p=128)
        wao = cx.w_ao.rearrange("(kc p) n -> p kc n", p=128)
        wo = cx.w_o.rearrange("(kc p) n -> p kc n", p=128)
        grp = [0]

        def load_w(src, kh, c0, nk=16):
            w, wr = wring.next()
            P.dma("pool", w[:, 0:nk, :], src[:, kh * 16:kh * 16 + nk, c0:c0 + 512], wr.dsem, writes=[wr])
            return w, wr

        for (t0, nt) in _blocks_n(NT, 4):
            W = nt * 128
            tok0 = t0 * 128
            P.dma("sp", hnT[:, :, 0:W], cx.HNT[:, :, tok0:tok0 + W], hnT_r.dsem, reads=[cx.HNT_r], writes=[hnT_r])
            P.dma("sp", ynT[:, :, 0:W], cx.YNT[:, :, tok0:tok0 + W], ynT_r.dsem, reads=[cx.YNT_r], writes=[ynT_r])
            P.dma("sp", yaT[:, :, 0:W], cx.YAT[:, :, tok0:tok0 + W], yaT_r.dsem, reads=[cx.YAT_r], writes=[yaT_r])

            def proj_b(src, c0, act_, act_r, nkh, kind, cg):
                base = 4 * (grp[0] % 2)
                grp[0] += 1
                nkc = nkh * 16
                for kh in range(nkh):
                    w, wr = load_w(src, kh, c0)
                    for j in range(4):
                        pb, pbr = ps[base + j], ps_r[base + j]
                        for k in range(16):
                            kc = kh * 16 + k
                            P.op("pe", lambda h, W=W, pb=pb, w=w, k=k, j=j, kc=kc, act_=act_, nkc=nkc: h.matmul(pb[:, 0:W], w[:, k, j * 128:(j + 1) * 128], act_[:, kc, 0:W],
                                                                                                      start=(kc == 0), stop=(kc == nkc - 1)),
                                 reads=[act_r, wr], writes=[pbr], sig=(k == 15), partial=(kc > 0))
                for j in range(4):
                    pb, pbr = ps[base + j], ps_r[base + j]
                    ch = cg * 4 + j
                    if kind == "g1":
                        P.op("act", lambda h, W=W, pb=pb, j=j, ch=ch: h.activation(out=G1[:, j, 0:W], in_=pb[:, 0:W], func=AF.Sigmoid, bias=bg[:, ch:ch + 1], scale=1.0),
                             reads=[pbr, bg_r], writes=[G1_r], partial=(j > 0))
                    elif kind == "g2":
                        P.op("act", lambda h, W=W, pb=pb, j=j, ch=ch: h.activation(out=G2[:, j, 0:W], in_=pb[:, 0:W], func=AF.Sigmoid, bias=bg[:, 32 + ch:33 + ch], scale=1.0),
                             reads=[pbr, bg_r], writes=[G2_r], partial=(j > 0))
                    elif kind == "s":
                        P.op("dve", lambda h, W=W, pb=pb, j=j: h.tensor_tensor(out=M[:, j, 0:W], in0=pb[:, 0:W], in1=G1[:, j, 0:W], op=ALU.mult),
                             reads=[pbr, G1_r], writes=[M_r], partial=(j > 0))
                    else:
                        P.op("dve", lambda h, W=W, pb=pb, j=j: h.tensor_tensor(out=G2[:, j, 0:W], in0=pb[:, 0:W], in1=G2[:, j, 0:W], op=ALU.mult),
                             reads=[pbr, G2_r], writes=[G2_r], partial=(j > 0))
                        P.op("pool", lambda h, W=W, j=j, ch=ch: h.tensor_tensor(out=mxT[:, ch, 0:W], in0=M[:, j, 0:W], in1=G2[:, j, 0:W], op=ALU.add),
                             reads=[M_r, G2_r], writes=[mxT_r], partial=True)

            for cg in range(8):
                proj_b(wbg, cg * 512, hnT, hnT_r, 2, "g1", cg)
                proj_b(wbg, D + cg * 512, hnT, hnT_r, 2, "g2", cg)
                proj_b(wso, cg * 512, ynT, ynT_r, 2, "s", cg)
                proj_b(wao, cg * 512, yaT, yaT_r, 1, "a", cg)
            for cg in range(8):
                base = 4 * (grp[0] % 2)
                grp[0] += 1
                for kh in range(2):
                    w, wr = load_w(wo, kh, cg * 512)
                    for i in range(nt):
                        pb, pbr = ps[base + i], ps_r[base + i]
                        for k in range(16):
                            kc = kh * 16 + k
                            P.op("pe", lambda h, pb=pb, w=w, k=k, i=i, kc=kc: h.matmul(pb[:, :], mxT[:, kc, i * 128:(i + 1) * 128], w[:, k, :],
                                                                                 start=(kc == 0), stop=(kc == NKC - 1)),
                                 reads=[mxT_r, wr], writes=[pbr], sig=(k == 15), partial=(kc > 0))
                for i in range(nt):
                    pb, pbr = ps[base + i], ps_r[base + i]
                    r0_ = tok0 + i * 128
                    x_, x_r = xring.next()
                    P.dma("sp", x_[:, :], cx.xin[r0_:r0_ + 128, cg * 512:(cg + 1) * 512], x_r.dsem, writes=[x_r])
                    P.op("dve", lambda h, pb=pb, x_=x_: h.tensor_tensor(out=x_[:, :], in0=pb[:, :], in1=x_[:, :], op=ALU.add), reads=[pbr, x_r], writes=[x_r])
                    P.dma("sp", cx.H1[r0_:r0_ + 128, cg * 512:(cg + 1) * 512], x_[:, :], x_r.dsem, reads=[x_r], writes=[cx.H1_r], partial=True)
        _flush(P)


def phase_D2(cx):
    nc, P, NT = cx.nc, cx.P, cx.NT
    NFC = D_FF // 128
    with ExitStack() as st:
        sb = lambda n, s, dt_: st.enter_context(nc.sbuf_tensor(n, s, dt_))
        R = lambda n, **kw: P.res("f_" + n, **kw)
        WB = 512
        xt = sb("f_xt", [128, D], F32); xt_r = R("xt", dma=True)
        hfT = sb("f_hfT", [128, NKC, WB], BF16); hfT_r = R("hfT")
        acT = sb("f_acT", [128, NFC, WB], BF16); acT_r = R("acT")
        cx.junk = acT[:, 0:8, :].rearrange("p a b -> p (a b)"); cx.junk_r = acT_r
        Gs = sb("f_Gs", [128, 4, WB], F32); Gs_r = R("Gs")
        gcol = sb("f_gcol", [128, NKC], F32); gcol_r = R("gcol", dma=True)
        sm = sb("f_sm", [128, 8], F32); sm_r = R("sm")
        wt = [sb("f_w%d" % i, [128, 16, 512], BF16) for i in range(2)]; wt_r = [R("w", dma=True) for i in range(2)]
        wring = Ring(list(zip(wt, wt_r)))
        xs_ = [sb("f_xs%d" % i, [128, 512], F32) for i in range(2)]; xs_r = [R("xs", dma=True) for i in range(2)]
        xring = Ring(list(zip(xs_, xs_r)))
        ps = [st.enter_context(nc.psum_tensor("f_ps%d" % i, [128, 512], F32)) for i in range(8)]
        ps_r = [R("ps", excl=True) for i in range(8)]
        _barrier(P)
        P.dma("sp", gcol[:, :], cx.gcol_ffn[:, :], gcol_r.dsem, writes=[gcol_r])
        wgu = cx.w_gu.rearrange("(kc p) n -> p kc n", p=128)
        wdn = cx.w_dn.rearrange("(kc p) n -> p kc n", p=128)
        grp = [0]
        for (t0, nt) in _blocks_n(NT, 4):
            W = nt * 128
            tok0 = t0 * 128
            for i in range(nt):
                r0_ = tok0 + i * 128
                _norm_transpose(cx, P, cx.H1[r0_:r0_ + 128, :], cx.H1_r, xt, xt_r, sm, sm_r, gcol, gcol_r, ps, ps_r, hfT, hfT_r, i, cx.c_identf)
            f0 = 0
            while f0 < NFC:
                nch = min(4, NFC - f0)
                for which in (0, 1):
                    base = 4 * (grp[0] % 2)
                    grp[0] += 1
                    c0 = which * D_FF + f0 * 128
                    for kh in range(2):
                        w, wr = wring.next()
                        P.dma("pool", w[:, :, 0:nch * 128], wgu[:, kh * 16:(kh + 1) * 16, c0:c0 + nch * 128], wr.dsem, writes=[wr])
                        for j in range(nch):
                            pb, pbr = ps[base + j], ps_r[base + j]
                            for k in range(16):
                                kc = kh * 16 + k
                                P.op("pe", lambda h, W=W, pb=pb, w=w, k=k, j=j, kc=kc: h.matmul(pb[:, 0:W], w[:, k, j * 128:(j + 1) * 128], hfT[:, kc, 0:W],
                                                                                     start=(kc == 0), stop=(kc == NKC - 1)),
                                     reads=[hfT_r, wr], writes=[pbr], sig=(k == 15), partial=(kc > 0))
                    for j in range(nch):
                        pb, pbr = ps[base + j], ps_r[base + j]
                        if which == 0:
                            P.op("act", lambda h, W=W, pb=pb, j=j: h.activation(out=Gs[:, j, 0:W], in_=pb[:, 0:W], func=AF.Silu), reads=[pbr], writes=[Gs_r], partial=(j > 0))
                        else:
                            P.op("dve", lambda h, W=W, pb=pb, j=j, f0=f0: h.tensor_tensor(out=acT[:, f0 + j, 0:W], in0=pb[:, 0:W], in1=Gs[:, j, 0:W], op=ALU.mult),
                                 reads=[pbr, Gs_r], writes=[acT_r], partial=True)
                f0 += nch
            for cg in range(8):
                base = 4 * (grp[0] % 2)
                grp[0] += 1
                fu = 0
                while fu < NFC:
                    nk = min(16, NFC - fu)
                    w, wr = wring.next()
                    P.dma("pool", w[:, 0:nk, :], wdn[:, fu:fu + nk, cg * 512:(cg + 1) * 512], wr.dsem, writes=[wr])
                    for i in range(nt):
                        pb, pbr = ps[base + i], ps_r[base + i]
                        for k in range(nk):
                            fc = fu + k
                            P.op("pe", lambda h, pb=pb, w=w, k=k, i=i, fc=fc: h.matmul(pb[:, :], acT[:, fc, i * 128:(i + 1) * 128], w[:, k, :],
                                                                                 start=(fc == 0), stop=(fc == NFC - 1)),
                                 reads=[acT_r, wr], writes=[pbr], sig=(k == nk - 1), partial=(fc > 0))
                    fu += nk
                for i in range(nt):
                    t = t0 + i
                    pb, pbr = ps[base + i], ps_r[base + i]
                    if t == 0:
                        continue
                    r0_ = t * 128
                    x_, x_r = xring.next()
                    P.dma("sp", x_[:, :], cx.H1[r0_:r0_ + 128, cg * 512:(cg + 1) * 512], x_r.dsem, reads=[cx.H1_r], writes=[x_r])
                    P.op("dve", lambda h, pb=pb, x_=x_: h.tensor_tensor(out=x_[:, :], in0=pb[:, :], in1=x_[:, :], op=ALU.add), reads=[pbr, x_r], writes=[x_r])
                    P.dma("sp", cx.out[r0_ - 128:r0_, cg * 512:(cg + 1) * 512], x_[:, :], x_r.dsem, reads=[x_r], writes=[cx.out_r], partial=True)
        _flush(P)


def build(NT, phases="A", dbg=()):
    T = NT * 128
    nc = bass.Bass("TRN2", target_bir_lowering=False)
    cx = Cx()
    cx.nc, cx.NT = nc, NT
    kin = "ExternalInput"

    def din(name, shape, dt=F32):
        return nc.dram_tensor(name, list(shape), dt, kind=kin).ap()

    def dscr(name, shape, dt):
        kind = "ExternalOutput" if name in dbg else "Internal"
        return nc.dram_tensor(name, list(shape), dt, kind=kind).ap()

    cx.xin = din("xin", [T, D])
    cx.w_in = din("w_in", [D, D_IN])
    cx.gcol_mix = din("gcol_mix", [128, NKC])
    cx.dt_bias = din("dt_bias", [1, 128])
    cx.a_log = din("a_log", [1, 128])
    cx.gqk = din("gqk", [128, 2])
    cx.convw = din("convw", [128, 48, 5])
    cx.convb = din("convb", [128, 48])
    cx.d_skip = din("d_skip", [1, 64])
    cx.ssd_norm_g = din("ssd_norm_g", [1, D])
    cx.gsub = din("gsub", [128, 1])
    cx.lam4 = din("lam4", [1, 256])
    cx.bgcol = din("bgcol", [128, 64])
    cx.gcol_ffn = din("gcol_ffn", [128, NKC])
    cx.w_bg = din("w_bg", [D, 2 * D])
    cx.w_so = din("w_so", [D, D])
    cx.w_ao = din("w_ao", [2048, D])
    cx.w_o = din("w_o", [D, D])
    cx.w_gu = din("w_gu", [D, 2 * D_FF])
    cx.w_dn = din("w_dn", [D_FF, D])
    consts = make_consts(NT)
    cdr = {}
    for k, v in consts.items():
        cdr[k] = din("c_" + k, v.shape, BF16 if v.dtype == ml_dtypes.bfloat16 else F32)
    cx.HNT = dscr("HNT", [128, NKC, T], BF16)
    cx.XBC = dscr("XBC", [128, 48, T + 4], F32)
    cx.QT = dscr("QT", [128, 16, T], BF16)
    cx.KT = dscr("KT", [128, 16, T], BF16)
    cx.V = dscr("V", [T, 2048], BF16)
    cx.ZS = dscr("ZS", [T, D], F32)
    cx.DTD = dscr("DTD", [128, NT, 128], F32)
    cx.XTOK = dscr("XTOK", [T, D], BF16)
    cx.BTOK = dscr("BTOK", [T, 1024], BF16)
    cx.BCT = dscr("BCT", [128, 16, T], BF16)
    cx.Y1 = dscr("Y1", [T, D], F32)
    cx.YNT = dscr("YNT", [128, NKC, T], BF16)
    cx.YAT = dscr("YAT", [128, 16, T], BF16)
    cx.H1 = dscr("H1", [T, D], F32)
    cx.out = nc.dram_tensor("out", [T - 128, D], F32, kind="ExternalOutput").ap()

    with ExitStack() as st:
        P = Prog(nc, st)
        cx.P = P
        for n in ("HNT", "XBC", "QT", "KT", "V", "ZS", "DTD", "XTOK", "BTOK", "BCT", "Y1", "YNT", "YAT", "H1", "out"):
            setattr(cx, n + "_r", P.res(n))
        sb = lambda n, s, d: st.enter_context(nc.sbuf_tensor(n, s, d))
        cx.DT_r = [P.res("DT") for _ in range(NT)]
        cx.DA_r = [P.res("DA") for _ in range(NT)]
        cx.c_r = P.res("consts", dma=True)
        for k in ("ident", "blk64", "ones", "m_kles", "m_kgts", "m_kgel", "m_klts", "onesf"):
            v = consts[k]
            t_ = sb("s_" + k, list(v.shape), BF16 if v.dtype == ml_dtypes.bfloat16 else F32)
            setattr(cx, "c_" + k, t_)
            P.dma("sp", t_[:, :], cdr[k][:, :], cx.c_r.dsem, writes=[cx.c_r], partial=True)
        cx.c_identf = sb("s_identf", [128, 128], F32)
        P.op("dve", lambda h: h.tensor_copy(out=cx.c_identf[:, :], in_=cx.c_ident[:, :]), reads=[cx.c_r], writes=[cx.c_r], partial=True)
        cx.cdr = cdr
        st_ab = ExitStack()
        cx.DT = st_ab.enter_context(nc.sbuf_tensor("DT", [128, NT, 128], F32))
        cx.DA = st_ab.enter_context(nc.sbuf_tensor("DA", [128, NT, 128], F32))
        _flush(P)
        if "A" in phases:
            phase_A(cx)
        if "B" in phases:
            phase_B(cx, 0)
            phase_B(cx, 1)
        if "DTD" not in dbg:
            st_ab.close()
        if "C" in phases:
            phase_C(cx)
        if "D" in phases:
            phase_D1(cx)
            phase_D2(cx)
        if "DTD" in dbg:
            _barrier(P)
            dd = P.new_dsem("dtd")
            P.dma("sp", cx.DTD[:, :, :], cx.DT[:, :, :], dd, reads=cx.DT_r)
        _barrier(P)
        _flush(P, final=True)
    return nc, consts


def host_inputs(NT, seq, meta_tokens, w, consts):
    T = NT * 128
    xin = np.zeros((T, D), np.float32)
    xin[NPAD:128] = meta_tokens
    xin[128:] = seq
    m = {"xin": xin}
    m["w_in"] = w["w_in"][0]
    m["gcol_mix"] = np.ascontiguousarray(w["norm_mix_g"][0].reshape(NKC, 128).T)
    m["dt_bias"] = np.ascontiguousarray(w["dt_bias"][0].reshape(1, 128))
    m["a_log"] = np.ascontiguousarray(w["a_log"][0].reshape(1, 128))
    gqk = np.stack([np.tile(w["q_norm_g"][0], 2), np.tile(w["k_norm_g"][0], 2)], axis=1)
    m["gqk"] = np.ascontiguousarray(gqk.astype(np.float32))
    m["convw"] = np.ascontiguousarray(w["conv_w"][0].T.reshape(48, 128, 5).transpose(1, 0, 2))
    m["convb"] = np.ascontiguousarray(w["conv_b"][0].reshape(48, 128).T)
    m["d_skip"] = np.ascontiguousarray(w["d_skip"][0].reshape(1, 64))
    m["ssd_norm_g"] = np.ascontiguousarray(w["ssd_norm_g"][0].reshape(1, D))
    m["gsub"] = np.ascontiguousarray(w["subln_g"][0].reshape(128, 1))
    m["lam4"] = np.ascontiguousarray(np.concatenate([w["lambda_q1"][0], w["lambda_k1"][0], w["lambda_q2"][0], w["lambda_k2"][0]]).reshape(1, 256))
    m["bgcol"] = np.ascontiguousarray(w["b_branch_gate"][0].reshape(64, 128).T)
    m["gcol_ffn"] = np.ascontiguousarray(w["norm_ffn_g"][0].reshape(NKC, 128).T)
    m["w_bg"] = w["w_branch_gate"][0]
    m["w_so"] = w["w_ssd_out"][0]
    m["w_ao"] = w["w_att_out"][0]
    m["w_o"] = w["w_o"][0]
    m["w_gu"] = w["w_gate_up"][0]
    m["w_dn"] = w["w_down"][0]
    for k, v in consts.items():
        m["c_" + k] = v
    return m


def kernel(**inputs):
    NT = 33
    w = {k: np.asarray(v) for k, v in inputs.items()}
    nc, consts = build(NT, phases="ABCD")
    xp, xs = w["x_prompt"], w["x_sample"]
    seqs = [xp[0], xp[1], xs[0], xs[1], xs[2], xs[3], xp[0], xp[1]]
    meta = w["meta_tokens"]
    in_maps = [host_inputs(NT, s_, meta, w, consts) for s_ in seqs]
    res = run_bass_kernel_spmd(nc, in_maps, core_ids=list(range(8)))
    outs = [np.asarray(r["out"], dtype=np.float32) for r in res.results]
    y_prompt = np.stack(outs[0:2], axis=0)
    y_sample = np.stack(outs[2:6], axis=0)
    return (y_prompt, y_sample)
```

```python
import numpy as np
import ml_dtypes
from contextlib import ExitStack
import concourse.bass as bass
import concourse.mybir as mybir
from concourse.bass_utils import run_bass_kernel_spmd

F32 = mybir.dt.float32
BF16 = mybir.dt.bfloat16
AF = mybir.ActivationFunctionType
ALU = mybir.AluOpType
AX = mybir.AxisListType

D = 4096
NKC = 32
D_IN = 16512
C_XBC = 6144
O_XBC = 4096
O_DT = 10240
O_Q = 10368
O_K = 12416
O_V = 14464
D_FF = 11008
EPS = 1e-6
NPAD = 112


class Res:
    __slots__ = ("name", "lw", "rd", "dsem", "excl", "base")

    def __init__(self, name, dsem=None, excl=False):
        self.name = name
        self.lw = {}
        self.rd = {}
        self.dsem = dsem
        self.excl = excl
        self.base = {}


class DSem:
    def __init__(self, h):
        self.h = h
        self.count = 0


class Eng:
    def __init__(self, name, sem):
        self.name = name
        self.sem = sem
        self.count = 0
        self.waited = {}
        self.ops = []
        self.pending_nosig = False


class Prog:
    def __init__(self, nc, stack):
        self.nc = nc
        self.stack = stack
        self.E = {}
        self.semobj = {}
        for n in ("pe", "act", "dve", "pool", "sp"):
            s = stack.enter_context(nc.semaphore("e_" + n))
            self.E[n] = Eng(n, s)
            self.semobj[id(s)] = ("eng", self.E[n], s)
        self.dsems = []
        self.nres = 0

    def new_dsem(self, name):
        s = self.stack.enter_context(self.nc.semaphore("d_" + name))
        d = DSem(s)
        self.semobj[id(s)] = ("dma", d, s)
        self.dsems.append(d)
        return d

    def res(self, name, dma=False, excl=False):
        self.nres += 1
        return Res(name, self.new_dsem(name + str(self.nres)) if dma else None, excl)

    def _waits(self, eng, reads, writes, partial):
        need = {}

        def add(tok):
            for k, v in tok.items():
                if need.get(k, 0) < v:
                    need[k] = v

        for r in reads:
            add(r.lw)
            if r.excl:
                add(r.rd)
        for w in writes:
            if not partial:
                add(w.lw)
            else:
                add(w.base)
            add(w.rd)
        out = []
        for k, v in need.items():
            kind, obj, h = self.semobj[k]
            if kind == "dma":
                v = obj.count
            else:
                if obj is eng and eng.name == "pe":
                    continue
            if eng.waited.get(k, 0) >= v:
                continue
            eng.waited[k] = v
            out.append((h, v))
        return out

    def _commit(self, key, val, reads, writes, partial):
        for r in reads:
            if r.rd.get(key, 0) < val:
                r.rd[key] = val
        for w in writes:
            if partial:
                if w.lw.get(key, 0) < val:
                    w.lw[key] = val
            else:
                b = dict(w.lw)
                for k2, v2 in w.rd.items():
                    if b.get(k2, 0) < v2:
                        b[k2] = v2
                w.base = b
                w.lw = {key: val}
                w.rd = {}

    def op(self, en, fn, reads=(), writes=(), sig=True, partial=False):
        eng = self.E[en]
        waits = self._waits(eng, reads, writes, partial)
        if sig:
            eng.count += 1
            val = eng.count
            inc = (eng.sem, 1)
            eng.pending_nosig = False
        else:
            val = eng.count + 1
            inc = None
            eng.pending_nosig = True
        eng.ops.append((waits, fn, inc))
        self._commit(id(eng.sem), val, reads, writes, partial)

    def dma(self, qn, out, in_, dsem, reads=(), writes=(), partial=False):
        eng = self.E[qn]
        waits = self._waits(eng, reads, writes, partial)
        dsem.count += 16
        eng.ops.append((waits, lambda h: h.dma_start(out=out, in_=in_), (dsem.h, 16)))
        self._commit(id(dsem.h), dsem.count, reads, writes, partial)

    def emit(self):
        nc = self.nc
        sp = self.E["sp"]
        fin = [(d.h, d.count) for d in self.dsems if d.count > 0]
        for n, e in self.E.items():
            assert not e.pending_nosig, n
        with nc.Block() as block:
            def mk(eng, final):
                def f(h):
                    for waits, fn, inc in eng.ops:
                        for (s, v) in waits:
                            h.wait_ge(s, v)
                        ins = fn(h)
                        if inc is not None:
                            ins.then_inc(inc[0], inc[1])
                    if final:
                        for (s, v) in fin:
                            h.wait_ge(s, v)
                return f
            block.tensor(mk(self.E["pe"], False))
            block.scalar(mk(self.E["act"], False))
            block.vector(mk(self.E["dve"], False))
            block.gpsimd(mk(self.E["pool"], False))
            block.sync(mk(self.E["sp"], True))


class Ring:
    def __init__(self, items):
        self.items = items
        self.i = 0

    def next(self):
        it = self.items[self.i % len(self.items)]
        self.i += 1
        return it


def make_consts(NT):
    T = NT * 128
    i = np.arange(128)[:, None]
    j = np.arange(128)[None, :]
    c = {}
    c["ident"] = (i == j).astype(ml_dtypes.bfloat16)
    c["blk64"] = ((i // 64) == (j // 64)).astype(ml_dtypes.bfloat16)
    c["ones"] = np.ones((128, 128), ml_dtypes.bfloat16)
    u = np.arange(2 * T + 512)[None, :]
    c["absd"] = np.abs(u - i - T).astype(np.float32)
    pb = np.zeros((128, 1), np.float32)
    pb[:NPAD] = -30000.0
    c["padbias"] = pb
    c["m_kles"] = (i <= j).astype(np.float32)
    c["m_kgts"] = (i > j).astype(np.float32)
    c["m_kgel"] = (i >= j).astype(np.float32)
    c["m_klts"] = (i < j).astype(np.float32)
    c["onesf"] = np.ones((128, 128), np.float32)
    return c


class Cx:
    pass


class _Stop(Exception):
    pass


def _chk(cx, n):
    import os
    if int(os.environ.get('K_STOP', '0')) == n:
        raise _Stop()


def _barrier(P):
    toks = []
    for n, e in P.E.items():
        assert not e.pending_nosig
        if e.count > 0:
            toks.append((id(e.sem), e.sem, e.count))
    for d in P.dsems:
        if d.count > 0:
            toks.append((id(d.h), d.h, d.count))
    for n, e in P.E.items():
        waits = []
        for k, h, v in toks:
            if k == id(e.sem):
                continue
            if e.waited.get(k, 0) >= v:
                continue
            e.waited[k] = v
            waits.append((h, v))
        if waits:
            e.ops.append((waits, None, None))


def _flush(P, final=False):
    nc = P.nc
    fin = [(d.h, d.count) for d in P.dsems if d.count > 0] if final else []
    for n, e in P.E.items():
        assert not e.pending_nosig, n
    with nc.Block() as block:
        def mk(eng, fl):
            ops = eng.ops

            def f(h):
                for waits, fn, inc in ops:
                    for (s, v) in waits:
                        h.wait_ge(s, v)
                    if fn is None:
                        continue
                    ins = fn(h)
                    if inc is not None:
                        ins.then_inc(inc[0], inc[1])
                if fl:
                    for (s, v) in fin:
                        h.wait_ge(s, v)
            return f
        block.tensor(mk(P.E["pe"], False))
        block.scalar(mk(P.E["act"], False))
        block.vector(mk(P.E["dve"], False))
        block.gpsimd(mk(P.E["pool"], False))
        block.sync(mk(P.E["sp"], final))
    for e in P.E.values():
        e.ops = []


def _blocks(NT):
    out = []
    t = 0
    while t < NT:
        n = min(4, NT - t)
        out.append((t, n))
        t += n
    return out


def phase_A(cx):
    nc, P, NT = cx.nc, cx.P, cx.NT
    T = NT * 128
    with ExitStack() as st:
        sb = lambda n, s, d: st.enter_context(nc.sbuf_tensor(n, s, d))
        xt = sb("a_xt", [128, D], F32)
        xt_r = P.res("a_xt", dma=True)
        hn = sb("a_hn", [128, D], F32)
        hn_r = P.res("a_hn")
        hnT = sb("a_hnT", [128, NKC, 512], BF16)
        hnT_r = P.res("a_hnT", dma=True)
        gcol = sb("a_gcol", [128, NKC], F32)
        gcol_r = P.res("a_gcol", dma=True)
        NW = 3
        wt = [sb("a_w%d" % i, [128, 16, 512], BF16) for i in range(NW)]
        wt_r = [P.res("a_w", dma=True) for i in range(NW)]
        wring = Ring(list(zip(wt, wt_r)))
        wdt = sb("a_wdt", [128, NKC, 128], BF16)
        wdt_r = P.res("a_wdt", dma=True)
        sm = sb("a_sm", [128, 8], F32)
        sm_r = P.res("a_sm")
        NS = 4
        sf = [sb("a_sf%d" % i, [128, 512], F32) for i in range(NS)]
        sf_r = [P.res("a_sf", dma=True) for i in range(NS)]
        sfring = Ring(list(zip(sf, sf_r)))
        sh = [sb("a_sh%d" % i, [128, 512], BF16) for i in range(NS)]
        sh_r = [P.res("a_sh", dma=True) for i in range(NS)]
        shring = Ring(list(zip(sh, sh_r)))
        sq = [sb("a_sq%d" % i, [128, 512], BF16) for i in range(2)]
        sq_r = [P.res("a_sq") for i in range(2)]
        sqring = Ring(list(zip(sq, sq_r)))
        rq = [sb("a_rq%d" % i, [128, 512], F32) for i in range(2)]
        rq_r = [P.res("a_rq") for i in range(2)]
        rqring = Ring(list(zip(rq, rq_r)))
        dtb = sb("a_dtb", [128, 128], F32)
        dtb_r = P.res("a_dtb", dma=True)
        Ab = sb("a_Ab", [128, 128], F32)
        Ab_r = P.res("a_Ab", dma=True)
        gq = sb("a_gq", [128, 2], F32)
        gq_r = P.res("a_gq", dma=True)
        zt = sb("a_zero", [128, 48, 2], F32)
        zt_r = P.res("a_zero", dma=True)
        ps = [st.enter_context(nc.psum_tensor("a_ps%d" % i, [128, 512], F32)) for i in range(8)]
        ps_r = [P.res("a_ps", excl=True) for i in range(8)]

        _barrier(P)
        try:
          _phase_A_body(cx, locals())
        except _Stop:
          pass
        _flush(P)


def _phase_A_body(cx, L):
        globals().update({})
        nc, P, NT = cx.nc, cx.P, cx.NT
        T = NT * 128
        for _k, _v in L.items():
            pass
        xt, xt_r, hn, hn_r, hnT, hnT_r, gcol, gcol_r, wring, wdt, wdt_r, sm, sm_r, sfring, shring, sqring, rqring, dtb, dtb_r, Ab, Ab_r, gq, gq_r, zt, zt_r, ps, ps_r = [L[k] for k in 'xt xt_r hn hn_r hnT hnT_r gcol gcol_r wring wdt wdt_r sm sm_r sfring shring sqring rqring dtb dtb_r Ab Ab_r gq gq_r zt zt_r ps ps_r'.split()]
        P.dma("sp", gcol[:, :], cx.gcol_mix[:, :], gcol_r.dsem, writes=[gcol_r])
        P.dma("sp", dtb[:, :], cx.dt_bias[0:1, :].partition_broadcast(128), dtb_r.dsem, writes=[dtb_r])
        P.dma("sp", Ab[:, :], cx.a_log[0:1, :].partition_broadcast(128), Ab_r.dsem, writes=[Ab_r])
        P.op("act", lambda h: h.activation(out=Ab[:, :], in_=Ab[:, :], func=AF.Exp), reads=[Ab_r], writes=[Ab_r])
        P.op("dve", lambda h: h.tensor_scalar(out=Ab[:, :], in0=Ab[:, :], scalar1=-1.0, scalar2=None, op0=ALU.mult),
             reads=[Ab_r], writes=[Ab_r])
        P.dma("sp", gq[:, :], cx.gqk[:, :], gq_r.dsem, writes=[gq_r])
        P.op("dve", lambda h: h.tensor_scalar(out=gq[:, 0:1], in0=gq[:, 0:1], scalar1=0.125, scalar2=None, op0=ALU.mult),
             reads=[gq_r], writes=[gq_r])
        P.op("pool", lambda h: h.memset(zt[:, :, :], 0.0), writes=[zt_r])
        P.dma("sp", cx.XBC[:, :, 0:2], zt[:, :, :], zt_r.dsem, reads=[zt_r], writes=[cx.XBC_r], partial=True)
        P.dma("sp", cx.XBC[:, :, T + 2:T + 4], zt[:, :, :], zt_r.dsem, reads=[zt_r], writes=[cx.XBC_r], partial=True)

        _chk(cx, 1)
        win = cx.w_in.rearrange("(kc p) n -> p kc n", p=128)
        identf = cx.c_identf
        evi = [0]

        def evac_copy(out_ap, in_ap, reads, writes):
            evi[0] += 1
            if evi[0] % 2 == 0:
                P.op("act", lambda h: h.copy(out=out_ap, in_=in_ap), reads=reads, writes=writes)
            else:
                P.op("dve", lambda h: h.tensor_copy(out=out_ap, in_=in_ap), reads=reads, writes=writes)

        def load_w(c0, kh, ncols=512):
            w, wr = wring.next()
            P.dma("pool", w[:, :, 0:ncols], win[:, kh * 16:(kh + 1) * 16, c0:c0 + ncols], wr.dsem, writes=[wr])
            return w, wr

        grp = [0]
        for (t0, nt) in _blocks(NT):
            W = nt * 128
            tok0 = t0 * 128
            for i in range(nt):
                P.dma("sp", xt[:, :], cx.xin[tok0 + i * 128: tok0 + (i + 1) * 128, :], xt_r.dsem, writes=[xt_r])
                P.op("act", lambda h: h.activation(out=hn[:, :], in_=xt[:, :], func=AF.Square, accum_out=sm[:, 0:1]),
                     reads=[xt_r], writes=[hn_r, sm_r])
                P.op("act", lambda h: h.activation(out=sm[:, 1:2], in_=sm[:, 0:1], func=AF.Sqrt, bias=EPS, scale=1.0 / D),
                     reads=[sm_r], writes=[sm_r])
                P.op("dve", lambda h: h.reciprocal(out=sm[:, 2:3], in_=sm[:, 1:2]), reads=[sm_r], writes=[sm_r])
                P.op("dve", lambda h: h.tensor_scalar(out=hn[:, :], in0=xt[:, :], scalar1=sm[:, 2:3], scalar2=None, op0=ALU.mult),
                     reads=[xt_r, sm_r], writes=[hn_r])
                for q4 in range(8):
                    pb, pbr = ps[q4 % 2], ps_r[q4 % 2]
                    for k4 in range(4):
                        kc = q4 * 4 + k4
                        P.op("pe", lambda h, pb=pb, k4=k4, kc=kc: h.transpose(pb[:, k4 * 128:(k4 + 1) * 128], hn[:, kc * 128:(kc + 1) * 128], identf[:, :]),
                             reads=[hn_r, cx.c_r], writes=[pbr], sig=(k4 == 3), partial=(k4 > 0))
                    P.op("dve", lambda h, pb=pb, q4=q4, i=i: h.tensor_tensor(
                        out=hnT[:, q4 * 4:(q4 + 1) * 4, i * 128:(i + 1) * 128],
                        in0=pb[:, :].rearrange("p (k n) -> p k n", k=4),
                        in1=gcol[:, q4 * 4:(q4 + 1) * 4].unsqueeze(2).to_broadcast([128, 4, 128]), op=ALU.mult),
                        reads=[pbr, gcol_r], writes=[hnT_r], partial=True)
            _chk(cx, 2)
            P.dma("sp", cx.HNT[:, :, tok0:tok0 + W], hnT[:, :, 0:W], hnT_r.dsem, reads=[hnT_r], writes=[cx.HNT_r], partial=True)

            _chk(cx, 3)
            P.dma("pool", wdt[:, :, :], win[:, :, O_DT:O_DT + 128], wdt_r.dsem, writes=[wdt_r])
            for i in range(nt):
                t = t0 + i
                pb, pbr = ps[i % 2], ps_r[i % 2]
                for kc in range(NKC):
                    P.op("pe", lambda h, pb=pb, kc=kc, i=i: h.matmul(pb[:, 0:128], hnT[:, kc, i * 128:(i + 1) * 128], wdt[:, kc, :],
                                                                   start=(kc == 0), stop=(kc == NKC - 1)),
                         reads=[hnT_r, wdt_r], writes=[pbr], sig=(kc == NKC - 1), partial=(kc > 0))
                dtt = cx.DT[:, t, :]
                dr = cx.DT_r[t]
                P.op("dve", lambda h, pb=pb, dtt=dtt: h.tensor_tensor(out=dtt, in0=pb[:, 0:128], in1=dtb[:, :], op=ALU.add),
                     reads=[pbr, dtb_r], writes=[dr])
                P.op("act", lambda h, dtt=dtt: h.activation(out=dtt, in_=dtt, func=AF.Exp), reads=[dr], writes=[dr])
                P.op("act", lambda h, dtt=dtt: h.activation(out=dtt, in_=dtt, func=AF.Ln, bias=1.0, scale=1.0),
                     reads=[dr], writes=[dr])
                if t == 0:
                    P.op("pool", lambda h: h.memset(cx.DT[0:NPAD, 0, :], 0.0), reads=[dr], writes=[dr])
                P.op("pool", lambda h, dtt=dtt, t=t: h.tensor_tensor(out=cx.DA[:, t, :], in0=dtt, in1=Ab[:, :], op=ALU.mult),
                     reads=[dr, Ab_r], writes=[cx.DA_r[t]])

            _chk(cx, 4)
            def group_b(c0, kind):
                base = 4 * (grp[0] % 2)
                grp[0] += 1
                for kh in range(2):
                    w, wr = load_w(c0, kh)
                    for j in range(4):
                        pb, pbr = ps[base + j], ps_r[base + j]
                        for k in range(16):
                            kc = kh * 16 + k
                            P.op("pe", lambda h, W=W, pb=pb, w=w, k=k, j=j, kc=kc: h.matmul(pb[:, 0:W], w[:, k, j * 128:(j + 1) * 128], hnT[:, kc, 0:W],
                                                                                 start=(kc == 0), stop=(kc == NKC - 1)),
                                 reads=[hnT_r, wr], writes=[pbr], sig=(k == 15), partial=(kc > 0))
                for j in range(4):
                    pb, pbr = ps[base + j], ps_r[base + j]
                    col = c0 + j * 128
                    if kind == "xbc":
                        s_, sr = sfring.next()
                        evac_copy(s_[:, 0:W], pb[:, 0:W], [pbr], [sr])
                        cc = (col - O_XBC) // 128
                        P.dma("sp", cx.XBC[:, cc, 2 + tok0: 2 + tok0 + W], s_[:, 0:W], sr.dsem, reads=[sr], writes=[cx.XBC_r], partial=True)
                    else:
                        isq = (kind == "q")
                        hd = (col - (O_Q if isq else O_K)) // 128
                        gc = gq[:, 0:1] if isq else gq[:, 1:2]
                        s_, sr = sfring.next()
                        q2, q2r = sqring.next()
                        r_, rr = rqring.next()
                        o_, orr = shring.next()
                        P.op("dve", lambda h, W=W, s_=s_, pb=pb: h.tensor_copy(out=s_[:, 0:W], in_=pb[:, 0:W]), reads=[pbr], writes=[sr])
                        P.op("act", lambda h, W=W, q2=q2, pb=pb: h.activation(out=q2[:, 0:W], in_=pb[:, 0:W], func=AF.Square), reads=[pbr], writes=[q2r])
                        import os as _os
                        _qv = int(_os.environ.get('K_QV', '0'))
                        if _qv == 1:
                            continue
                        P.op("pe", lambda h, W=W, pb=pb, q2=q2: h.matmul(pb[:, 0:W], cx.c_blk64[:, :], q2[:, 0:W], start=True, stop=True),
                             reads=[q2r, cx.c_r], writes=[pbr])
                        if _qv == 2:
                            continue
                        P.op("act", lambda h, W=W, r_=r_, pb=pb: h.activation(out=r_[:, 0:W], in_=pb[:, 0:W], func=AF.Sqrt, bias=EPS, scale=1.0 / 64),
                             reads=[pbr], writes=[rr])
                        if _qv == 3:
                            continue
                        P.op("dve", lambda h, W=W, r_=r_: h.reciprocal(out=r_[:, 0:W], in_=r_[:, 0:W]), reads=[rr], writes=[rr])
                        P.op("dve", lambda h, W=W, o_=o_, s_=s_, r_=r_, gc=gc: h.scalar_tensor_tensor(out=o_[:, 0:W], in0=s_[:, 0:W], scalar=gc, in1=r_[:, 0:W],
                                                                                      op0=ALU.mult, op1=ALU.mult),
                             reads=[sr, rr, gq_r], writes=[orr])
                        dst = cx.QT if isq else cx.KT
                        dst_r = cx.QT_r if isq else cx.KT_r
                        P.dma("sp", dst[:, hd, tok0:tok0 + W], o_[:, 0:W], orr.dsem, reads=[orr], writes=[dst_r], partial=True)

            for g in range(12):
                group_b(O_XBC + g * 512, "xbc")
                _chk(cx, 5)
            _chk(cx, 6)
            for g in range(4):
                group_b(O_Q + g * 512, "q")
            for g in range(4):
                group_b(O_K + g * 512, "k")
            _chk(cx, 7)

            def group_a(c0, kind):
                base = 4 * (grp[0] % 2)
                grp[0] += 1
                for kh in range(2):
                    w, wr = load_w(c0, kh)
                    for i in range(nt):
                        pb, pbr = ps[base + i], ps_r[base + i]
                        for k in range(16):
                            kc = kh * 16 + k
                            P.op("pe", lambda h, pb=pb, w=w, k=k, i=i, kc=kc: h.matmul(pb[:, :], hnT[:, kc, i * 128:(i + 1) * 128], w[:, k, :],
                                                                                 start=(kc == 0), stop=(kc == NKC - 1)),
                                 reads=[hnT_r, wr], writes=[pbr], sig=(k == 15), partial=(kc > 0))
                for i in range(nt):
                    pb, pbr = ps[base + i], ps_r[base + i]
                    r0 = tok0 + i * 128
                    if kind == "v":
                        o_, orr = shring.next()
                        evac_copy(o_[:, :], pb[:, :], [pbr], [orr])
                        cv = c0 - O_V
                        P.dma("sp", cx.V[r0:r0 + 128, cv:cv + 512], o_[:, :], orr.dsem, reads=[orr], writes=[cx.V_r], partial=True)
                    else:
                        s_, sr = sfring.next()
                        P.op("act", lambda h, s_=s_, pb=pb: h.activation(out=s_[:, :], in_=pb[:, :], func=AF.Silu), reads=[pbr], writes=[sr])
                        P.dma("sp", cx.ZS[r0:r0 + 128, c0:c0 + 512], s_[:, :], sr.dsem, reads=[sr], writes=[cx.ZS_r], partial=True)

            for g in range(4):
                group_a(O_V + g * 512, "v")
            _chk(cx, 8)
            for g in range(8):
                group_a(g * 512, "z")


def _bc(ap, shape, axis):
    return ap.unsqueeze(axis).to_broadcast(shape)


def phase_B(cx, sweep):
    nc, P, NT = cx.nc, cx.P, cx.NT
    T = NT * 128
    d = sweep
    with ExitStack() as st:
        sb = lambda n, s, dt_: st.enter_context(nc.sbuf_tensor(n, s, dt_))
        pfx = "b%d_" % sweep
        R = lambda n, **kw: P.res(pfx + n, **kw)
        xin_ = sb(pfx + "xin", [128, 16, 132] if sweep == 0 else [128, 1, 2], F32); xin_r = R("xin", dma=True)
        acc = sb(pfx + "acc", [128, 16, 128] if sweep == 0 else [128, 1, 2], F32); acc_r = R("acc")
        tmp = sb(pfx + "tmp", [128, 16, 128] if sweep == 0 else [128, 1, 2], F32); tmp_r = R("tmp")
        xbcT = sb(pfx + "xbcT", [128, 48, 128], BF16); xbcT_r = R("xbcT", dma=True)
        xtok = sb(pfx + "xtok", [128, D], BF16); xtok_r = R("xtok", dma=True)
        btok = sb(pfx + "btok", [128, 1024], BF16); btok_r = R("btok", dma=True)
        xw = sb(pfx + "xw", [128, D], BF16); xw_r = R("xw")
        S = sb(pfx + "S", [128, D], F32); S_r = R("S")
        Sb = sb(pfx + "Sb", [128, D], BF16); Sb_r = R("Sb")
        aUs = [(sb(pfx + "aU%d" % i, [128, 8, 128], F32), R("aU")) for i in range(2)]
        Es = [(sb(pfx + "E%d" % i, [128, 8, 128], F32), R("E")) for i in range(2)]
        MTs = [(sb(pfx + "MT%d" % i, [128, 8, 128], BF16), R("MT")) for i in range(2)]
        CBms = [(sb(pfx + "CBm%d" % i, [128, 128], F32), R("CBm")) for i in range(2)]
        yacc = sb(pfx + "yacc", [128, D], F32); yacc_r = R("yacc", dma=True)
        yoss = [(sb(pfx + "yos%d" % i, [128, 512], F32), R("yos")) for i in range(2)]
        ecs = sb(pfx + "ecs", [128, 64], F32); ecs_r = R("ecs")
        wst = sb(pfx + "wst", [128, 64], F32); wst_r = R("wst")
        dec = sb(pfx + "dec", [128, 64], F32); dec_r = R("dec")
        cw = sb(pfx + "cw", [128, 48, 5], F32); cw_r = R("cw", dma=True)
        cb = sb(pfx + "cb", [128, 48], F32); cb_r = R("cb", dma=True)
        dsk = sb(pfx + "dsk", [128, 64], F32); dsk_r = R("dsk", dma=True)
        zs = sb(pfx + "zs", [128, D] if sweep == 1 else [128, 2], F32); zs_r = R("zs", dma=True)
        gs = sb(pfx + "gs", [128, D] if sweep == 1 else [128, 2], F32); gs_r = R("gs", dma=True)
        ynb = sb(pfx + "ynb", [128, D] if sweep == 1 else [128, 2], BF16); ynb_r = R("ynb")
        ynT = sb(pfx + "ynT", [128, NKC, 128] if sweep == 1 else [128, 1, 2], BF16); ynT_r = R("ynT", dma=True)
        sm = sb(pfx + "sm", [128, 24], F32); sm_r = R("sm")
        psA = [st.enter_context(nc.psum_tensor(pfx + "psA%d" % i, [128, 512], F32)) for i in range(5)]
        psA_r = [R("psA", excl=True) for i in range(5)]
        psT = [st.enter_context(nc.psum_tensor(pfx + "psT%d" % i, [128, 1024], BF16)) for i in range(2)]
        psT_r = [R("psT", excl=True) for i in range(2)]
        psS = st.enter_context(nc.psum_tensor(pfx + "psS", [128, 512], F32)); psS_r = R("psS", excl=True)
        ident = cx.c_ident
        _barrier(P)
        if sweep == 0:
            P.dma("sp", cw[:, :, :], cx.convw[:, :, :], cw_r.dsem, writes=[cw_r])
            P.dma("sp", cb[:, :], cx.convb[:, :], cb_r.dsem, writes=[cb_r])
            P.dma("sp", dsk[:, :], cx.d_skip[0:1, :].partition_broadcast(128), dsk_r.dsem, writes=[dsk_r])
        else:
            P.dma("sp", gs[:, :], cx.ssd_norm_g[0:1, :].partition_broadcast(128), gs_r.dsem, writes=[gs_r])
        P.op("pool", lambda h: h.memset(S[:, :], 0.0), writes=[S_r])
        P.op("pool", lambda h: h.memset(Sb[:, :], 0.0), writes=[Sb_r])
        m_cum = cx.c_m_kles if d == 0 else cx.c_m_kgel
        m_seg = cx.c_m_kgts if d == 0 else cx.c_m_klts
        order = list(range(NT)) if d == 0 else list(range(NT - 1, -1, -1))
        for t in order:
            tok0 = t * 128
            dtr, dar = cx.DT_r[t], cx.DA_r[t]
            DTt = cx.DT[:, t, d * 64:(d + 1) * 64]
            DAt = cx.DA[:, t, d * 64:(d + 1) * 64]
            if sweep == 0:
                for c3 in range(3):
                    P.dma("sp", xin_[:, :, :], cx.XBC[:, c3 * 16:(c3 + 1) * 16, tok0:tok0 + 132], xin_r.dsem, reads=[cx.XBC_r], writes=[xin_r])
                    for k in range(5):
                        wk = _bc(cw[:, c3 * 16:(c3 + 1) * 16, k], [128, 16, 128], 2)
                        if k == 0:
                            P.op("dve", lambda h, wk=wk: h.tensor_tensor(out=acc[:, :, :], in0=xin_[:, :, 0:128], in1=wk, op=ALU.mult),
                                 reads=[xin_r, cw_r], writes=[acc_r])
                        else:
                            P.op("pool", lambda h, wk=wk, k=k: h.tensor_tensor(out=tmp[:, :, :], in0=xin_[:, :, k:k + 128], in1=wk, op=ALU.mult),
                                 reads=[xin_r, cw_r], writes=[tmp_r])
                            P.op("dve", lambda h: h.tensor_tensor(out=acc[:, :, :], in0=acc[:, :, :], in1=tmp[:, :, :], op=ALU.add),
                                 reads=[acc_r, tmp_r], writes=[acc_r])
                    P.op("dve", lambda h, c3=c3: h.tensor_tensor(out=acc[:, :, :], in0=acc[:, :, :], in1=_bc(cb[:, c3 * 16:(c3 + 1) * 16], [128, 16, 128], 2), op=ALU.add),
                         reads=[acc_r, cb_r], writes=[acc_r])
                    P.op("act", lambda h, c3=c3: h.activation(out=xbcT[:, c3 * 16:(c3 + 1) * 16, :], in_=acc[:, :, :], func=AF.Silu),
                         reads=[acc_r], writes=[xbcT_r], partial=True)
                if t == 0:
                    P.op("pool", lambda h: h.memset(xbcT[:, :, 0:NPAD], 0.0), reads=[xbcT_r], writes=[xbcT_r])
                for q in range(5):
                    pb, pbr = psT[q % 2], psT_r[q % 2]
                    for k8 in range(8):
                        cc = q * 8 + k8
                        P.op("pe", lambda h, pb=pb, k8=k8, cc=cc: h.transpose(pb[:, k8 * 128:(k8 + 1) * 128], xbcT[:, cc, :], ident[:, :]),
                             reads=[xbcT_r, cx.c_r], writes=[pbr], sig=(k8 == 7), partial=(k8 > 0))
                    if q < 4:
                        P.op("act" if q % 2 else "dve", (lambda h, pb=pb, q=q: h.copy(out=xtok[:, q * 1024:(q + 1) * 1024], in_=pb[:, :])) if q % 2 else
                             (lambda h, pb=pb, q=q: h.tensor_copy(out=xtok[:, q * 1024:(q + 1) * 1024], in_=pb[:, :])),
                             reads=[pbr], writes=[xtok_r], partial=(q > 0))
                    else:
                        P.op("dve", lambda h, pb=pb: h.tensor_copy(out=btok[:, :], in_=pb[:, :]), reads=[pbr], writes=[btok_r])
                P.dma("sp", cx.XTOK[tok0:tok0 + 128, :], xtok[:, :], xtok_r.dsem, reads=[xtok_r], writes=[cx.XTOK_r], partial=True)
                P.dma("sp", cx.BTOK[tok0:tok0 + 128, :], btok[:, :], btok_r.dsem, reads=[btok_r], writes=[cx.BTOK_r], partial=True)
                P.dma("sp", cx.BCT[:, :, tok0:tok0 + 128], xbcT[:, 32:48, :], xbcT_r.dsem, reads=[xbcT_r], writes=[cx.BCT_r], partial=True)
            else:
                P.dma("sp", xtok[:, :], cx.XTOK[tok0:tok0 + 128, :], xtok_r.dsem, reads=[cx.XTOK_r], writes=[xtok_r])
                P.dma("sp", btok[:, :], cx.BTOK[tok0:tok0 + 128, :], btok_r.dsem, reads=[cx.BTOK_r], writes=[btok_r])
                P.dma("sp", xbcT[:, 32:48, :], cx.BCT[:, :, tok0:tok0 + 128], xbcT_r.dsem, reads=[cx.BCT_r], writes=[xbcT_r])
                P.dma("sp", yacc[:, :], cx.Y1[tok0:tok0 + 128, :], yacc_r.dsem, reads=[cx.Y1_r], writes=[yacc_r])
                P.dma("sp", zs[:, :], cx.ZS[tok0:tok0 + 128, :], zs_r.dsem, reads=[cx.ZS_r], writes=[zs_r])

            pb, pbr = psS, psS_r
            P.op("pe", lambda h, DAt=DAt: h.matmul(psS[:, 0:64], m_cum[:, :], DAt, start=True, stop=True), reads=[dar, cx.c_r], writes=[pbr])
            P.op("act", lambda h: h.activation(out=ecs[:, :], in_=psS[:, 0:64], func=AF.Exp), reads=[pbr], writes=[ecs_r])
            P.op("pe", lambda h, DAt=DAt: h.matmul(psS[:, 64:128], m_seg[:, :], DAt, start=True, stop=True), reads=[dar, cx.c_r], writes=[pbr])
            P.op("act", lambda h: h.activation(out=wst[:, :], in_=psS[:, 64:128], func=AF.Exp), reads=[pbr], writes=[wst_r])
            P.op("dve", lambda h, DTt=DTt: h.tensor_tensor(out=wst[:, :], in0=wst[:, :], in1=DTt, op=ALU.mult), reads=[wst_r, dtr], writes=[wst_r])
            P.op("pe", lambda h, DAt=DAt: h.matmul(psS[:, 128:192], cx.c_onesf[:, :], DAt, start=True, stop=True), reads=[dar, cx.c_r], writes=[pbr])
            P.op("act", lambda h: h.activation(out=dec[:, :], in_=psS[:, 128:192], func=AF.Exp), reads=[pbr], writes=[dec_r])
            P.op("pool", lambda h: h.tensor_tensor(out=xw[:, :].rearrange("p (h e) -> p h e", h=64), in0=xtok[:, :].rearrange("p (h e) -> p h e", h=64),
                                                  in1=_bc(wst[:, :], [128, 64, 64], 2), op=ALU.mult),
                 reads=[xtok_r, wst_r], writes=[xw_r])
            if sweep == 0:
                P.op("pool", lambda h: h.tensor_tensor(out=yacc[:, :].rearrange("p (h e) -> p h e", h=64), in0=xtok[:, :].rearrange("p (h e) -> p h e", h=64),
                                                      in1=_bc(dsk[:, :], [128, 64, 64], 2), op=ALU.mult),
                     reads=[xtok_r, dsk_r], writes=[yacc_r])
            def do_group(g, aU, aU_r, E, E_r, MT, MT_r, CBm, CBm_r, yos, yos_r):
                BTg = xbcT[:, 32 + g, :]
                CTg = xbcT[:, 40 + g, :]
                pCB, pCB_r = psA[0], psA_r[0]
                pSeg = [psA[1], psA[2]]; pSeg_r = [psA_r[1], psA_r[2]]
                pY, pY_r = psA[3], psA_r[3]
                pO, pO_r = psA[4], psA_r[4]
                hs = slice(g * 8, (g + 1) * 8)
                P.op("pe", lambda h, BTg=BTg, CTg=CTg, pCB=pCB: h.matmul(pCB[:, 0:128], BTg, CTg, start=True, stop=True), reads=[xbcT_r], writes=[pCB_r])
                P.op("dve", lambda h, pCB=pCB: h.tensor_tensor(out=CBm[:, :], in0=pCB[:, 0:128], in1=m_cum[:, :], op=ALU.mult), reads=[pCB_r, cx.c_r], writes=[CBm_r])
                P.op("pool", lambda h, DAt=DAt, hs=hs: h.tensor_tensor(out=aU[:, :, :], in0=_bc(DAt[:, hs], [128, 8, 128], 2), in1=_bc(m_cum[:, :], [128, 8, 128], 1), op=ALU.mult),
                     reads=[dar, cx.c_r], writes=[aU_r])
                for hf in range(2):
                    P.op("pe", lambda h, hf=hf, pSeg=pSeg: h.matmul(pSeg[hf][:, :], m_seg[:, :], aU[:, hf * 4:(hf + 1) * 4, :].rearrange("p a b -> p (a b)"), start=True, stop=True),
                         reads=[aU_r, cx.c_r], writes=[pSeg_r[hf]])
                    P.op("act", lambda h, hf=hf, pSeg=pSeg: h.activation(out=E[:, hf * 4:(hf + 1) * 4, :].rearrange("p a b -> p (a b)"), in_=pSeg[hf][:, :], func=AF.Exp),
                         reads=[pSeg_r[hf]], writes=[E_r], partial=(hf > 0))
                P.op("dve", lambda h, DTt=DTt, hs=hs: h.tensor_tensor(out=E[:, :, :], in0=E[:, :, :], in1=_bc(DTt[:, hs], [128, 8, 128], 2), op=ALU.mult),
                     reads=[E_r, dtr], writes=[E_r])
                P.op("dve", lambda h: h.tensor_tensor(out=MT[:, :, :], in0=E[:, :, :], in1=_bc(CBm[:, :], [128, 8, 128], 1), op=ALU.mult),
                     reads=[E_r, CBm_r], writes=[MT_r])
                for j in range(8):
                    hh = g * 8 + j
                    P.op("pe", lambda h, j=j, hh=hh, pY=pY: h.matmul(pY[:, j * 64:(j + 1) * 64], MT[:, j, :], xtok[:, hh * 64:(hh + 1) * 64], start=True, stop=True),
                         reads=[MT_r, xtok_r], writes=[pY_r], sig=(j == 7), partial=(j > 0))
                P.op("pe", lambda h, CTg=CTg, g=g, pO=pO: h.matmul(pO[:, :], CTg, Sb[:, g * 512:(g + 1) * 512], start=True, stop=True),
                     reads=[xbcT_r, Sb_r], writes=[pO_r])
                P.op("dve", lambda h, pO=pO, hs=hs: h.tensor_tensor(out=yos[:, :].rearrange("p (h e) -> p h e", h=8), in0=pO[:, :].rearrange("p (h e) -> p h e", h=8),
                                                                in1=_bc(ecs[:, hs], [128, 8, 64], 2), op=ALU.mult),
                     reads=[pO_r, ecs_r], writes=[yos_r])
                P.op("dve", lambda h, pY=pY: h.tensor_tensor(out=yos[:, :], in0=pY[:, :], in1=yos[:, :], op=ALU.add), reads=[pY_r, yos_r], writes=[yos_r])
                P.op("pool", lambda h, g=g: h.tensor_tensor(out=yacc[:, g * 512:(g + 1) * 512], in0=yacc[:, g * 512:(g + 1) * 512], in1=yos[:, :], op=ALU.add),
                     reads=[yos_r, yacc_r], writes=[yacc_r])
                P.op("pe", lambda h, g=g, pO=pO: h.matmul(pO[:, :], btok[:, g * 128:(g + 1) * 128], xw[:, g * 512:(g + 1) * 512], start=True, stop=True),
                     reads=[btok_r, xw_r], writes=[pO_r])
                P.op("pool", lambda h, g=g, hs=hs: h.tensor_tensor(out=S[:, g * 512:(g + 1) * 512].rearrange("p (h e) -> p h e", h=8),
                                                                 in0=S[:, g * 512:(g + 1) * 512].rearrange("p (h e) -> p h e", h=8),
                                                                 in1=_bc(dec[:, hs], [128, 8, 64], 2), op=ALU.mult),
                     reads=[S_r, dec_r, Sb_r], writes=[S_r])
                P.op("dve", lambda h, g=g, pO=pO: h.tensor_tensor(out=S[:, g * 512:(g + 1) * 512], in0=pO[:, :], in1=S[:, g * 512:(g + 1) * 512], op=ALU.add),
                     reads=[pO_r, S_r], writes=[S_r])
                P.op("act", lambda h, g=g: h.copy(out=Sb[:, g * 512:(g + 1) * 512], in_=S[:, g * 512:(g + 1) * 512]), reads=[S_r], writes=[Sb_r])
            for g in range(8):
                b2 = g % 2
                do_group(g, aUs[b2][0], aUs[b2][1], Es[b2][0], Es[b2][1], MTs[b2][0], MTs[b2][1],
                         CBms[b2][0], CBms[b2][1], yoss[b2][0], yoss[b2][1])
            if sweep == 0:
                P.dma("sp", cx.Y1[tok0:tok0 + 128, :], yacc[:, :], yacc_r.dsem, reads=[yacc_r], writes=[cx.Y1_r], partial=True)
            else:
                P.op("dve", lambda h: h.tensor_tensor(out=yacc[:, :], in0=yacc[:, :], in1=zs[:, :], op=ALU.mult), reads=[yacc_r, zs_r], writes=[yacc_r])
                for g in range(8):
                    P.op("act", lambda h, g=g: h.activation(out=ynb[:, g * 512:(g + 1) * 512], in_=yacc[:, g * 512:(g + 1) * 512], func=AF.Square, accum_out=sm[:, g:g + 1]),
                         reads=[yacc_r], writes=[ynb_r, sm_r])
                P.op("act", lambda h: h.activation(out=sm[:, 8:16], in_=sm[:, 0:8], func=AF.Sqrt, bias=EPS, scale=1.0 / 512), reads=[sm_r], writes=[sm_r])
                P.op("dve", lambda h: h.reciprocal(out=sm[:, 16:24], in_=sm[:, 8:16]), reads=[sm_r], writes=[sm_r])
                P.op("dve", lambda h: h.tensor_tensor(out=yacc[:, :].rearrange("p (g e) -> p g e", g=8), in0=yacc[:, :].rearrange("p (g e) -> p g e", g=8),
                                                     in1=_bc(sm[:, 16:24], [128, 8, 512], 2), op=ALU.mult), reads=[yacc_r, sm_r], writes=[yacc_r])
                P.op("dve", lambda h: h.tensor_tensor(out=ynb[:, :], in0=yacc[:, :], in1=gs[:, :], op=ALU.mult), reads=[yacc_r, gs_r], writes=[ynb_r])
                for q in range(4):
                    pb, pbr = psT[q % 2], psT_r[q % 2]
                    for k8 in range(8):
                        cc = q * 8 + k8
                        P.op("pe", lambda h, pb=pb, k8=k8, cc=cc: h.transpose(pb[:, k8 * 128:(k8 + 1) * 128], ynb[:, cc * 128:(cc + 1) * 128], ident[:, :]),
                             reads=[ynb_r, cx.c_r], writes=[pbr], sig=(k8 == 7), partial=(k8 > 0))
                    P.op("act" if q % 2 else "dve", (lambda h, pb=pb, q=q: h.copy(out=ynT[:, q * 8:(q + 1) * 8, :], in_=pb[:, :].rearrange("p (k n) -> p k n", k=8))) if q % 2 else
                         (lambda h, pb=pb, q=q: h.tensor_copy(out=ynT[:, q * 8:(q + 1) * 8, :], in_=pb[:, :].rearrange("p (k n) -> p k n", k=8))),
                         reads=[pbr], writes=[ynT_r], partial=(q > 0))
                P.dma("sp", cx.YNT[:, :, tok0:tok0 + 128], ynT[:, :, :], ynT_r.dsem, reads=[ynT_r], writes=[cx.YNT_r], partial=True)
        _flush(P)


def _blocks_n(NT, n):
    out = []
    t = 0
    while t < NT:
        k = min(n, NT - t)
        out.append((t, k))
        t += k
    return out


SLOPES = [2.0 ** (-8.0 * (h + 1) / 16.0) for h in range(16)]


def phase_C(cx):
    nc, P, NT = cx.nc, cx.P, cx.NT
    T = NT * 128
    with ExitStack() as st:
        sb = lambda n, s, dt_: st.enter_context(nc.sbuf_tensor(n, s, dt_))
        R = lambda n, **kw: P.res("c_" + n, **kw)
        KTh = sb("cc_KT", [128, T], BF16); KTh_r = R("KT", dma=True)
        QTh = sb("cc_QT", [128, T], BF16); QTh_r = R("QT", dma=True)
        Vh = sb("cc_V", [128, NT, 128], BF16); Vh_r = R("V", dma=True)
        absd = sb("cc_absd", [128, 2 * T + 512], F32); absd_r = R("absd", dma=True)
        padb = sb("cc_padb", [128, 1], F32); padb_r = R("padb", dma=True)
        lam4 = sb("c_lam4", [128, 4, 64], F32); lam4_r = R("lam4", dma=True)
        lsm = sb("c_lsm", [128, 8], F32); lsm_r = R("lsm")
        gsub = sb("c_gsub", [128, 1], F32); gsub_r = R("gsub", dma=True)
        tmpS = [sb("c_tmp%d" % i, [128, 512], F32) for i in range(2)]; tmpS_r = [R("tmp") for i in range(2)]
        tring = Ring(list(zip(tmpS, tmpS_r)))
        PT = [sb("c_PT%d" % i, [128, 512], BF16) for i in range(3)]; PT_r = [R("PT") for i in range(3)]
        pring = Ring(list(zip(PT, PT_r)))
        r0 = sb("c_r0", [128, 512], F32); r0_r = R("r0")
        r1 = sb("c_r1", [128, 512], F32); r1_r = R("r1")
        o0 = sb("c_o0", [128, 512], F32); o0_r = R("o0")
        o1 = sb("c_o1", [128, 512], F32); o1_r = R("o1")
        sqb = sb("c_sqb", [128, 512], BF16); sqb_r = R("sqb")
        ya = [sb("c_ya%d" % i, [128, 512], BF16) for i in range(2)]; ya_r = [R("ya", dma=True) for i in range(2)]
        yring = Ring(list(zip(ya, ya_r)))
        psS = [st.enter_context(nc.psum_tensor("c_psS%d" % i, [128, 512], F32)) for i in range(2)]
        psS_r = [R("psS", excl=True) for i in range(2)]
        sring = Ring(list(zip(psS, psS_r)))
        psO = [st.enter_context(nc.psum_tensor("c_psO%d" % i, [128, 512], F32)) for i in range(2)]
        psO_r = [R("psO", excl=True) for i in range(2)]
        psR = [st.enter_context(nc.psum_tensor("c_psR%d" % i, [128, 512], F32)) for i in range(2)]
        psR_r = [R("psR", excl=True) for i in range(2)]
        ones = cx.c_ones
        _barrier(P)
        P.dma("sp", absd[:, :], cx.cdr["absd"][:, :], absd_r.dsem, writes=[absd_r])
        P.dma("sp", padb[:, :], cx.cdr["padbias"][:, :], padb_r.dsem, writes=[padb_r])
        P.dma("sp", gsub[:, :], cx.gsub[:, :], gsub_r.dsem, writes=[gsub_r])
        P.op("dve", lambda h: h.tensor_scalar(out=gsub[:, :], in0=gsub[:, :], scalar1=0.8, scalar2=None, op0=ALU.mult), reads=[gsub_r], writes=[gsub_r])
        P.dma("sp", lam4[:, :, :].rearrange("p a b -> p (a b)"), cx.lam4[0:1, :].partition_broadcast(128), lam4_r.dsem, writes=[lam4_r])
        P.op("dve", lambda h: h.tensor_tensor(out=lam4[:, 0, :], in0=lam4[:, 0, :], in1=lam4[:, 1, :], op=ALU.mult), reads=[lam4_r], writes=[lam4_r])
        P.op("dve", lambda h: h.tensor_tensor(out=lam4[:, 2, :], in0=lam4[:, 2, :], in1=lam4[:, 3, :], op=ALU.mult), reads=[lam4_r], writes=[lam4_r])
        P.op("dve", lambda h: h.tensor_reduce(out=lsm[:, 0:1], in_=lam4[:, 0, :], axis=AX.X, op=ALU.add), reads=[lam4_r], writes=[lsm_r])
        P.op("dve", lambda h: h.tensor_reduce(out=lsm[:, 1:2], in_=lam4[:, 2, :], axis=AX.X, op=ALU.add), reads=[lam4_r, lsm_r], writes=[lsm_r])
        P.op("act", lambda h: h.activation(out=lsm[:, 2:4], in_=lsm[:, 0:2], func=AF.Exp), reads=[lsm_r], writes=[lsm_r])
        P.op("dve", lambda h: h.tensor_tensor(out=lsm[:, 4:5], in0=lsm[:, 3:4], in1=lsm[:, 2:3], op=ALU.subtract), reads=[lsm_r], writes=[lsm_r])
        P.op("dve", lambda h: h.tensor_scalar(out=lsm[:, 4:5], in0=lsm[:, 4:5], scalar1=-0.2, scalar2=None, op0=ALU.add), reads=[lsm_r], writes=[lsm_r])
        nlam = lsm[:, 4:5]
        for hd in range(16):
            m_h = SLOPES[hd]
            P.dma("sp", KTh[:, :], cx.KT[:, hd, :], KTh_r.dsem, reads=[cx.KT_r], writes=[KTh_r])
            P.dma("sp", QTh[:, :], cx.QT[:, hd, :], QTh_r.dsem, reads=[cx.QT_r], writes=[QTh_r])
            P.dma("sp", Vh[:, :, :], cx.V[:, hd * 128:(hd + 1) * 128].rearrange("(t p) c -> p t c", p=128), Vh_r.dsem, reads=[cx.V_r], writes=[Vh_r])
            for (t0, nt) in _blocks_n(NT, 4):
                W = nt * 128
                q0 = t0 * 128
                for kt in range(NT):
                    u0 = T + 128 * (t0 - kt)
                    for i in range(2):
                        pS, pS_r = sring.next()
                        tm, tm_r = tring.next()
                        pt, pt_r = pring.next()
                        P.op("pe", lambda h, W=W, pS=pS, i=i, kt=kt, q0=q0: h.matmul(pS[:, 0:W], KTh[i * 64:(i + 1) * 64, kt * 128:(kt + 1) * 128],
                                                                               QTh[i * 64:(i + 1) * 64, q0:q0 + W], start=True, stop=True),
                             reads=[KTh_r, QTh_r], writes=[pS_r])
                        P.op("dve", lambda h, W=W, pS=pS, tm=tm, u0=u0, m_h=m_h: h.scalar_tensor_tensor(out=tm[:, 0:W], in0=absd[:, u0:u0 + W], scalar=-m_h, in1=pS[:, 0:W],
                                                                                                  op0=ALU.mult, op1=ALU.add),
                             reads=[pS_r, absd_r], writes=[tm_r])
                        if kt == 0:
                            P.op("act", lambda h, W=W, pt=pt, tm=tm: h.activation(out=pt[:, 0:W], in_=tm[:, 0:W], func=AF.Exp, bias=padb[:, 0:1], scale=1.0),
                                 reads=[tm_r, padb_r], writes=[pt_r])
                        else:
                            P.op("act", lambda h, W=W, pt=pt, tm=tm: h.activation(out=pt[:, 0:W], in_=tm[:, 0:W], func=AF.Exp), reads=[tm_r], writes=[pt_r])
                        P.op("pe", lambda h, W=W, pt=pt, i=i, kt=kt: h.matmul(psO[i][:, 0:W], Vh[:, kt, :], pt[:, 0:W], start=(kt == 0), stop=(kt == NT - 1)),
                             reads=[Vh_r, pt_r], writes=[psO_r[i]], sig=False, partial=(kt > 0))
                        P.op("pe", lambda h, W=W, pt=pt, i=i, kt=kt: h.matmul(psR[i][:, 0:W], ones[:, :], pt[:, 0:W], start=(kt == 0), stop=(kt == NT - 1)),
                             reads=[cx.c_r, pt_r], writes=[psR_r[i]], partial=(kt > 0))
                P.op("dve", lambda h, W=W: h.reciprocal(out=r0[:, 0:W], in_=psR[0][:, 0:W]), reads=[psR_r[0]], writes=[r0_r])
                P.op("dve", lambda h, W=W: h.reciprocal(out=r1[:, 0:W], in_=psR[1][:, 0:W]), reads=[psR_r[1]], writes=[r1_r])
                P.op("dve", lambda h, W=W: h.tensor_tensor(out=o0[:, 0:W], in0=psO[0][:, 0:W], in1=r0[:, 0:W], op=ALU.mult), reads=[psO_r[0], r0_r], writes=[o0_r])
                P.op("dve", lambda h, W=W: h.tensor_tensor(out=o1[:, 0:W], in0=psO[1][:, 0:W], in1=r1[:, 0:W], op=ALU.mult), reads=[psO_r[1], r1_r], writes=[o1_r])
                P.op("dve", lambda h, W=W: h.scalar_tensor_tensor(out=o0[:, 0:W], in0=o1[:, 0:W], scalar=nlam, in1=o0[:, 0:W], op0=ALU.mult, op1=ALU.add),
                     reads=[o0_r, o1_r, lsm_r], writes=[o0_r])
                P.op("act", lambda h, W=W: h.activation(out=sqb[:, 0:W], in_=o0[:, 0:W], func=AF.Square), reads=[o0_r], writes=[sqb_r])
                P.op("pe", lambda h, W=W: h.matmul(psR[0][:, 0:W], ones[:, :], sqb[:, 0:W], start=True, stop=True), reads=[cx.c_r, sqb_r], writes=[psR_r[0]])
                P.op("act", lambda h, W=W: h.activation(out=r0[:, 0:W], in_=psR[0][:, 0:W], func=AF.Sqrt, bias=EPS, scale=1.0 / 128), reads=[psR_r[0]], writes=[r0_r])
                P.op("dve", lambda h, W=W: h.reciprocal(out=r0[:, 0:W], in_=r0[:, 0:W]), reads=[r0_r], writes=[r0_r])
                y_, y_r = yring.next()
                P.op("dve", lambda h, W=W, y_=y_: h.scalar_tensor_tensor(out=y_[:, 0:W], in0=o0[:, 0:W], scalar=gsub[:, 0:1], in1=r0[:, 0:W], op0=ALU.mult, op1=ALU.mult),
                     reads=[o0_r, r0_r, gsub_r], writes=[y_r])
                P.dma("sp", cx.YAT[:, hd, q0:q0 + W], y_[:, 0:W], y_r.dsem, reads=[y_r], writes=[cx.YAT_r], partial=True)
        _flush(P)


def _norm_transpose(cx, P, src_rows_ap, src_res, xt, xt_r, sm, sm_r, gcol, gcol_r, ps, ps_r, hT, hT_r, i, identf):
    P.dma("sp", xt[:, :], src_rows_ap, xt_r.dsem, reads=[src_res] if src_res is not None else [], writes=[xt_r])
    P.op("act", lambda h: h.activation(out=cx.junk[:, :], in_=xt[:, :], func=AF.Square, accum_out=sm[:, 0:1]), reads=[xt_r], writes=[cx.junk_r, sm_r])
    P.op("act", lambda h: h.activation(out=sm[:, 1:2], in_=sm[:, 0:1], func=AF.Sqrt, bias=EPS, scale=1.0 / D), reads=[sm_r], writes=[sm_r])
    P.op("dve", lambda h: h.reciprocal(out=sm[:, 2:3], in_=sm[:, 1:2]), reads=[sm_r], writes=[sm_r])
    P.op("dve", lambda h: h.tensor_scalar(out=xt[:, :], in0=xt[:, :], scalar1=sm[:, 2:3], scalar2=None, op0=ALU.mult), reads=[xt_r, sm_r], writes=[xt_r])
    for q4 in range(8):
        pb, pbr = ps[q4 % 2], ps_r[q4 % 2]
        for k4 in range(4):
            kc = q4 * 4 + k4
            P.op("pe", lambda h, pb=pb, k4=k4, kc=kc: h.transpose(pb[:, k4 * 128:(k4 + 1) * 128], xt[:, kc * 128:(kc + 1) * 128], identf[:, :]),
                 reads=[xt_r, cx.c_r], writes=[pbr], sig=(k4 == 3), partial=(k4 > 0))
        P.op("dve", lambda h, pb=pb, q4=q4, i=i: h.tensor_tensor(
            out=hT[:, q4 * 4:(q4 + 1) * 4, i * 128:(i + 1) * 128],
            in0=pb[:, :].rearrange("p (k n) -> p k n", k=4),
            in1=gcol[:, q4 * 4:(q4 + 1) * 4].unsqueeze(2).to_broadcast([128, 4, 128]), op=ALU.mult),
            reads=[pbr, gcol_r], writes=[hT_r], partial=True)


def phase_D1(cx):
    nc, P, NT = cx.nc, cx.P, cx.NT
    with ExitStack() as st:
        sb = lambda n, s, dt_: st.enter_context(nc.sbuf_tensor(n, s, dt_))
        R = lambda n, **kw: P.res("d_" + n, **kw)
        WB = 384
        hnT = sb("d_hnT", [128, NKC, WB], BF16); hnT_r = R("hnT", dma=True)
        ynT = sb("d_ynT", [128, NKC, WB], BF16); ynT_r = R("ynT", dma=True)
        yaT = sb("d_yaT", [128, 16, WB], BF16); yaT_r = R("yaT", dma=True)
        mxT = sb("d_mxT", [128, NKC, WB], BF16); mxT_r = R("mxT")
        G1 = sb("d_G1", [128, 4, WB], F32); G1_r = R("G1")
        G2 = sb("d_G2", [128, 4, WB], F32); G2_r = R("G2")
        M = sb("d_M", [128, 4, WB], F32); M_r = R("M")
        bg = sb("d_bg", [128, 64], F32); bg_r = R("bg", dma=True)
        wt = [sb("d_w%d" % i, [128, 16, 512], BF16) for i in range(3)]; wt_r = [R("w", dma=True) for i in range(3)]
        wring = Ring(list(zip(wt, wt_r)))
        xs_ = [sb("d_xs%d" % i, [128, 512], F32) for i in range(3)]; xs_r = [R("xs", dma=True) for i in range(3)]
        xring = Ring(list(zip(xs_, xs_r)))
        ps = [st.enter_context(nc.psum_tensor("d_ps%d" % i, [128, 512], F32)) for i in range(8)]
        ps_r = [R("ps", excl=True) for i in range(8)]
        _barrier(P)
        P.dma("sp", bg[:, :], cx.bgcol[:, :], bg_r.dsem, writes=[bg_r])
        wbg = cx.w_bg.rearrange("(kc p) n -> p kc n", p=128)
        wso = cx.w_so.rearrange("(kc p) n -> p kc n", p=128)
        wao = cx.w_ao.rearrange("(kc p) n -> p kc n", p=128)
        wo = cx.w_o.rearrange("(kc p) n -> p kc n", p=128)
        grp = [0]

        def load_w(src, kh, c0, nk=16):
            w, wr = wring.next()
            P.dma("pool", w[:, 0:nk, :], src[:, kh * 16:kh * 16 + nk, c0:c0 + 512], wr.dsem, writes=[wr])
            return w, wr

        for (t0, nt) in _blocks_n(NT, 3):
            W = nt * 128
            tok0 = t0 * 128
            P.dma("sp", hnT[:, :, 0:W], cx.HNT[:, :, tok0:tok0 + W], hnT_r.dsem, reads=[cx.HNT_r], writes=[hnT_r])
            P.dma("sp", ynT[:, :, 0:W], cx.YNT[:, :, tok0:tok0 + W], ynT_r.dsem, reads=[cx.YNT_r], writes=[ynT_r])
            P.dma("sp", yaT[:, :, 0:W], cx.YAT[:, :, tok0:tok0 + W], yaT_r.dsem, reads=[cx.YAT_r], writes=[yaT_r])

            def proj_b(src, c0, act_, act_r, nkh, kind, cg):
                base = 4 * (grp[0] % 2)
                grp[0] += 1
                nkc = nkh * 16
                for kh in range(nkh):
                    w, wr = load_w(src, kh, c0)
                    for j in range(4):
                        pb, pbr = ps[base + j], ps_r[base + j]
                        for k in range(16):
                            kc = kh * 16 + k
                            P.op("pe", lambda h, W=W, pb=pb, w=w, k=k, j=j, kc=kc, act_=act_, nkc=nkc: h.matmul(pb[:, 0:W], w[:, k, j * 128:(j + 1) * 128], act_[:, kc, 0:W],
                                                                                                      start=(kc == 0), stop=(kc == nkc - 1)),
                                 reads=[act_r, wr], writes=[pbr], sig=(k == 15), partial=(kc > 0))
                for j in range(4):
                    pb, pbr = ps[base + j], ps_r[base + j]
                    ch = cg * 4 + j
                    if kind == "g1":
                        P.op("act", lambda h, W=W, pb=pb, j=j, ch=ch: h.activation(out=G1[:, j, 0:W], in_=pb[:, 0:W], func=AF.Sigmoid, bias=bg[:, ch:ch + 1], scale=1.0),
                             reads=[pbr, bg_r], writes=[G1_r], partial=(j > 0))
                    elif kind == "g2":
                        P.op("act", lambda h, W=W, pb=pb, j=j, ch=ch: h.activation(out=G2[:, j, 0:W], in_=pb[:, 0:W], func=AF.Sigmoid, bias=bg[:, 32 + ch:33 + ch], scale=1.0),
                             reads=[pbr, bg_r], writes=[G2_r], partial=(j > 0))
                    elif kind == "s":
                        P.op("dve", lambda h, W=W, pb=pb, j=j: h.tensor_tensor(out=M[:, j, 0:W], in0=pb[:, 0:W], in1=G1[:, j, 0:W], op=ALU.mult),
                             reads=[pbr, G1_r], writes=[M_r], partial=(j > 0))
                    else:
                        P.op("dve", lambda h, W=W, pb=pb, j=j: h.tensor_tensor(out=G2[:, j, 0:W], in0=pb[:, 0:W], in1=G2[:, j, 0:W], op=ALU.mult),
                             reads=[pbr, G2_r], writes=[G2_r], partial=(j > 0))
                        P.op("pool", lambda h, W=W, j=j, ch=ch: h.tensor_tensor(out=mxT[:, ch, 0:W], in0=M[:, j, 0:W], in1=G2[:, j, 0:W], op=ALU.add),
                             reads=[M_r, G2_r], writes=[mxT_r], partial=True)

            for cg in range(8):
                proj_b(wbg, cg * 512, hnT, hnT_r, 2, "g1", cg)
                proj_b(wbg, D + cg * 512, hnT, hnT_r, 2, "g2", cg)
                proj_b(wso, cg * 512, ynT, ynT_r, 2, "s", cg)
                proj_b(wao, cg * 512, yaT, yaT_r, 1, "a", cg)
            for cg in range(8):
                base = 4 * (grp[0] % 2)
                grp[0] += 1
                for kh in range(2):
                    w, wr = load_w(wo, kh, cg * 512)
                    for i in range(nt):
                        pb, pbr = ps[base + i], ps_r[base + i]
                        for k in range(16):
                            kc = kh * 16 + k
                            P.op("pe", lambda h, pb=pb, w=w, k=k, i=i, kc=kc: h.matmul(pb[:, :], mxT[:, kc, i * 128:(i + 1) * 128], w[:, k, :],
                                                                                 start=(kc == 0), stop=(kc == NKC - 1)),
                                 reads=[mxT_r, wr], writes=[pbr], sig=(k == 15), partial=(kc > 0))
                for i in range(nt):
                    pb, pbr = ps[base + i], ps_r[base + i]
                    r0_ = tok0 + i * 128
                    x_, x_r = xring.next()
                    P.dma("sp", x_[:, :], cx.xin[r0_:r0_ + 128, cg * 512:(cg + 1) * 512], x_r.dsem, writes=[x_r])
                    P.op("dve", lambda h, pb=pb, x_=x_: h.tensor_tensor(out=x_[:, :], in0=pb[:, :], in1=x_[:, :], op=ALU.add), reads=[pbr, x_r], writes=[x_r])
                    P.dma("sp", cx.H1[r0_:r0_ + 128, cg * 512:(cg + 1) * 512], x_[:, :], x_r.dsem, reads=[x_r], writes=[cx.H1_r], partial=True)
        _flush(P)


def phase_D2(cx):
    nc, P, NT = cx.nc, cx.P, cx.NT
    NFC = D_FF // 128
    with ExitStack() as st:
        sb = lambda n, s, dt_: st.enter_context(nc.sbuf_tensor(n, s, dt_))
        R = lambda n, **kw: P.res("f_" + n, **kw)
        WB = 384
        xt = sb("f_xt", [128, D], F32); xt_r = R("xt", dma=True)
        cx.junk = sb("f_junk", [128, D], BF16); cx.junk_r = R("junk")
        hfT = sb("f_hfT", [128, NKC, WB], BF16); hfT_r = R("hfT")
        acT = sb("f_acT", [128, NFC, WB], BF16); acT_r = R("acT")
        Gs = sb("f_Gs", [128, 4, WB], F32); Gs_r = R("Gs")
        gcol = sb("f_gcol", [128, NKC], F32); gcol_r = R("gcol", dma=True)
        sm = sb("f_sm", [128, 8], F32); sm_r = R("sm")
        wt = [sb("f_w%d" % i, [128, 16, 512], BF16) for i in range(3)]; wt_r = [R("w", dma=True) for i in range(3)]
        wring = Ring(list(zip(wt, wt_r)))
        xs_ = [sb("f_xs%d" % i, [128, 512], F32) for i in range(3)]; xs_r = [R("xs", dma=True) for i in range(3)]
        xring = Ring(list(zip(xs_, xs_r)))
        ps = [st.enter_context(nc.psum_tensor("f_ps%d" % i, [128, 512], F32)) for i in range(8)]
        ps_r = [R("ps", excl=True) for i in range(8)]
        _barrier(P)
        P.dma("sp", gcol[:, :], cx.gcol_ffn[:, :], gcol_r.dsem, writes=[gcol_r])
        wgu = cx.w_gu.rearrange("(kc p) n -> p kc n", p=128)
        wdn = cx.w_dn.rearrange("(kc p) n -> p kc n", p=128)
        grp = [0]
        for (t0, nt) in _blocks_n(NT, 3):
            W = nt * 128
            tok0 = t0 * 128
            for i in range(nt):
                r0_ = tok0 + i * 128
                _norm_transpose(cx, P, cx.H1[r0_:r0_ + 128, :], cx.H1_r, xt, xt_r, sm, sm_r, gcol, gcol_r, ps, ps_r, hfT, hfT_r, i, cx.c_identf)
            f0 = 0
            while f0 < NFC:
                nch = min(4, NFC - f0)
                for which in (0, 1):
                    base = 4 * (grp[0] % 2)
                    grp[0] += 1
                    c0 = which * D_FF + f0 * 128
                    for kh in range(2):
                        w, wr = wring.next()
                        P.dma("pool", w[:, :, 0:nch * 128], wgu[:, kh * 16:(kh + 1) * 16, c0:c0 + nch * 128], wr.dsem, writes=[wr])
                        for j in range(nch):
                            pb, pbr = ps[base + j], ps_r[base + j]
                            for k in range(16):
                                kc = kh * 16 + k
                                P.op("pe", lambda h, W=W, pb=pb, w=w, k=k, j=j, kc=kc: h.matmul(pb[:, 0:W], w[:, k, j * 128:(j + 1) * 128], hfT[:, kc, 0:W],
                                                                                     start=(kc == 0), stop=(kc == NKC - 1)),
                                     reads=[hfT_r, wr], writes=[pbr], sig=(k == 15), partial=(kc > 0))
                    for j in range(nch):
                        pb, pbr = ps[base + j], ps_r[base + j]
                        if which == 0:
                            P.op("act", lambda h, W=W, pb=pb, j=j: h.activation(out=Gs[:, j, 0:W], in_=pb[:, 0:W], func=AF.Silu), reads=[pbr], writes=[Gs_r], partial=(j > 0))
                        else:
                            P.op("dve", lambda h, W=W, pb=pb, j=j, f0=f0: h.tensor_tensor(out=acT[:, f0 + j, 0:W], in0=pb[:, 0:W], in1=Gs[:, j, 0:W], op=ALU.mult),
                                 reads=[pbr, Gs_r], writes=[acT_r], partial=True)
                f0 += nch
            for cg in range(8):
                base = 4 * (grp[0] % 2)
                grp[0] += 1
                fu = 0
                while fu < NFC:
                    nk = min(16, NFC - fu)
                    w, wr = wring.next()
                    P.dma("pool", w[:, 0:nk, :], wdn[:, fu:fu + nk, cg * 512:(cg + 1) * 512], wr.dsem, writes=[wr])
                    for i in range(nt):
                        pb, pbr = ps[base + i], ps_r[base + i]
                        for k in range(nk):
                            fc = fu + k
                            P.op("pe", lambda h, pb=pb, w=w, k=k, i=i, fc=fc: h.matmul(pb[:, :], acT[:, fc, i * 128:(i + 1) * 128], w[:, k, :],
                                                                                 start=(fc == 0), stop=(fc == NFC - 1)),
                                 reads=[acT_r, wr], writes=[pbr], sig=(k == nk - 1), partial=(fc > 0))
                    fu += nk
                for i in range(nt):
                    t = t0 + i
                    pb, pbr = ps[base + i], ps_r[base + i]
                    if t == 0:
                        continue
                    r0_ = t * 128
                    x_, x_r = xring.next()
                    P.dma("sp", x_[:, :], cx.H1[r0_:r0_ + 128, cg * 512:(cg + 1) * 512], x_r.dsem, reads=[cx.H1_r], writes=[x_r])
                    P.op("dve", lambda h, pb=pb, x_=x_: h.tensor_tensor(out=x_[:, :], in0=pb[:, :], in1=x_[:, :], op=ALU.add), reads=[pbr, x_r], writes=[x_r])
                    P.dma("sp", cx.out[r0_ - 128:r0_, cg * 512:(cg + 1) * 512], x_[:, :], x_r.dsem, reads=[x_r], writes=[cx.out_r], partial=True)
        _flush(P)


def build(NT, phases="A", dbg=()):
    T = NT * 128
    nc = bass.Bass("TRN2", target_bir_lowering=False)
    cx = Cx()
    cx.nc, cx.NT = nc, NT
    kin = "ExternalInput"

    def din(name, shape, dt=F32):
        return nc.dram_tensor(name, list(shape), dt, kind=kin).ap()

    def dscr(name, shape, dt):
        kind = "ExternalOutput" if name in dbg else "Internal"
        return nc.dram_tensor(name, list(shape), dt, kind=kind).ap()

    cx.xin = din("xin", [T, D])
    cx.w_in = din("w_in", [D, D_IN])
    cx.gcol_mix = din("gcol_mix", [128, NKC])
    cx.dt_bias = din("dt_bias", [1, 128])
    cx.a_log = din("a_log", [1, 128])
    cx.gqk = din("gqk", [128, 2])
    cx.convw = din("convw", [128, 48, 5])
    cx.convb = din("convb", [128, 48])
    cx.d_skip = din("d_skip", [1, 64])
    cx.ssd_norm_g = din("ssd_norm_g", [1, D])
    cx.gsub = din("gsub", [128, 1])
    cx.lam4 = din("lam4", [1, 256])
    cx.bgcol = din("bgcol", [128, 64])
    cx.gcol_ffn = din("gcol_ffn", [128, NKC])
    cx.w_bg = din("w_bg", [D, 2 * D])
    cx.w_so = din("w_so", [D, D])
    cx.w_ao = din("w_ao", [2048, D])
    cx.w_o = din("w_o", [D, D])
    cx.w_gu = din("w_gu", [D, 2 * D_FF])
    cx.w_dn = din("w_dn", [D_FF, D])
    consts = make_consts(NT)
    cdr = {}
    for k, v in consts.items():
        cdr[k] = din("c_" + k, v.shape, BF16 if v.dtype == ml_dtypes.bfloat16 else F32)
    cx.HNT = dscr("HNT", [128, NKC, T], BF16)
    cx.XBC = dscr("XBC", [128, 48, T + 4], F32)
    cx.QT = dscr("QT", [128, 16, T], BF16)
    cx.KT = dscr("KT", [128, 16, T], BF16)
    cx.V = dscr("V", [T, 2048], BF16)
    cx.ZS = dscr("ZS", [T, D], F32)
    cx.DTD = dscr("DTD", [128, NT, 128], F32)
    cx.XTOK = dscr("XTOK", [T, D], BF16)
    cx.BTOK = dscr("BTOK", [T, 1024], BF16)
    cx.BCT = dscr("BCT", [128, 16, T], BF16)
    cx.Y1 = dscr("Y1", [T, D], F32)
    cx.YNT = dscr("YNT", [128, NKC, T], BF16)
    cx.YAT = dscr("YAT", [128, 16, T], BF16)
    cx.H1 = dscr("H1", [T, D], F32)
    cx.out = nc.dram_tensor("out", [T - 128, D], F32, kind="ExternalOutput").ap()

    with ExitStack() as st:
        P = Prog(nc, st)
        cx.P = P
        for n in ("HNT", "XBC", "QT", "KT", "V", "ZS", "DTD", "XTOK", "BTOK", "BCT", "Y1", "YNT", "YAT", "H1", "out"):
            setattr(cx, n + "_r", P.res(n))
        sb = lambda n, s, d: st.enter_context(nc.sbuf_tensor(n, s, d))
        cx.DT_r = [P.res("DT") for _ in range(NT)]
        cx.DA_r = [P.res("DA") for _ in range(NT)]
        cx.c_r = P.res("consts", dma=True)
        for k in ("ident", "blk64", "ones", "m_kles", "m_kgts", "m_kgel", "m_klts", "onesf"):
            v = consts[k]
            t_ = sb("s_" + k, list(v.shape), BF16 if v.dtype == ml_dtypes.bfloat16 else F32)
            setattr(cx, "c_" + k, t_)
            P.dma("sp", t_[:, :], cdr[k][:, :], cx.c_r.dsem, writes=[cx.c_r], partial=True)
        cx.c_identf = sb("s_identf", [128, 128], F32)
        P.op("dve", lambda h: h.tensor_copy(out=cx.c_identf[:, :], in_=cx.c_ident[:, :]), reads=[cx.c_r], writes=[cx.c_r], partial=True)
        cx.cdr = cdr
        st_ab = ExitStack()
        cx.DT = st_ab.enter_context(nc.sbuf_tensor("DT", [128, NT, 128], F32))
        cx.DA = st_ab.enter_context(nc.sbuf_tensor("DA", [128, NT, 128], F32))
        _flush(P)
        if "A" in phases:
            phase_A(cx)
        if "B" in phases:
            phase_B(cx, 0)
            phase_B(cx, 1)
        if "DTD" not in dbg:
            st_ab.close()
        if "C" in phases:
            phase_C(cx)
        if "D" in phases:
            phase_D1(cx)
            phase_D2(cx)
        if "DTD" in dbg:
            _barrier(P)
            dd = P.new_dsem("dtd")
            P.dma("sp", cx.DTD[:, :, :], cx.DT[:, :, :], dd, reads=cx.DT_r)
        _barrier(P)
        _flush(P, final=True)
    return nc, consts


def host_inputs(NT, seq, meta_tokens, w, consts):
    T = NT * 128
    xin = np.zeros((T, D), np.float32)
    xin[NPAD:128] = meta_tokens
    xin[128:] = seq
    m = {"xin": xin}
    m["w_in"] = w["w_in"][0]
    m["gcol_mix"] = np.ascontiguousarray(w["norm_mix_g"][0].reshape(NKC, 128).T)
    m["dt_bias"] = np.ascontiguousarray(w["dt_bias"][0].reshape(1, 128))
    m["a_log"] = np.ascontiguousarray(w["a_log"][0].reshape(1, 128))
    gqk = np.stack([np.tile(w["q_norm_g"][0], 2), np.tile(w["k_norm_g"][0], 2)], axis=1)
    m["gqk"] = np.ascontiguousarray(gqk.astype(np.float32))
    m["convw"] = np.ascontiguousarray(w["conv_w"][0].T.reshape(48, 128, 5).transpose(1, 0, 2))
    m["convb"] = np.ascontiguousarray(w["conv_b"][0].reshape(48, 128).T)
    m["d_skip"] = np.ascontiguousarray(w["d_skip"][0].reshape(1, 64))
    m["ssd_norm_g"] = np.ascontiguousarray(w["ssd_norm_g"][0].reshape(1, D))
    m["gsub"] = np.ascontiguousarray(w["subln_g"][0].reshape(128, 1))
    m["lam4"] = np.ascontiguousarray(np.concatenate([w["lambda_q1"][0], w["lambda_k1"][0], w["lambda_q2"][0], w["lambda_k2"][0]]).reshape(1, 256))
    m["bgcol"] = np.ascontiguousarray(w["b_branch_gate"][0].reshape(64, 128).T)
    m["gcol_ffn"] = np.ascontiguousarray(w["norm_ffn_g"][0].reshape(NKC, 128).T)
    m["w_bg"] = w["w_branch_gate"][0]
    m["w_so"] = w["w_ssd_out"][0]
    m["w_ao"] = w["w_att_out"][0]
    m["w_o"] = w["w_o"][0]
    m["w_gu"] = w["w_gate_up"][0]
    m["w_dn"] = w["w_down"][0]
    for k, v in consts.items():
        m["c_" + k] = v
    return m


def kernel(**inputs):
    NT = 33
    w = {k: np.asarray(v) for k, v in inputs.items()}
    nc, consts = build(NT, phases="ABCD")
    xp, xs = w["x_prompt"], w["x_sample"]
    seqs = [xp[0], xp[1], xs[0], xs[1], xs[2], xs[3], xp[0], xp[1]]
    meta = w["meta_tokens"]
    in_maps = [host_inputs(NT, s_, meta, w, consts) for s_ in seqs]
    res = run_bass_kernel_spmd(nc, in_maps, core_ids=list(range(8)))
    outs = [np.asarray(r["out"], dtype=np.float32) for r in res.results]
    y_prompt = np.stack(outs[0:2], axis=0)
    y_sample = np.stack(outs[2:6], axis=0)
    return (y_prompt, y_sample)
```
